# Optimizing a Trainium2 kernel written in Bass

```python
import math
import jax
import jax.numpy as jnp
from jax import lax
import numpy as np

D_MODEL = 1024
BATCH = 4
SEQ = 4096
DEPTH = 4

CHUNK = 64
N_META = 16
N_MIXERS = 4
NORM_EPS = 1e-6

S5_GROUP = 16
S5_GROUPS = D_MODEL // S5_GROUP
S5_STATE = 64
S5_DT_MIN = 1e-3
S5_DT_MAX = 1e-1

RG_WIDTH = (4 * D_MODEL // 3) // 64 * 64
RG_BLOCKS = 16
RG_BLOCK = RG_WIDTH // RG_BLOCKS
RG_CONV = 4
RG_C = 8.0

SB_HEADS = 16
SB_HEAD_DIM = D_MODEL // SB_HEADS
SB_QBLOCK = 128

HG_HEADS = 8
HG_HEAD_DIM = D_MODEL // HG_HEADS
HG_BLOCK = math.gcd(CHUNK, N_META)

D_FF = (8 * D_MODEL // 3 + 127) // 128 * 128
FFN_CONV = 3

kernel_name = 'hybrid_streaming_encoder'


def n_uses(m):
    return len(range(m, DEPTH, N_MIXERS))


def rms_norm(x, g):
    xf = x.astype(jnp.float32)
    y = xf * lax.rsqrt(jnp.mean(xf * xf, axis=-1, keepdims=True) + NORM_EPS)
    return (y * g.astype(jnp.float32)).astype(x.dtype)


def causal_dwconv(x, w, b):
    width, ch = w.shape
    y = lax.conv_general_dilated(x, w[:, None, :].astype(x.dtype), window_strides=(1,),
                                 padding=[(width - 1, 0)], dimension_numbers=('NWC', 'WIO', 'NWC'),
                                 feature_group_count=ch)
    return y + b.astype(x.dtype)


def _lin_combine(e1, e2):
    a1, b1 = e1
    a2, b2 = e2
    return a1 * a2, a2 * b1 + b2


def _complex_lin_combine(e1, e2):
    a1r, a1i, b1r, b1i = e1
    a2r, a2i, b2r, b2i = e2
    return (a2r * a1r - a2i * a1i, a2r * a1i + a2i * a1r,
            a2r * b1r - a2i * b1i + b2r, a2r * b1i + a2i * b1r + b2i)


def s5_mixer(xn, w_in, lam_re, lam_im, log_dt, b_re, b_im, c_re, c_im, d_skip, w_glu):
    f32 = jnp.float32
    bsz, L, _ = xn.shape
    u = (xn @ w_in).reshape(bsz, L, S5_GROUPS, S5_GROUP).astype(f32)
    lr = lam_re.astype(f32)
    li = lam_im.astype(f32)
    dt = jnp.exp(log_dt.astype(f32))[:, None]
    mag = jnp.exp(lr * dt)
    ang = li * dt
    ab_re = mag * jnp.cos(ang)
    ab_im = mag * jnp.sin(ang)
    den = lr * lr + li * li
    cr = ((ab_re - 1.0) * lr + ab_im * li) / den
    ci = (ab_im * lr - (ab_re - 1.0) * li) / den
    br = b_re.astype(f32)
    bi = b_im.astype(f32)
    bb_re = cr[..., None] * br - ci[..., None] * bi
    bb_im = cr[..., None] * bi + ci[..., None] * br
    bu_re = jnp.einsum('gph,blgh->blgp', bb_re, u)
    bu_im = jnp.einsum('gph,blgh->blgp', bb_im, u)
    a_re = jnp.broadcast_to(ab_re[None, None], (1, L, S5_GROUPS, S5_STATE))
    a_im = jnp.broadcast_to(ab_im[None, None], (1, L, S5_GROUPS, S5_STATE))
    _, _, s_re, s_im = lax.associative_scan(_complex_lin_combine, (a_re, a_im, bu_re, bu_im), axis=1)
    y = (jnp.einsum('ghp,blgp->blgh', c_re.astype(f32), s_re)
         - jnp.einsum('ghp,blgp->blgh', c_im.astype(f32), s_im)
         + d_skip.astype(f32).reshape(S5_GROUPS, S5_GROUP) * u)
    y = jax.nn.gelu(y.reshape(bsz, L, D_MODEL).astype(xn.dtype))
    a, g = jnp.split(y @ w_glu, 2, axis=-1)
    return a * jax.nn.sigmoid(g)


def rglru_mixer(xn, w_in, conv_w, conv_b, w_a, b_a, w_i, b_i, lam, w_out):
    f32 = jnp.float32
    bsz, L, _ = xn.shape
    gate, xb = jnp.split(xn @ w_in, 2, axis=-1)
    xb = causal_dwconv(xb, conv_w, conv_b)
    xblk = xb.reshape(bsz, L, RG_BLOCKS, RG_BLOCK)
    r = jax.nn.sigmoid(jnp.einsum('blnj,njk->blnk', xblk, w_a).reshape(bsz, L, RG_WIDTH) + b_a)
    i = jax.nn.sigmoid(jnp.einsum('blnj,njk->blnk', xblk, w_i).reshape(bsz, L, RG_WIDTH) + b_i)
    log_a = (-RG_C * r.astype(f32)) * jax.nn.softplus(-lam.astype(f32))
    a = jnp.exp(log_a)
    b = jnp.sqrt(-jnp.expm1(2.0 * log_a)) * (i * xb).astype(f32)
    _, h = lax.associative_scan(_lin_combine, (a, b), axis=1)
    y = h.astype(xn.dtype) * jax.nn.gelu(gate)
    return y @ w_out


def stick_breaking_mixer(xn, w_qkv, q_norm, k_norm, w_out):
    f32 = jnp.float32
    bsz, L, _ = xn.shape
    q, k, v = jnp.split(xn @ w_qkv, 3, axis=-1)

    def heads(t):
        return t.reshape(bsz, L, SB_HEADS, SB_HEAD_DIM).transpose(0, 2, 1, 3)

    q = rms_norm(heads(q), q_norm)
    k = rms_norm(heads(k), k_norm)
    v = heads(v)
    n_blk = -(-L // SB_QBLOCK)
    lp = n_blk * SB_QBLOCK
    pad = ((0, 0), (0, 0), (0, lp - L), (0, 0))
    q, k, v = jnp.pad(q, pad), jnp.pad(k, pad), jnp.pad(v, pad)
    scale = SB_HEAD_DIM ** -0.5
    outs = []
    for blk in range(n_blk):
        q0 = blk * SB_QBLOCK
        q1 = q0 + SB_QBLOCK
        z = jnp.einsum('bhqd,bhkd->bhqk', q[:, :, q0:q1], k[:, :, :q1]).astype(f32) * scale
        past = jnp.arange(q1)[None, :] < jnp.arange(q0, q1)[:, None]
        log_keep = jnp.where(past, jax.nn.log_sigmoid(-z), 0.0)
        between = lax.cumsum(log_keep, axis=3, reverse=True) - log_keep
        w = jnp.where(past, jnp.exp(jax.nn.log_sigmoid(z) + between), 0.0)
        outs.append(jnp.einsum('bhqk,bhkd->bhqd', w.astype(v.dtype), v[:, :, :q1]))
    o = jnp.concatenate(outs, axis=2)[:, :, :L]
    return o.transpose(0, 2, 1, 3).reshape(bsz, L, D_MODEL) @ w_out


def hgrn2_mixer(xn, w_in, gamma, layer, o_norm, w_out):
    f32 = jnp.float32
    bsz, L, _ = xn.shape
    q, fz, i, g = jnp.split(xn @ w_in, 4, axis=-1)
    lb = jnp.sum(jax.nn.softmax(gamma.astype(f32), axis=0)[:layer], axis=0)
    log_f = jnp.logaddexp(jnp.log(lb), jnp.log1p(-lb) + jax.nn.log_sigmoid(fz.astype(f32)))
    k = -jnp.expm1(log_f)
    nb = L // HG_BLOCK

    def blocks(t):
        t = t.astype(f32).reshape(bsz, nb, HG_BLOCK, HG_HEADS, HG_HEAD_DIM)
        return t.transpose(1, 0, 3, 2, 4)

    qc, kc, vc, gc = blocks(q), blocks(k), blocks(i), blocks(log_f)
    cum = jnp.cumsum(gc, axis=3)
    last = cum[:, :, :, -1:, :]
    q_dec = qc * jnp.exp(cum)
    k_inv = kc * jnp.exp(-cum)
    k_end = kc * jnp.exp(last - cum)
    causal = jnp.tril(jnp.ones((HG_BLOCK, HG_BLOCK), dtype=bool))
    scores = jnp.where(causal, jnp.einsum('nbhtd,nbhsd->nbhts', q_dec, k_inv), 0.0)
    o_intra = jnp.einsum('nbhts,nbhse->nbhte', scores, vc)

    def step(state, inp):
        qd, ke, vv, dec = inp
        o = jnp.einsum('bhtd,bhde->bhte', qd, state)
        state = state * dec[..., None] + jnp.einsum('bhtd,bhte->bhde', ke, vv)
        return state, o

    s0 = jnp.zeros((bsz, HG_HEADS, HG_HEAD_DIM, HG_HEAD_DIM), f32)
    _, o_inter = lax.scan(step, s0, (q_dec, k_end, vc, jnp.exp(last[:, :, :, 0, :])))
    o = (o_intra + o_inter).transpose(1, 0, 3, 2, 4).reshape(bsz, L, HG_HEADS, HG_HEAD_DIM)
    o = rms_norm(o.astype(xn.dtype), o_norm.reshape(HG_HEADS, HG_HEAD_DIM)).reshape(bsz, L, D_MODEL)
    return (o * jax.nn.silu(g)) @ w_out


def conv_ffn(xn, w_up, conv_w, conv_b, w_down):
    h = causal_dwconv(xn @ w_up, conv_w, conv_b)
    a, b = jnp.split(h, 2, axis=-1)
    return (jax.nn.silu(a) * b) @ w_down


def setup_inputs(seed: int = 0):
    key = jax.random.key(seed)
    ks = list(jax.random.split(key, 40))
    f32 = jnp.float32

    def nrm(shape, scale):
        return jax.random.normal(ks.pop(), shape, f32) * scale

    def gain(shape):
        return 1.0 + nrm(shape, 0.01)

    na, nb, nc, nd = [n_uses(m) for m in range(N_MIXERS)]
    d = D_MODEL
    x = nrm((BATCH, SEQ, d), 1.0)
    meta_tokens = nrm((N_META, d), 1.0)
    norm_mix = gain((DEPTH, d))
    norm_ffn = gain((DEPTH, d))
    s5_w_in = nrm((na, d, d), d ** -0.5)
    s5_lam_re = -0.5 + nrm((na, S5_GROUPS, S5_STATE), 0.01)
    s5_lam_im = (jnp.broadcast_to(math.pi * jnp.arange(S5_STATE, dtype=f32), (na, S5_GROUPS, S5_STATE))
                 + nrm((na, S5_GROUPS, S5_STATE), 0.01))
    s5_log_dt = math.log(S5_DT_MIN) + jax.random.uniform(ks.pop(), (na, S5_GROUPS), f32) * (
        math.log(S5_DT_MAX) - math.log(S5_DT_MIN))
    s5_b_re = nrm((na, S5_GROUPS, S5_STATE, S5_GROUP), (2 * S5_GROUP) ** -0.5)
    s5_b_im = nrm((na, S5_GROUPS, S5_STATE, S5_GROUP), (2 * S5_GROUP) ** -0.5)
    s5_c_re = nrm((na, S5_GROUPS, S5_GROUP, S5_STATE), S5_STATE ** -0.5)
    s5_c_im = nrm((na, S5_GROUPS, S5_GROUP, S5_STATE), S5_STATE ** -0.5)
    s5_d = nrm((na, d), 1.0)
    s5_w_glu = nrm((na, d, 2 * d), d ** -0.5)
    rg_w_in = nrm((nb, d, 2 * RG_WIDTH), d ** -0.5)
    rg_conv_w = nrm((nb, RG_CONV, RG_WIDTH), RG_CONV ** -0.5)
    rg_conv_b = nrm((nb, RG_WIDTH), 0.01)
    rg_w_a = nrm((nb, RG_BLOCKS, RG_BLOCK, RG_BLOCK), RG_BLOCK ** -0.5)
    rg_b_a = nrm((nb, RG_WIDTH), 0.01)
    rg_w_i = nrm((nb, RG_BLOCKS, RG_BLOCK, RG_BLOCK), RG_BLOCK ** -0.5)
    rg_b_i = nrm((nb, RG_WIDTH), 0.01)
    s = jax.random.uniform(ks.pop(), (nb, RG_WIDTH), f32, minval=0.9, maxval=0.999) ** (1.0 / RG_C)
    rg_lambda = jnp.log(s) - jnp.log1p(-s)
    rg_w_out = nrm((nb, RG_WIDTH, d), RG_WIDTH ** -0.5)
    sb_w_qkv = nrm((nc, d, 3 * d), d ** -0.5)
    sb_q_norm = gain((nc, SB_HEAD_DIM))
    sb_k_norm = gain((nc, SB_HEAD_DIM))
    sb_w_out = nrm((nc, d, d), d ** -0.5)
    hg_w_in = nrm((nd, d, 4 * d), d ** -0.5)
    hg_gamma = nrm((DEPTH, d), 0.1)
    hg_o_norm = gain((nd, d))
    hg_w_out = nrm((nd, d, d), d ** -0.5)
    ffn_w_up = nrm((DEPTH, d, 2 * D_FF), d ** -0.5)
    ffn_conv_w = nrm((DEPTH, FFN_CONV, 2 * D_FF), FFN_CONV ** -0.5)
    ffn_conv_b = nrm((DEPTH, 2 * D_FF), 0.01)
    ffn_w_down = nrm((DEPTH, D_FF, d), D_FF ** -0.5)
    return {'x': x, 'meta_tokens': meta_tokens, 'norm_mix': norm_mix, 'norm_ffn': norm_ffn,
            's5_w_in': s5_w_in, 's5_lam_re': s5_lam_re, 's5_lam_im': s5_lam_im, 's5_log_dt': s5_log_dt,
            's5_b_re': s5_b_re, 's5_b_im': s5_b_im, 's5_c_re': s5_c_re, 's5_c_im': s5_c_im,
            's5_d': s5_d, 's5_w_glu': s5_w_glu,
            'rg_w_in': rg_w_in, 'rg_conv_w': rg_conv_w, 'rg_conv_b': rg_conv_b, 'rg_w_a': rg_w_a,
            'rg_b_a': rg_b_a, 'rg_w_i': rg_w_i, 'rg_b_i': rg_b_i, 'rg_lambda': rg_lambda, 'rg_w_out': rg_w_out,
            'sb_w_qkv': sb_w_qkv, 'sb_q_norm': sb_q_norm, 'sb_k_norm': sb_k_norm, 'sb_w_out': sb_w_out,
            'hg_w_in': hg_w_in, 'hg_gamma': hg_gamma, 'hg_o_norm': hg_o_norm, 'hg_w_out': hg_w_out,
            'ffn_w_up': ffn_w_up, 'ffn_conv_w': ffn_conv_w, 'ffn_conv_b': ffn_conv_b, 'ffn_w_down': ffn_w_down}


def reference(x, meta_tokens, norm_mix, norm_ffn,
              s5_w_in, s5_lam_re, s5_lam_im, s5_log_dt, s5_b_re, s5_b_im, s5_c_re, s5_c_im, s5_d, s5_w_glu,
              rg_w_in, rg_conv_w, rg_conv_b, rg_w_a, rg_b_a, rg_w_i, rg_b_i, rg_lambda, rg_w_out,
              sb_w_qkv, sb_q_norm, sb_k_norm, sb_w_out,
              hg_w_in, hg_gamma, hg_o_norm, hg_w_out,
              ffn_w_up, ffn_conv_w, ffn_conv_b, ffn_w_down):
    bsz = x.shape[0]
    meta = jnp.broadcast_to(meta_tokens.astype(x.dtype)[None], (bsz, N_META, D_MODEL))
    h = jnp.concatenate([meta, x], axis=1)
    for layer in range(DEPTH):
        m, j = layer % N_MIXERS, layer // N_MIXERS
        hn = rms_norm(h, norm_mix[layer])
        if m == 0:
            y = s5_mixer(hn, s5_w_in[j], s5_lam_re[j], s5_lam_im[j], s5_log_dt[j], s5_b_re[j], s5_b_im[j],
                         s5_c_re[j], s5_c_im[j], s5_d[j], s5_w_glu[j])
        elif m == 1:
            y = rglru_mixer(hn, rg_w_in[j], rg_conv_w[j], rg_conv_b[j], rg_w_a[j], rg_b_a[j], rg_w_i[j],
                            rg_b_i[j], rg_lambda[j], rg_w_out[j])
        elif m == 2:
            y = stick_breaking_mixer(hn, sb_w_qkv[j], sb_q_norm[j], sb_k_norm[j], sb_w_out[j])
        else:
            y = hgrn2_mixer(hn, hg_w_in[j], hg_gamma, layer, hg_o_norm[j], hg_w_out[j])
        h = h + y.astype(h.dtype)
        hn = rms_norm(h, norm_ffn[layer])
        h = h + conv_ffn(hn, ffn_w_up[layer], ffn_conv_w[layer], ffn_conv_b[layer],
                         ffn_w_down[layer]).astype(h.dtype)
    return h[:, N_META:]
```

```python
import contextlib
import numpy as np
import concourse.bass as bass
import concourse.mybir as mybir
from concourse.bass_utils import run_bass_kernel_spmd

F32 = mybir.dt.float32
BF16 = mybir.dt.bfloat16
I32 = mybir.dt.int32
AF = mybir.ActivationFunctionType
ALU = mybir.AluOpType

ENGS = ["tensor", "vector", "scalar", "gpsimd", "sync"]

L = 4112
D = 1024
KT = 8
DFF = 2816
NMETA = 16
EPS = 1e-6
SUBS = [(i * 512, 512) for i in range(8)] + [(4096, 16)]


class Buf:
    def __init__(self, name):
        self.name = name
        self.w = None
        self.r = []


class Prog:
    def __init__(self, nc, stack):
        self.nc = nc
        self.stack = stack
        self.q = {e: [] for e in ENGS}
        self.waited = {e: {} for e in ENGS}
        self.sems = {}
        self.semval = {}
        self.nops = 0
        for e in ENGS:
            self._sem("eng_" + e)

    def _sem(self, key):
        if key not in self.sems:
            self.sems[key] = self.stack.enter_context(self.nc.semaphore("s_" + key))
            self.semval[key] = 0
        return self.sems[key]

    def _collect(self, eng, reads, writes):
        need = {}

        def add(ev):
            k, v = ev
            if k == "eng_tensor" and eng == "tensor":
                return
            if v > need.get(k, 0):
                need[k] = v

        for b in reads:
            if b.w is not None:
                add(b.w)
        for b in writes:
            if b.w is not None:
                add(b.w)
            for ev in b.r:
                add(ev)
        out = []
        wd = self.waited[eng]
        for k, v in need.items():
            if wd.get(k, 0) >= v:
                continue
            wd[k] = v
            out.append((k, v))
        return out

    def _mark(self, ev, reads, writes):
        for b in writes:
            b.w = ev
            b.r = []
        for b in reads:
            if b not in writes:
                if len(b.r) > 24:
                    m = {}
                    for k, v in b.r:
                        if v > m.get(k, 0):
                            m[k] = v
                    b.r = list(m.items())
                b.r.append(ev)

    _cap = None

    def capture(self, fn):
        self._cap = []
        fn()
        c, self._cap = self._cap, None
        return c

    def emit_interleaved(self, lists):
        def items(l):
            out, curi = [], []
            for c in l:
                curi.append(c)
                if not (c[0] == "op" and c[2].get("inc") is False):
                    out.append(curi)
                    curi = []
            if curi:
                out.append(curi)
            return out

        lists = [items(l) for l in lists]
        for j in range(max(len(l) for l in lists)):
            for l in lists:
                if j < len(l):
                    for kind, a, kw in l[j]:
                        (self.op if kind == "op" else self.dma)(*a, **kw)

    def op(self, eng, fn, reads=(), writes=(), inc=True):
        if self._cap is not None:
            self._cap.append(("op", (eng, fn), dict(reads=list(reads), writes=list(writes), inc=inc)))
            return
        waits = self._collect(eng, reads, writes)
        if inc:
            k = "eng_" + eng
            self.semval[k] += 1
            self._mark((k, self.semval[k]), reads, writes)
        self.q[eng].append((fn, waits, ("eng_" + eng, 1) if inc else None))
        self.nops += 1

    def dma(self, eng, out, in_, reads=(), writes=(), key=None, **kw):
        if self._cap is not None:
            self._cap.append(("dma", (eng, out, in_), dict(reads=list(reads), writes=list(writes), key=key, **kw)))
            return
        waits = self._collect(eng, reads, writes)
        if key is None:
            key = "d_" + (writes[0].name if writes else reads[0].name + "_o")
        self._sem(key)
        self.semval[key] += 16
        self._mark((key, self.semval[key]), reads, writes)
        self.q[eng].append((lambda e: e.dma_start(out=out, in_=in_, **kw), waits, (key, 16)))
        self.nops += 1

    def wait_all(self, eng, bufs):
        waits = self._collect(eng, bufs, bufs)
        self.q[eng].append((None, waits, None))

    def barrier(self):
        for e in ENGS:
            wd = self.waited[e]
            waits = []
            for k, v in self.semval.items():
                if v > wd.get(k, 0):
                    wd[k] = v
                    waits.append((k, v))
            self.q[e].append((None, waits, None))

    def replay(self):
        self._replay()
        self.q = {e: [] for e in ENGS}

    def _replay(self):
        with self.nc.Block() as block:
            for ename in ENGS:
                ops = self.q[ename]
                if not ops:
                    continue

                def body(e, ops=ops):
                    for fn, waits, inc in ops:
                        for k, v in waits:
                            e.wait_ge(self.sems[k], v)
                        if fn is None:
                            continue
                        ins = fn(e)
                        if inc is not None:
                            ins.then_inc(self.sems[inc[0]], inc[1])

                getattr(block, ename)(body)


def host_layout(inp):
    f = np.float32
    o = {}
    g = np.stack([inp["norm_mix"][0], inp["norm_ffn"][0], inp["norm_mix"][1], inp["norm_ffn"][1],
                  inp["norm_mix"][2], inp["norm_ffn"][2], inp["norm_mix"][3], inp["norm_ffn"][3]], 0)
    o["gn"] = np.ascontiguousarray(g.reshape(8, KT, 128).transpose(2, 0, 1)).astype(f)
    wu = inp["ffn_w_up"].reshape(4, KT, 128, 2, 11, 256)
    o["wup"] = np.ascontiguousarray(wu.transpose(0, 4, 2, 1, 3, 5)).reshape(4, 11, 128, KT * 512).astype(f)
    wd = inp["ffn_w_down"].reshape(4, 22, 128, 8, 128)
    o["wdn"] = np.ascontiguousarray(wd.transpose(0, 3, 2, 1, 4)).reshape(4, 8, 128, 22 * 128).astype(f)
    cw = np.concatenate([inp["ffn_conv_w"], inp["ffn_conv_b"][:, None, :]], 1)
    o["fcw"] = np.ascontiguousarray(cw.reshape(4, 4, 44, 128).transpose(3, 0, 2, 1)).reshape(128, 4 * 44 * 4).astype(f)
    o["wrgin"] = np.ascontiguousarray(inp["rg_w_in"][0].reshape(KT, 128, 2688).transpose(1, 0, 2)).astype(f)
    o["wrgout"] = np.ascontiguousarray(inp["rg_w_out"][0].reshape(16, 84, D).transpose(1, 0, 2)).astype(f)
    o["wrga"] = np.ascontiguousarray(inp["rg_w_a"][0].transpose(1, 0, 2)).astype(f)
    o["wrgi"] = np.ascontiguousarray(inp["rg_w_i"][0].transpose(1, 0, 2)).astype(f)
    pr = np.concatenate([inp["rg_conv_w"][0], inp["rg_conv_b"], inp["rg_b_a"], inp["rg_b_i"], inp["rg_lambda"]], 0)
    o["rgp"] = np.ascontiguousarray(pr.reshape(8, 16, 84).transpose(2, 1, 0)).astype(f)
    o["whgin"] = np.ascontiguousarray(inp["hg_w_in"][0].reshape(KT, 128, 4096).transpose(1, 0, 2)).astype(f)
    o["whgout"] = np.ascontiguousarray(inp["hg_w_out"][0].reshape(KT, 128, D).transpose(1, 0, 2)).astype(f)
    o["hgg"] = np.ascontiguousarray(inp["hg_gamma"].reshape(4, KT, 128).transpose(2, 0, 1)).astype(f)
    o["hgon"] = np.ascontiguousarray(inp["hg_o_norm"][0].reshape(KT, 128).T).astype(f)
    wq = inp["sb_w_qkv"][0].reshape(KT, 128, 3, 2, 512)
    o["wqkv"] = np.ascontiguousarray(wq.transpose(3, 1, 0, 2, 4)).reshape(2, 128, KT, 1536).astype(f)
    wo = inp["sb_w_out"][0].reshape(2, 8, 64, D)
    o["wsbo"] = np.ascontiguousarray(wo.transpose(0, 2, 1, 3)).astype(f)
    o["qkn"] = np.ascontiguousarray(np.stack([np.tile(inp["sb_q_norm"][0], 2), np.tile(inp["sb_k_norm"][0], 2)], 1)).astype(f)
    bo = np.zeros((128, 128), f)
    bo[:64, :64] = 1.0
    bo[64:, 64:] = 1.0
    o["blockones"] = bo
    o["ones32"] = np.ones((128, 128), f)
    o["lstrict"] = np.tril(np.ones((128, 128), f), -1)
    jj = np.arange(128)[:, None]
    tt = np.arange(512)[None, :]
    am = np.zeros((2, 4, 128, 512), f)
    for r in range(4):
        past = (128 * r + jj) < tt
        am[0, r] = past.astype(f)
        am[1, r] = np.where(past, 0.0, -30000.0)
    o["amask"] = am
    o["ws5in"] = np.ascontiguousarray(inp["s5_w_in"][0].reshape(KT, 128, D).transpose(1, 0, 2)).astype(f)
    o["ws5glu"] = np.ascontiguousarray(inp["s5_w_glu"][0].reshape(KT, 128, 2 * D).transpose(1, 0, 2)).astype(f)
    lr, li, ldt = inp["s5_lam_re"][0], inp["s5_lam_im"][0], inp["s5_log_dt"][0]
    ldt2 = np.repeat(ldt[:, None], 64, 1)
    lane = lambda a: a.reshape(32, 2, 64).transpose(1, 2, 0).reshape(128, 32)
    o["s5lane"] = np.ascontiguousarray(np.stack([lane(lr), lane(li), lane(ldt2)], 1)).astype(f)
    row = np.stack([lr.reshape(-1), li.reshape(-1), ldt2.reshape(-1)], 0)
    o["s5row"] = np.ascontiguousarray(np.broadcast_to(row[None], (128, 3, 4096))).astype(f)
    gg = np.arange(64)
    bpad = np.zeros((2, 128, 32, 128), f)
    cpad = np.zeros((2, 128, 32, 128), f)
    for g in range(64):
        r0, pr, l0 = (g % 8) * 16, g // 2, (g % 2) * 64
        bpad[0, r0:r0 + 16, pr, l0:l0 + 64] = inp["s5_b_re"][0][g].T
        bpad[1, r0:r0 + 16, pr, l0:l0 + 64] = inp["s5_b_im"][0][g].T
        cpad[0, l0:l0 + 64, pr, r0:r0 + 16] = inp["s5_c_re"][0][g].T
        cpad[1, l0:l0 + 64, pr, r0:r0 + 16] = inp["s5_c_im"][0][g].T
    o["s5b"] = bpad
    o["s5c"] = cpad
    o["s5d"] = np.ascontiguousarray(inp["s5_d"][0].reshape(KT, 128).T).astype(f)
    o["tau1"] = np.ascontiguousarray(np.broadcast_to(np.arange(1, 257, dtype=f)[None], (128, 256)))
    o["ident"] = np.eye(128, dtype=f)
    rm = np.ones((128, 512), f)
    rm[:, ::128] = 0.0
    o["rmask"] = rm
    o["triu"] = np.triu(np.ones((128, 128), f))
    return o


def build(stages, wshapes, debug_out=False):
    nc = bass.Bass("TRN2", target_bir_lowering=False)
    st = contextlib.ExitStack()
    with st:
        P = Prog(nc, st)

        def din(name, shape, dt=F32):
            return nc.dram_tensor(name, list(shape), dt, kind="ExternalInput").ap()

        def dscr(name, shape, dt):
            return nc.dram_tensor(name, list(shape), dt).ap()

        cur = [st]

        uid = [0]

        def sb(name, shape, dt):
            uid[0] += 1
            return cur[0].enter_context(nc.sbuf_tensor(f"sb{uid[0]}_{name}", list(shape), dt))

        class NS:
            pass

        C = NS()

        @contextlib.contextmanager
        def stage_scope(NT):
            with contextlib.ExitStack() as sst:
                cur[0] = sst
                C.ht = sb("ht", [128, KT, NT], F32)
                C.Bht = Bht
                C.Bxn = Bxn
                C.xn = sb("xn", [128, KT, NT], BF16)
                C.sq = sb("sq", [128, 2, NT], BF16)
                C.rs = sb("rs", [128, NT], F32)
                yield
                P.barrier()
                P.replay()
                cur[0] = st

        h0 = din("h0", [D, L])
        W = {k: din(k, s) for k, s in wshapes.items()}
        out = nc.dram_tensor("out", [D, L - NMETA], F32, kind="ExternalOutput").ap()
        hd = dscr("hd", [D, L], F32)
        Bh = [Buf(f"hd{i}") for i in range(len(SUBS))]
        hd_v = hd.rearrange("(kt p) t -> p kt t", p=128)

        PS = [st.enter_context(nc.psum_tensor(f"ps{i}", [128, 512], F32)) for i in range(8)]
        BPS = [Buf(f"ps{i}") for i in range(8)]
        ps_rr = [0]

        def pipeline(ntiles, phases):
            k = len(phases)
            for step in range(ntiles + k - 1):
                for p in range(k):
                    t = step - p
                    if 0 <= t < ntiles:
                        phases[p](t)

        ps_reserved = set()

        def next_ps():
            while True:
                i = ps_rr[0]
                ps_rr[0] = (i + 1) % 8
                if i not in ps_reserved:
                    return PS[i], BPS[i]

        ones_bf = sb("ones_bf", [128, 128], BF16)
        Bones = Buf("ones")
        P.op("vector", lambda e: e.memset(ones_bf[:], 1.0), writes=[Bones])
        gn = sb("gn", [128, 8, KT], F32)
        Bgn = Buf("gn")
        P.dma("sync", gn[:], W["gn"], writes=[Bgn])

        for i, (c0, n) in enumerate(SUBS):
            P.dma("sync", hd[:, c0:c0 + n], h0[:, c0:c0 + n], writes=[Bh[i]], key="hinit")

        W16 = {}
        BW16 = {}

        def precast(name, lead):
            src = W[name]
            dst = dscr(name + "16", src.shape, BF16)
            W16[name] = dst
            BW16[name] = Buf(name + "16")
            import itertools
            for idx in itertools.product(*[range(s) for s in src.shape[:lead]]):
                s_ap, d_ap = src, dst
                for i in idx:
                    s_ap, d_ap = s_ap[i], d_ap[i]
                P.dma("gpsimd", d_ap, s_ap, writes=[BW16[name]], key="precast")

        NTM = 1040
        Bht = Buf("ht")
        Bxn = Buf("xn")
        Bsq = Buf("sq")
        Brs = Buf("rs")

        def subs_of(ch):
            c0 = SUBS[ch[0]][0]
            lst = [(SUBS[i][0] - c0, SUBS[i][1]) for i in ch]
            return c0, sum(x[1] for x in lst), lst

        def load_h(ch):
            ht = C.ht
            c0, n, _ = subs_of(ch)
            P.dma("sync", ht[:, :, :n], hd_v[:, :, c0:c0 + n], reads=[Bh[i] for i in ch], writes=[C.Bht])

        def store_h(ch):
            ht = C.ht
            c0, n, _ = subs_of(ch)
            P.dma("sync", hd_v[:, :, c0:c0 + n], ht[:, :, :n], reads=[C.Bht], writes=[Bh[i] for i in ch])

        Bsq2 = [Buf("sq_a"), Buf("sq_b")]

        def norm_impl(n, lst, sidx):
            ht, xn, sq, rs = C.ht, C.xn, C.sq, C.rs
            banks = [next_ps() for _ in lst]
            for kt in range(KT):
                P.op("scalar", lambda e, kt=kt: e.activation(sq[:, kt % 2, :n], ht[:, kt, :n], AF.Square),
                     reads=[C.Bht], writes=[Bsq2[kt % 2]])
                for (off, nsb), (ps, bps) in zip(lst, banks):
                    P.op("tensor", lambda e, ps=ps, kt=kt, off=off, nsb=nsb: e.matmul(
                        ps[:, :nsb], ones_bf[:], sq[:, kt % 2, off:off + nsb], start=(kt == 0), stop=(kt == KT - 1)),
                        reads=[Bones, Bsq2[kt % 2]], writes=[bps])
            for (off, nsb), (ps, bps) in zip(lst, banks):
                P.op("scalar", lambda e, ps=ps, off=off, nsb=nsb: e.activation(
                    rs[:, off:off + nsb], ps[:, :nsb], AF.Sqrt, bias=eps_t[:, 0:1], scale=1.0 / D),
                    reads=[bps, Beps], writes=[Brs])
            P.op("vector", lambda e: e.reciprocal(rs[:, :n], rs[:, :n]), reads=[Brs], writes=[Brs])
            for kt in range(KT):
                P.op("vector", lambda e, kt=kt: e.scalar_tensor_tensor(
                    xn[:, kt, :n], ht[:, kt, :n], gn[:, sidx, kt:kt + 1], rs[:, :n], ALU.mult, ALU.mult),
                    reads=[C.Bht, Bgn, Brs], writes=[C.Bxn])

        def norm(ch, sidx):
            c0, n, lst = subs_of(ch)
            norm_impl(n, lst, sidx)

        def norm_n(n, sidx):
            norm_impl(n, [(0, n)], sidx)

        eps_t = sb("eps_t", [128, 1], F32)
        Beps = Buf("eps")
        P.op("vector", lambda e: e.memset(eps_t[:], EPS), writes=[Beps])

        if any(s[0] == "ffn" for s in stages):
            precast("wup", 2)
            precast("wdn", 2)
        Bwu = [Buf(f"wu{i}") for i in range(2)]
        Bwd = [Buf(f"wd{i}") for i in range(2)]
        Bact = Buf("act")
        BU = [Buf(f"U{i}") for i in range(2)]
        BY = [Buf(f"Y{i}") for i in range(2)]
        Bhalo = Buf("halo")
        Bfcw = Buf("fcw")

        def ffn_stage(layer, sidx):
            with stage_scope(NTM):
                F = NS()
                F.fcw = sb("fcw", [128, 4 * 44 * 4], F32)
                P.dma("sync", F.fcw[:], W["fcw"], writes=[Bfcw])
                F.wu = [sb(f"wu{i}", [128, KT, 512], BF16) for i in range(2)]
                F.wdt = [sb(f"wd{i}", [128, 22, 128], BF16) for i in range(2)]
                F.act = sb("act", [128, 22, NTM], BF16)
                F.U = [sb(f"U{i}", [128, NTM + 2], F32) for i in range(2)]
                F.Y = [sb(f"Y{i}", [128, NTM], F32) for i in range(4)]
                F.BY = [Buf(f"Yb{i}") for i in range(4)]
                F.halo = sb("halo", [128, 44, 2], F32)
                halo = F.halo
                P.op("vector", lambda e: e.memset(halo[:], 0.0), writes=[Bhalo])
                HT = [C.ht, sb("ht2", [128, KT, NTM], F32)]
                BHT = [Bht, Buf("ht2")]
                XN = [C.xn, sb("xn2", [128, KT, NTM], BF16)]
                BXN = [Bxn, Buf("xn2")]
                F.Bact = [Buf(f"act{m}") for m in range(22)]
                F.first = True
                chs = [[0, 1], [2, 3], [4, 5], [6, 7, 8]]
                load_h(chs[0])
                norm(chs[0], sidx)
                for i, ch in enumerate(chs):
                    C.ht, C.Bht, C.xn, C.Bxn = HT[i % 2], BHT[i % 2], XN[i % 2], BXN[i % 2]
                    j = (i + 1) % 2
                    nxt = (chs[i + 1], HT[j], BHT[j], XN[j], BXN[j]) if i + 1 < len(chs) else None
                    ffn_chunk(F, layer, sidx, ch, nxt)
                C.ht, C.Bht, C.xn, C.Bxn = HT[0], BHT[0], XN[0], BXN[0]

        def ffn_chunk(F, layer, sidx, ch, nxt):
            fcw, wu, wdt, act, U, Y, halo = F.fcw, F.wu, F.wdt, F.act, F.U, F.Y, F.halo
            ht, xn = C.ht, C.xn
            Bht, Bxn, Bact = C.Bht, C.Bxn, F.Bact
            if True:
                c0, n, lst = subs_of(ch)
                if nxt is not None:
                    cur = (C.ht, C.Bht)
                    C.ht, C.Bht = nxt[1], nxt[2]
                    load_h(nxt[0])
                    C.ht, C.Bht = cur
                if F.first:
                    P.dma("sync", wu[0][:], W16["wup"][layer, 0].rearrange("p (k c) -> p k c", k=KT),
                          reads=[BW16["wup"]], writes=[Bwu[0]])
                    F.first = False
                for g in range(11):
                    if g == 8:
                        P.dma("sync", wdt[0][:], W16["wdn"][layer, 0].rearrange("p (k c) -> p k c", k=22),
                              reads=[BW16["wdn"]], writes=[Bwd[0]])
                    if g + 1 < 11:
                        P.dma("sync", wu[(g + 1) % 2][:], W16["wup"][layer, g + 1].rearrange("p (k c) -> p k c", k=KT),
                              reads=[BW16["wup"]], writes=[Bwu[(g + 1) % 2]])
                    wt, bwt = wu[g % 2], Bwu[g % 2]
                    for half in range(2):
                        m = 2 * g + half
                        for ab in range(2):
                            tidx = m + 22 * ab
                            col = 256 * ab + 128 * half
                            u, bu, y, by = U[ab], BU[ab], Y[2 * (m % 2) + ab], F.BY[2 * (m % 2) + ab]
                            P.op("gpsimd", lambda e, u=u, tidx=tidx: e.tensor_copy(u[:, 0:2], halo[:, tidx, :]),
                                 reads=[Bhalo], writes=[bu])
                            for (off, nsb) in lst:
                                ps, bps = next_ps()
                                for kt in range(KT):
                                    P.op("tensor", lambda e, ps=ps, wt=wt, kt=kt, col=col, off=off, nsb=nsb: e.matmul(
                                        ps[:, :nsb], wt[:, kt, col:col + 128], xn[:, kt, off:off + nsb],
                                        start=(kt == 0), stop=(kt == KT - 1)),
                                        reads=[bwt, Bxn], writes=[bps], inc=(kt == KT - 1))
                                P.op("scalar", lambda e, ps=ps, u=u, off=off, nsb=nsb: e.copy(
                                    u[:, 2 + off:2 + off + nsb], ps[:, :nsb]), reads=[bps], writes=[bu])
                            P.op("gpsimd", lambda e, u=u, tidx=tidx: e.tensor_copy(halo[:, tidx, :], u[:, n:n + 2]),
                                 reads=[bu], writes=[Bhalo])
                            cb = (layer * 44 + tidx) * 4
                            P.op("scalar", lambda e, u=u, y=y, cb=cb: e.activation(
                                y[:, :n], u[:, 2:2 + n], AF.Identity, bias=fcw[:, cb + 3:cb + 4], scale=fcw[:, cb + 2:cb + 3]),
                                reads=[bu, Bfcw], writes=[by])
                            P.op("vector", lambda e, u=u, y=y, cb=cb: e.scalar_tensor_tensor(
                                y[:, :n], u[:, 1:1 + n], fcw[:, cb + 1:cb + 2], y[:, :n], ALU.mult, ALU.add),
                                reads=[bu, Bfcw, by], writes=[by])
                            P.op("vector", lambda e, u=u, y=y, cb=cb: e.scalar_tensor_tensor(
                                y[:, :n], u[:, 0:n], fcw[:, cb:cb + 1], y[:, :n], ALU.mult, ALU.add),
                                reads=[bu, Bfcw, by], writes=[by])
                        def gate(m):
                            ya, yb, bya, byb = Y[2 * (m % 2)], Y[2 * (m % 2) + 1], F.BY[2 * (m % 2)], F.BY[2 * (m % 2) + 1]
                            P.op("scalar", lambda e: e.activation(ya[:, :n], ya[:, :n], AF.Silu), reads=[bya], writes=[bya])
                            P.op("gpsimd", lambda e: e.tensor_tensor(act[:, m, :n], ya[:, :n], yb[:, :n], ALU.mult),
                                 reads=[bya, byb], writes=[Bact[m]])
                        if m > 0:
                            gate(m - 1)
                gate(21)
                if nxt is not None:
                    cur = (C.ht, C.Bht, C.xn, C.Bxn)
                    C.ht, C.Bht, C.xn, C.Bxn = nxt[1], nxt[2], nxt[3], nxt[4]
                    norm(nxt[0], sidx)
                    C.ht, C.Bht, C.xn, C.Bxn = cur
                if nxt is not None:
                    P.dma("sync", wu[0][:], W16["wup"][layer, 0].rearrange("p (k c) -> p k c", k=KT),
                          reads=[BW16["wup"]], writes=[Bwu[0]])
                for j in range(8):
                    if j + 1 < 8:
                        P.dma("sync", wdt[(j + 1) % 2][:], W16["wdn"][layer, j + 1].rearrange("p (k c) -> p k c", k=22),
                              reads=[BW16["wdn"]], writes=[Bwd[(j + 1) % 2]])
                    wt, bwt = wdt[j % 2], Bwd[j % 2]
                    o = j
                    for (off, nsb) in lst:
                        ps, bps = next_ps()
                        for kt in range(22):
                            P.op("tensor", lambda e, ps=ps, wt=wt, kt=kt, off=off, nsb=nsb: e.matmul(
                                ps[:, :nsb], wt[:, kt, :], act[:, kt, off:off + nsb],
                                start=(kt == 0), stop=(kt == 21)),
                                reads=([bwt] + Bact) if kt == 21 else [bwt, Bact[kt]], writes=[bps], inc=(kt == 21))
                        P.op("vector", lambda e, ps=ps, o=o, off=off, nsb=nsb: e.tensor_tensor(
                            ht[:, o, off:off + nsb], ht[:, o, off:off + nsb], ps[:, :nsb], ALU.add),
                            reads=[bps, Bht], writes=[Bht])
                store_h(ch)

        one_t = sb("one_t", [128, 1], F32)
        Bone = Buf("one")
        P.op("vector", lambda e: e.memset(one_t[:], 1.0), writes=[Bone])

        def rg_stage(sidx):
            with stage_scope(512):
                rg_stage_(sidx)

        def rg_stage_(sidx):
            RW = 84
            ht, xn = C.ht, C.xn
            wrg = sb("wrg", [128, KT, 2688], BF16); Bwrg = Buf("wrg")
            P.dma("gpsimd", wrg[:], W["wrgin"], writes=[Bwrg])
            wro = sb("wro", [RW, 16, D], BF16); Bwro = Buf("wro")
            P.dma("gpsimd", wro[:], W["wrgout"], writes=[Bwro])
            wa = sb("wa", [RW, 16, RW], BF16); Bwa = Buf("wa")
            P.dma("gpsimd", wa[:], W["wrga"], writes=[Bwa])
            wi = sb("wi", [RW, 16, RW], BF16); Bwi = Buf("wi")
            P.dma("gpsimd", wi[:], W["wrgi"], writes=[Bwi])
            rgp = sb("rgp", [RW, 16, 8], F32); Brgp = Buf("rgp")
            P.dma("sync", rgp[:], W["rgp"], writes=[Brgp])
            cneg = sb("cneg", [RW, 16], F32); c2 = sb("c2", [RW, 16], F32); Bc = Buf("cneg")
            P.op("scalar", lambda e: e.activation(cneg[:], rgp[:, :, 7], AF.Exp, scale=-1.0), reads=[Brgp], writes=[Bc])
            P.op("scalar", lambda e: e.activation(cneg[:], cneg[:], AF.Ln, bias=one_t[:RW, 0:1]), reads=[Bc, Bone], writes=[Bc])
            P.op("vector", lambda e: e.tensor_scalar(cneg[:], cneg[:], -8.0, None, ALU.mult), reads=[Bc], writes=[Bc])
            P.op("vector", lambda e: e.tensor_scalar(c2[:], cneg[:], 2.0, None, ALU.mult), reads=[Bc], writes=[Bc])
            carry = sb("carry", [RW, 16], F32); Bcar = Buf("carry")
            P.op("vector", lambda e: e.memset(carry[:], 0.0), writes=[Bcar])
            rhalo = sb("rhalo", [RW, 16, 3], F32); Brh = Buf("rhalo")
            P.op("vector", lambda e: e.memset(rhalo[:], 0.0), writes=[Brh])
            YM = sb("YM", [RW, 16, 512], BF16); BYM = Buf("YM")
            def mkset(i):
                d = {"i": i}
                for nm_, shp, dt_ in [("T1", 512, F32), ("XB", 515, F32), ("XC", 512, F32), ("XCb", 512, BF16), ("Rr", 512, F32),
                                      ("Ii", 512, F32), ("Aa", 512, F32), ("HS", 512, F32), ("GG", 512, F32)]:
                    d[nm_] = sb(f"{nm_}_{i}", [RW, shp], dt_)
                    d["B" + nm_] = Buf(f"rg{nm_}_{i}")
                d["Mm"], d["BMm"] = d["Rr"], d["BRr"]
                d["Bt"], d["BBt"] = d["Ii"], d["BIi"]
                return d

            NSET = 4
            tsets = [mkset(i) for i in range(NSET)]

            def rg_tile(t, n, ts):
                T1, XB, XC, XCb, Rr, Ii, Aa, Mm, Bt, HS, GGt = (ts[k] for k in ["T1", "XB", "XC", "XCb", "Rr", "Ii", "Aa", "Mm", "Bt", "HS", "GG"])
                BT1, BXB, BXC, BXCb, BRr, BIi, BAa, BMm, BBt, BHS, BGG = (ts["B" + k] for k in
                                                                          ["T1", "XB", "XC", "XCb", "Rr", "Ii", "Aa", "Mm", "Bt", "HS", "GG"])
                ps, bps = PS[2 * ts['i']], BPS[2 * ts['i']]
                for kt in range(KT):
                    P.op("tensor", lambda e, kt=kt: e.matmul(ps[:RW, :n], wrg[:, kt, t * RW:(t + 1) * RW], xn[:, kt, :n],
                                                             start=(kt == 0), stop=(kt == KT - 1)),
                         reads=[Bwrg, Bxn], writes=[bps], inc=(kt == KT - 1))
                P.op("scalar", lambda e: e.activation(T1[:, :n], ps[:RW, :n], AF.Square), reads=[bps], writes=[BT1])
                P.op("gpsimd", lambda e: e.tensor_scalar(T1[:, :n], T1[:, :n], 0.044715, 1.0, ALU.mult, ALU.add),
                     reads=[BT1], writes=[BT1])
                P.op("vector", lambda e: e.tensor_tensor(T1[:, :n], T1[:, :n], ps[:RW, :n], ALU.mult),
                     reads=[BT1, bps], writes=[BT1])
                P.op("scalar", lambda e: e.activation(T1[:, :n], T1[:, :n], AF.Sigmoid, scale=1.5957691216), reads=[BT1], writes=[BT1])
                P.op("vector", lambda e: e.tensor_tensor(GGt[:, :n], T1[:, :n], ps[:RW, :n], ALU.mult),
                     reads=[BT1, bps], writes=[BGG])
                ps2, bps2 = PS[2 * ts['i'] + 1], BPS[2 * ts['i'] + 1]
                c0 = 1344 + t * RW
                for kt in range(KT):
                    P.op("tensor", lambda e, kt=kt: e.matmul(ps2[:RW, :n], wrg[:, kt, c0:c0 + RW], xn[:, kt, :n],
                                                             start=(kt == 0), stop=(kt == KT - 1)),
                         reads=[Bwrg, Bxn], writes=[bps2], inc=(kt == KT - 1))
                P.op("gpsimd", lambda e: e.tensor_copy(XB[:, 0:3], rhalo[:, t, :]), reads=[Brh], writes=[BXB])
                P.op("scalar", lambda e: e.copy(XB[:, 3:3 + n], ps2[:RW, :n]), reads=[bps2], writes=[BXB])
                P.op("gpsimd", lambda e: e.tensor_copy(rhalo[:, t, :], XB[:, n:n + 3]), reads=[BXB], writes=[Brh])
                P.op("gpsimd", lambda e: e.tensor_scalar(XC[:, :n], XB[:, 3:3 + n], rgp[:, t, 3:4], rgp[:, t, 4:5], ALU.mult, ALU.add),
                     reads=[BXB, Brgp], writes=[BXC])
                for k in range(3):
                    P.op("vector", lambda e, k=k: e.scalar_tensor_tensor(XC[:, :n], XB[:, k:k + n], rgp[:, t, k:k + 1], XC[:, :n],
                                                                         ALU.mult, ALU.add),
                         reads=[BXB, Brgp, BXC], writes=[BXC])
                P.op("gpsimd", lambda e: e.tensor_copy(XCb[:, :n], XC[:, :n]), reads=[BXC], writes=[BXCb])
                ps3, bps3 = PS[2 * ts['i']], BPS[2 * ts['i']]
                P.op("tensor", lambda e: e.matmul(ps3[:RW, :n], wa[:, t, :], XCb[:, :n], start=True, stop=True),
                     reads=[Bwa, BXCb], writes=[bps3])
                P.op("scalar", lambda e: e.activation(Rr[:, :n], ps3[:RW, :n], AF.Sigmoid, bias=rgp[:, t, 5:6]),
                     reads=[bps3, Brgp], writes=[BRr])
                ps4, bps4 = PS[2 * ts['i'] + 1], BPS[2 * ts['i'] + 1]
                P.op("tensor", lambda e: e.matmul(ps4[:RW, :n], wi[:, t, :], XCb[:, :n], start=True, stop=True),
                     reads=[Bwi, BXCb], writes=[bps4])
                P.op("scalar", lambda e: e.activation(Ii[:, :n], ps4[:RW, :n], AF.Sigmoid, bias=rgp[:, t, 6:7]),
                     reads=[bps4, Brgp], writes=[BIi])
                P.op("scalar", lambda e: e.activation(Aa[:, :n], Rr[:, :n], AF.Exp, scale=cneg[:, t:t + 1]),
                     reads=[BRr, Bc], writes=[BAa])
                P.op("scalar", lambda e: e.activation(Mm[:, :n], Rr[:, :n], AF.Exp, scale=c2[:, t:t + 1]),
                     reads=[BRr, Bc], writes=[BMm])
                P.op("vector", lambda e: e.tensor_scalar(Mm[:, :n], Mm[:, :n], 0.99999994, None, ALU.min), reads=[BMm], writes=[BMm])
                P.op("scalar", lambda e: e.activation(Mm[:, :n], Mm[:, :n], AF.Sqrt, scale=-1.0, bias=one_t[:RW, 0:1]),
                     reads=[BMm, Bone], writes=[BMm])
                P.op("gpsimd", lambda e: e.tensor_tensor(Bt[:, :n], Ii[:, :n], XC[:, :n], ALU.mult), reads=[BIi, BXC], writes=[BBt])
                P.op("gpsimd", lambda e: e.tensor_tensor(Bt[:, :n], Bt[:, :n], Mm[:, :n], ALU.mult), reads=[BBt, BMm], writes=[BBt])
                P.op("vector", lambda e: e.tensor_tensor_scan(HS[:, :n], Aa[:, :n], Bt[:, :n], carry[:, t:t + 1], ALU.mult, ALU.add),
                     reads=[BAa, BBt, Bcar], writes=[BHS])
                P.op("vector", lambda e: e.tensor_copy(carry[:, t:t + 1], HS[:, n - 1:n]), reads=[BHS], writes=[Bcar])
                P.op("gpsimd", lambda e: e.tensor_tensor(YM[:, t, :n], HS[:, :n], GGt[:, :n], ALU.mult),
                     reads=[BHS, BGG], writes=[BYM])

            def rg_chunk(si):
                c0, n, lst = subs_of([si])
                load_h([si])
                norm([si], sidx)
                for t0 in range(0, 16, NSET):
                    P.emit_interleaved([P.capture(lambda i=i: rg_tile(t0 + i, n, tsets[i])) for i in range(NSET)])
                for o in range(KT):
                    ps, bps = next_ps()
                    for t in range(16):
                        P.op("tensor", lambda e, t=t, o=o, ps=ps: e.matmul(ps[:, :n], wro[:, t, o * 128:(o + 1) * 128], YM[:, t, :n],
                                                                          start=(t == 0), stop=(t == 15)),
                             reads=[Bwro, BYM], writes=[bps], inc=(t == 15))
                    P.op("vector", lambda e, o=o, ps=ps: e.tensor_tensor(ht[:, o, :n], ht[:, o, :n], ps[:, :n], ALU.add),
                         reads=[bps, Bht], writes=[Bht])
                store_h([si])

            for si in range(len(SUBS)):
                rg_chunk(si)

        def hg_stage(sidx):
            with stage_scope(512):
                hg_stage_(sidx)

        def hg_stage_(sidx):
            ht, xn = C.ht, C.xn
            whg = sb("whg", [128, KT, 4096], BF16); Bwhg = Buf("whg")
            for kt in range(KT):
                P.dma("gpsimd", whg[:, kt, :], W["whgin"][:, kt, :], writes=[Bwhg])
            who = sb("who", [128, KT, D], BF16); Bwho = Buf("who")
            P.dma("gpsimd", who[:], W["whgout"], writes=[Bwho])
            ident = sb("ident", [128, 128], F32); rmask = sb("rmask", [128, 512], F32); triu = sb("triu", [128, 128], F32)
            Bcst = Buf("hgconst")
            P.dma("sync", ident[:], W["ident"], writes=[Bcst])
            P.dma("sync", rmask[:], W["rmask"], writes=[Bcst])
            P.dma("sync", triu[:], W["triu"], writes=[Bcst])
            gam = sb("gam", [128, 4, KT], F32); on = sb("on", [128, KT], F32); Bgam = Buf("gam")
            P.dma("sync", gam[:], W["hgg"], writes=[Bgam])
            P.dma("sync", on[:], W["hgon"], writes=[Bgam])
            lb = sb("lb", [128, KT], F32); oml = sb("oml", [128, KT], F32); s4 = sb("s4", [128, KT], F32); Blb = Buf("lb")
            P.op("scalar", lambda e: e.activation(gam[:], gam[:], AF.Exp), reads=[Bgam], writes=[Bgam])
            P.op("vector", lambda e: e.tensor_tensor(lb[:], gam[:, 0, :], gam[:, 1, :], ALU.add), reads=[Bgam], writes=[Blb])
            P.op("vector", lambda e: e.tensor_tensor(lb[:], lb[:], gam[:, 2, :], ALU.add), reads=[Bgam, Blb], writes=[Blb])
            P.op("vector", lambda e: e.tensor_tensor(s4[:], lb[:], gam[:, 3, :], ALU.add), reads=[Bgam, Blb], writes=[Blb])
            P.op("vector", lambda e: e.reciprocal(s4[:], s4[:]), reads=[Blb], writes=[Blb])
            P.op("vector", lambda e: e.tensor_tensor(lb[:], lb[:], s4[:], ALU.mult), reads=[Blb], writes=[Blb])
            P.op("vector", lambda e: e.tensor_tensor(oml[:], gam[:, 3, :], s4[:], ALU.mult), reads=[Bgam, Blb], writes=[Blb])
            S = sb("S", [128, KT, 128], F32)
            VT = sb("VT", [128, 4, D], BF16); BVT = Buf("VT")
            OG = sb("OG", [128, KT, 512], BF16); BOG = Buf("OG")
            BSl = [Buf(f"S{i}") for i in range(KT)]
            names = ["FH", "LF", "K1", "CUM", "E1", "E2", "E3", "KE", "SG", "O", "RS2", "ON"]

            def mkhs(i):
                h = NS()
                h.i = i
                h.rr = [0]
                h.Sb = sb(f"Sb{i}", [128, 128], BF16); h.BSb = Buf(f"Sb{i}")
                h.QD = sb(f"QD{i}", [128, 512], BF16); h.BQD = Buf(f"QD{i}")
                h.KI = sb(f"KI{i}", [128, 512], BF16); h.BKI = Buf(f"KI{i}")
                h.KET = sb(f"KET{i}", [128, 4, 128], BF16); h.BKET = Buf(f"KET{i}")
                h.SM = sb(f"SM{i}", [128, 128], BF16); h.BSM = Buf(f"SM{i}")
                h.T = {k: sb(f"hg{k}{i}", [128, 512], F32) for k in names}
                h.BT = {k: Buf(f"hg{k}{i}") for k in names}
                h.OSQ = sb(f"OSQ{i}", [128, 512], BF16); h.BOSQ = Buf(f"OSQ{i}")
                h.mids = sb(f"mids{i}", [128, 8], F32); h.Bmid = Buf(f"mids{i}")
                return h

            HS2 = [mkhs(0), mkhs(1)]
            P.op("vector", lambda e: e.memset(S[:], 0.0), writes=BSl)

            def hg_head(hd, n, subs, h):
                Sb, BSb, QD, BQD, KI, BKI, KET, BKET, SM, BSM = h.Sb, h.BSb, h.QD, h.BQD, h.KI, h.BKI, h.KET, h.BKET, h.SM, h.BSM
                T, BT, OSQ, BOSQ, mids, Bmid = h.T, h.BT, h.OSQ, h.BOSQ, h.mids, h.Bmid
                BS = BSl[hd]

                def next_ps():
                    i = 4 * h.i + h.rr[0]
                    h.rr[0] = (h.rr[0] + 1) % 4
                    return PS[i], BPS[i]

                FH, LF, K1, CUM, E1, E2, E3, KE, SG, O, RS2, ON = (T[k] for k in
                    ["FH", "LF", "K1", "CUM", "E1", "E2", "E3", "KE", "SG", "O", "RS2", "ON"])

                def proj(col0):
                    ps, bps = next_ps()
                    for kt in range(KT):
                        P.op("tensor", lambda e, kt=kt: e.matmul(ps[:, :n], whg[:, kt, col0:col0 + 128], xn[:, kt, :n],
                                                                 start=(kt == 0), stop=(kt == KT - 1)),
                             reads=[Bwhg, Bxn], writes=[bps], inc=(kt == KT - 1))
                    return ps, bps

                psf, bpsf = proj(1024 + hd * 128)
                P.op("scalar", lambda e: e.activation(FH[:, :n], psf[:, :n], AF.Sigmoid), reads=[bpsf], writes=[BT["FH"]])
                P.op("gpsimd", lambda e: e.tensor_scalar(FH[:, :n], FH[:, :n], oml[:, hd:hd + 1], lb[:, hd:hd + 1], ALU.mult, ALU.add),
                     reads=[BT["FH"], Blb], writes=[BT["FH"]])
                P.op("scalar", lambda e: e.activation(LF[:, :n], FH[:, :n], AF.Ln), reads=[BT["FH"]], writes=[BT["LF"]])
                P.op("gpsimd", lambda e: e.tensor_scalar(K1[:, :n], FH[:, :n], -1.0, 1.0, ALU.mult, ALU.add),
                     reads=[BT["FH"]], writes=[BT["K1"]])
                P.op("vector", lambda e: e.tensor_tensor_scan(CUM[:, :n], rmask[:, :n], LF[:, :n], 0.0, ALU.mult, ALU.add),
                     reads=[BT["LF"], Bcst], writes=[BT["CUM"]])
                for j, (o0, m) in enumerate(subs):
                    im, il = o0 + max(m // 2 - 1, 0), o0 + m - 1
                    P.op("scalar", lambda e, j=j, im=im: e.activation(mids[:, j:j + 1], CUM[:, im:im + 1], AF.Exp),
                         reads=[BT["CUM"]], writes=[Bmid])
                    P.op("scalar", lambda e, j=j, il=il: e.activation(mids[:, 4 + j:5 + j], CUM[:, il:il + 1], AF.Exp),
                         reads=[BT["CUM"]], writes=[Bmid])
                    P.op("scalar", lambda e, o0=o0, m=m, im=im: e.activation(E2[:, o0:o0 + m], CUM[:, o0:o0 + m], AF.Exp,
                                                                             scale=-1.0, bias=CUM[:, im:im + 1]),
                         reads=[BT["CUM"]], writes=[BT["E2"]])
                    P.op("scalar", lambda e, o0=o0, m=m, il=il: e.activation(E3[:, o0:o0 + m], CUM[:, o0:o0 + m], AF.Exp,
                                                                             scale=-1.0, bias=CUM[:, il:il + 1]),
                         reads=[BT["CUM"]], writes=[BT["E3"]])
                P.op("vector", lambda e: e.reciprocal(E1[:, :n], E2[:, :n]), reads=[BT["E2"]], writes=[BT["E1"]])
                psq, bpsq = proj(hd * 128)
                P.op("vector", lambda e: e.tensor_tensor(QD[:, :n], E1[:, :n], psq[:, :n], ALU.mult),
                     reads=[BT["E1"], bpsq], writes=[BQD])
                P.op("gpsimd", lambda e: e.tensor_tensor(KI[:, :n], K1[:, :n], E2[:, :n], ALU.mult),
                     reads=[BT["K1"], BT["E2"]], writes=[BKI])
                P.op("gpsimd", lambda e: e.tensor_tensor(KE[:, :n], K1[:, :n], E3[:, :n], ALU.mult),
                     reads=[BT["K1"], BT["E3"]], writes=[BT["KE"]])
                for j, (o0, m) in enumerate(subs):
                    pst, bpst = next_ps()
                    P.op("tensor", lambda e, o0=o0, m=m, pst=pst: e.transpose(pst[:m, :128], KE[:, o0:o0 + m], ident[:, :]),
                         reads=[BT["KE"], Bcst], writes=[bpst])
                    P.op("scalar", lambda e, j=j, m=m, pst=pst: e.copy(KET[:m, j, :], pst[:m, :128]), reads=[bpst], writes=[BKET])
                psg, bpsg = proj(3072 + hd * 128)
                P.op("scalar", lambda e: e.activation(SG[:, :n], psg[:, :n], AF.Silu), reads=[bpsg], writes=[BT["SG"]])
                for j, (o0, m) in enumerate(subs):
                    pss, bpss = next_ps()
                    P.op("tensor", lambda e, o0=o0, m=m, pss=pss: e.matmul(pss[:m, :m], KI[:, o0:o0 + m], QD[:, o0:o0 + m],
                                                                           start=True, stop=True),
                         reads=[BKI, BQD], writes=[bpss])
                    P.op("vector", lambda e, m=m, pss=pss: e.tensor_tensor(SM[:m, :m], pss[:m, :m], triu[:m, :m], ALU.mult),
                         reads=[bpss, Bcst], writes=[BSM])
                    P.op("gpsimd", lambda e, j=j: e.tensor_scalar(Sb[:], S[:, hd, :], mids[:, j:j + 1], None, ALU.mult),
                         reads=[BS, Bmid], writes=[BSb])
                    pso, bpso = next_ps()
                    P.op("tensor", lambda e, j=j, m=m, pso=pso: e.matmul(pso[:, :m], VT[:m, j, hd * 128:(hd + 1) * 128], SM[:m, :m],
                                                                         start=True, stop=False),
                         reads=[BVT, BSM], writes=[bpso], inc=False)
                    P.op("tensor", lambda e, o0=o0, m=m, pso=pso: e.matmul(pso[:, :m], Sb[:], QD[:, o0:o0 + m], start=False, stop=True),
                         reads=[BSb, BQD, BVT, BSM], writes=[bpso])
                    P.op("scalar", lambda e, o0=o0, m=m, pso=pso: e.copy(O[:, o0:o0 + m], pso[:, :m]), reads=[bpso], writes=[BT["O"]])
                    psu, bpsu = next_ps()
                    P.op("tensor", lambda e, j=j, m=m, psu=psu: e.matmul(psu[:, :128], KET[:m, j, :], VT[:m, j, hd * 128:(hd + 1) * 128],
                                                                         start=True, stop=True),
                         reads=[BKET, BVT], writes=[bpsu])
                    P.op("vector", lambda e, j=j, psu=psu: e.scalar_tensor_tensor(S[:, hd, :], S[:, hd, :], mids[:, 4 + j:5 + j], psu[:, :128],
                                                                                 ALU.mult, ALU.add),
                         reads=[BS, Bmid, bpsu], writes=[BS])
                P.op("gpsimd", lambda e: e.tensor_tensor(OSQ[:, :n], O[:, :n], O[:, :n], ALU.mult), reads=[BT["O"]], writes=[BOSQ])
                psn, bpsn = next_ps()
                P.op("tensor", lambda e: e.matmul(psn[:, :n], ones_bf[:], OSQ[:, :n], start=True, stop=True),
                     reads=[Bones, BOSQ], writes=[bpsn])
                P.op("scalar", lambda e: e.activation(RS2[:, :n], psn[:, :n], AF.Sqrt, bias=eps_t[:, 0:1], scale=1.0 / 128),
                     reads=[bpsn, Beps], writes=[BT["RS2"]])
                P.op("vector", lambda e: e.reciprocal(RS2[:, :n], RS2[:, :n]), reads=[BT["RS2"]], writes=[BT["RS2"]])
                P.op("vector", lambda e: e.scalar_tensor_tensor(ON[:, :n], O[:, :n], on[:, hd:hd + 1], RS2[:, :n], ALU.mult, ALU.mult),
                     reads=[BT["O"], Bgam, BT["RS2"]], writes=[BT["ON"]])
                P.op("gpsimd", lambda e: e.tensor_tensor(OG[:, hd, :n], ON[:, :n], SG[:, :n], ALU.mult),
                     reads=[BT["ON"], BT["SG"]], writes=[BOG])

            def hg_chunk(si):
                c0, n, lst = subs_of([si])
                subs = [(o0, min(128, n - o0)) for o0 in range(0, n, 128)]
                load_h([si])
                norm([si], sidx)
                for j, (o0, m) in enumerate(subs):
                    for half in range(2):
                        ps, bps = next_ps()
                        for kt in range(KT):
                            P.op("tensor", lambda e, kt=kt, ps=ps, o0=o0, m=m, half=half: e.matmul(
                                ps[:m, :512], xn[:, kt, o0:o0 + m], whg[:, kt, 2048 + 512 * half:2048 + 512 * (half + 1)],
                                start=(kt == 0), stop=(kt == KT - 1)),
                                reads=[Bwhg, Bxn], writes=[bps], inc=(kt == KT - 1))
                        P.op("scalar", lambda e, ps=ps, j=j, m=m, half=half: e.copy(VT[:m, j, 512 * half:512 * (half + 1)], ps[:m, :512]),
                             reads=[bps], writes=[BVT])
                for hd in range(0, KT, 2):
                    P.emit_interleaved([P.capture(lambda: hg_head(hd, n, subs, HS2[0])),
                                        P.capture(lambda: hg_head(hd + 1, n, subs, HS2[1]))])
                for o in range(KT):
                    ps, bps = next_ps()
                    for kt in range(KT):
                        P.op("tensor", lambda e, kt=kt, o=o, ps=ps: e.matmul(ps[:, :n], who[:, kt, o * 128:(o + 1) * 128], OG[:, kt, :n],
                                                                            start=(kt == 0), stop=(kt == KT - 1)),
                             reads=[Bwho, BOG], writes=[bps], inc=(kt == KT - 1))
                    P.op("vector", lambda e, o=o, ps=ps: e.tensor_tensor(ht[:, o, :n], ht[:, o, :n], ps[:, :n], ALU.add),
                         reads=[bps, Bht], writes=[Bht])
                store_h([si])

            for si in range(len(SUBS)):
                hg_chunk(si)

        hd2 = dscr("hd2", [D, L], F32)
        Bh2 = [Buf(f"hd2_{i}") for i in range(len(SUBS))]
        hd2_v = hd2.rearrange("(kt p) t -> p kt t", p=128)

        def sb_stage(sidx):
            for pas in range(2):
                with stage_scope(512):
                    sb_pass(sidx, pas)

        def sb_pass(sidx, pas):
            ht, xn = C.ht, C.xn
            wqkv = sb("wqkv", [128, KT, 1536], BF16); Bwqkv = Buf("wqkv")
            P.dma("gpsimd", wqkv[:], W["wqkv"][pas], writes=[Bwqkv])
            wo = sb("wo", [64, 8, D], BF16); Bwo = Buf("wo")
            P.dma("gpsimd", wo[:], W["wsbo"][pas], writes=[Bwo])
            Bcst = Buf("sbconst")
            qkn = sb("qkn", [128, 2], F32)
            P.dma("sync", qkn[:], W["qkn"], writes=[Bcst])
            P.op("vector", lambda e: e.tensor_scalar(qkn[:, 0:1], qkn[:, 0:1], 0.125, None, ALU.mult), reads=[Bcst], writes=[Bcst])
            bones = sb("bones", [128, 128], BF16)
            P.dma("gpsimd", bones[:], W["blockones"], writes=[Bcst])
            ones32 = sb("ones32", [128, 128], F32)
            P.dma("sync", ones32[:], W["ones32"], writes=[Bcst])
            lstr = sb("lstr", [128, 128], F32)
            P.dma("sync", lstr[:], W["lstrict"], writes=[Bcst])
            am = sb("am", [128, 2, 4, 512], F32)
            for a in range(2):
                for r in range(4):
                    P.dma("sync", am[:, a, r, :], W["amask"][a, r], writes=[Bcst])
            KS = sb("KS", [128, 4, L], BF16); BKS = Buf("KS")
            VS = sb("VS", [128, 33, 512], BF16); BVS = Buf("VS")
            Q = sb("Q", [128, 4, 512], BF16); BQ = Buf("Q")
            OA = sb("OA", [64, 8, 512], BF16); BOA = Buf("OA")
            SQ = sb("SQ", [128, 512], BF16); BSQ = Buf("SQ")
            RSq = sb("RSq", [128, 512], F32); BRSq = Buf("RSq")
            def ring(name, depth, dt=F32):
                return [sb(f"{name}{i}", [128, 512], dt) for i in range(depth)], [Buf(f"{name}{i}") for i in range(depth)]

            LD, BLD = ring("LD", 5)
            LK, BLK = ring("LK", 3)
            LSX, BLSX = ring("LSX", 2)
            NL, BNL = ring("NL", 3)
            ARG, BARG = ring("ARG", 2)
            WT, BWT = ring("WT", 2, BF16)
            zero_t = sb("zero_t", [128, 512], F32); Bzero = Buf("zero_t")
            P.op("vector", lambda e: e.memset(zero_t[:], 0.0), writes=[Bzero])
            src_v, src_B = (hd_v, Bh)
            dst_v, dst_B = (hd2_v, Bh2) if pas == 0 else (hd_v, Bh)

            def qk_norm(ps, bps, n, col, dst_fn, dB):
                P.op("scalar", lambda e: e.activation(SQ[:, :n], ps[:, :n], AF.Square), reads=[bps], writes=[BSQ])
                psn, bpsn = next_ps()
                P.op("tensor", lambda e: e.matmul(psn[:, :n], bones[:], SQ[:, :n], start=True, stop=True),
                     reads=[Bcst, BSQ], writes=[bpsn])
                P.op("scalar", lambda e: e.activation(RSq[:, :n], psn[:, :n], AF.Sqrt, bias=eps_t[:, 0:1], scale=1.0 / 64),
                     reads=[bpsn, Beps], writes=[BRSq])
                P.op("vector", lambda e: e.reciprocal(RSq[:, :n], RSq[:, :n]), reads=[BRSq], writes=[BRSq])
                P.op("vector", lambda e: e.scalar_tensor_tensor(dst_fn(), ps[:, :n], qkn[:, col:col + 1], RSq[:, :n], ALU.mult, ALU.mult),
                     reads=[bps, Bcst, BRSq], writes=[dB])

            def attention(n, c0):
                nkb = (c0 + n + 127) // 128
                tiles = [(hl, kb) for hl in range(8) for kb in range(nkb - 1, -1, -1)]

                def info(t):
                    hl, kb = tiles[t]
                    k0 = kb * 128
                    m = min(128, L - k0)
                    r = (k0 - c0) // 128 if k0 >= c0 else -1
                    return hl, kb, k0, m, r, (kb == nkb - 1), (kb == 0)

                PZ = [(PS[i], BPS[i]) for i in (0, 1, 2)]
                PB = [(PS[i], BPS[i]) for i in (3, 4)]
                PA = [(PS[i], BPS[i]) for i in (5, 6)]

                def ph0(t):
                    hl, kb, k0, m, r, first, last = info(t)
                    ft, half = hl // 2, hl % 2
                    psz, bpsz = PZ[t % 3]
                    P.op("tensor", lambda e: e.matmul(psz[:m, :n], KS[half * 64:(half + 1) * 64, ft, k0:k0 + m],
                                                      Q[half * 64:(half + 1) * 64, ft, :n], start=True, stop=True),
                         reads=[BKS, BQ], writes=[bpsz])

                def ph1(t):
                    hl, kb, k0, m, r, first, last = info(t)
                    psz, bpsz = PZ[t % 3]
                    ld, bld = LD[t % 5], BLD[t % 5]
                    P.op("scalar", lambda e: e.activation(ld[:m, :n], psz[:m, :n], AF.Exp, scale=-1.0), reads=[bpsz], writes=[bld])
                    P.op("scalar", lambda e: e.activation(ld[:m, :n], ld[:m, :n], AF.Ln, bias=one_t[:m, 0:1]), reads=[bld, Bone], writes=[bld])

                def ph2(t):
                    hl, kb, k0, m, r, first, last = info(t)
                    psz, bpsz = PZ[t % 3]
                    ld, bld = LD[t % 5], BLD[t % 5]
                    lk, blk = LK[t % 3], BLK[t % 3]
                    P.op("vector", lambda e: e.scalar_tensor_tensor(lk[:m, :n], psz[:m, :n], -1.0, ld[:m, :n], ALU.mult, ALU.subtract),
                         reads=[bpsz, bld], writes=[blk])
                    if r >= 0:
                        nl, bnl = NL[t % 3], BNL[t % 3]
                        P.op("gpsimd", lambda e: e.tensor_tensor(lk[:m, :n], lk[:m, :n], am[:m, 0, r, :n], ALU.mult),
                             reads=[blk, Bcst], writes=[blk])
                        P.op("gpsimd", lambda e: e.tensor_tensor(nl[:m, :n], am[:m, 1, r, :n], ld[:m, :n], ALU.subtract),
                             reads=[bld, Bcst], writes=[bnl])
                    if not first:
                        lsx, blsx = LSX[t % 2], BLSX[t % 2]
                        pl, bpl = LK[(t - 1) % 3], BLK[(t - 1) % 3]
                        mp = info(t - 1)[3]
                        if info(t - 1)[5]:
                            P.op("gpsimd", lambda e: e.tensor_copy(lsx[:, :n], zero_t[:, :n]), reads=[Bzero], writes=[blsx])
                            P.op("gpsimd", lambda e: e.tensor_copy(lsx[:mp, :n], pl[:mp, :n]), reads=[bpl], writes=[blsx])
                        else:
                            px, bpx = LSX[(t - 1) % 2], BLSX[(t - 1) % 2]
                            P.op("gpsimd", lambda e: e.tensor_tensor(lsx[:, :n], px[:, :n], pl[:, :n], ALU.add),
                                 reads=[bpx, bpl], writes=[blsx])

                def ph3(t):
                    hl, kb, k0, m, r, first, last = info(t)
                    lk, blk = LK[t % 3], BLK[t % 3]
                    psb, bpsb = PB[t % 2]
                    P.op("tensor", lambda e: e.matmul(psb[:m, :n], lstr[:m, :m], lk[:m, :n], start=True, stop=first),
                         reads=[Bcst, blk], writes=[bpsb], inc=first)
                    if not first:
                        lsx, blsx = LSX[t % 2], BLSX[t % 2]
                        P.op("tensor", lambda e: e.matmul(psb[:m, :n], ones32[:, :m], lsx[:, :n], start=False, stop=True),
                             reads=[Bcst, blk, blsx], writes=[bpsb])

                def ph4(t):
                    hl, kb, k0, m, r, first, last = info(t)
                    psb, bpsb = PB[t % 2]
                    arg, barg = ARG[t % 2], BARG[t % 2]
                    if r >= 0:
                        nl, bnl = NL[t % 3], BNL[t % 3]
                        P.op("vector", lambda e: e.tensor_tensor(arg[:m, :n], psb[:m, :n], nl[:m, :n], ALU.add),
                             reads=[bpsb, bnl], writes=[barg])
                    else:
                        ld, bld = LD[t % 5], BLD[t % 5]
                        P.op("vector", lambda e: e.tensor_tensor(arg[:m, :n], psb[:m, :n], ld[:m, :n], ALU.subtract),
                             reads=[bpsb, bld], writes=[barg])

                def ph5(t):
                    hl, kb, k0, m, r, first, last = info(t)
                    arg, barg = ARG[t % 2], BARG[t % 2]
                    wt, bwt = WT[t % 2], BWT[t % 2]
                    P.op("scalar", lambda e: e.activation(wt[:m, :n], arg[:m, :n], AF.Exp), reads=[barg], writes=[bwt])

                def ph6(t):
                    hl, kb, k0, m, r, first, last = info(t)
                    wt, bwt = WT[t % 2], BWT[t % 2]
                    acc, bacc = PA[hl % 2]
                    P.op("tensor", lambda e: e.matmul(acc[:64, :n], VS[:m, kb, hl * 64:(hl + 1) * 64], wt[:m, :n], start=first, stop=last),
                         reads=[BVS, bwt], writes=[bacc])
                    if last:
                        P.op("scalar", lambda e: e.copy(OA[:, hl, :n], acc[:64, :n]), reads=[bacc], writes=[BOA])

                pipeline(len(tiles), [ph0, ph1, ph2, ph3, ph4, ph5, ph6])

            def sb_chunk(si):
                c0, n, lst = subs_of([si])
                subs = [(o0, min(128, n - o0)) for o0 in range(0, n, 128)]
                P.dma("sync", ht[:, :, :n], src_v[:, :, c0:c0 + n], reads=[src_B[si]], writes=[Bht])
                norm([si], sidx)
                for ft in range(4):
                    for which in range(2):
                        ps, bps = next_ps()
                        cb = 512 * which + ft * 128
                        for kt in range(KT):
                            P.op("tensor", lambda e, kt=kt, ps=ps, cb=cb: e.matmul(ps[:, :n], wqkv[:, kt, cb:cb + 128], xn[:, kt, :n],
                                                                               start=(kt == 0), stop=(kt == KT - 1)),
                                 reads=[Bwqkv, Bxn], writes=[bps], inc=(kt == KT - 1))
                        if which == 0:
                            qk_norm(ps, bps, n, 0, lambda ft=ft: Q[:, ft, :n], BQ)
                        else:
                            qk_norm(ps, bps, n, 1, lambda ft=ft: KS[:, ft, c0:c0 + n], BKS)
                for (o0, m) in subs:
                    ps, bps = next_ps()
                    blk = (c0 + o0) // 128
                    for kt in range(KT):
                        P.op("tensor", lambda e, kt=kt, ps=ps, o0=o0, m=m: e.matmul(ps[:m, :512], xn[:, kt, o0:o0 + m], wqkv[:, kt, 1024:1536],
                                                                                 start=(kt == 0), stop=(kt == KT - 1)),
                             reads=[Bwqkv, Bxn], writes=[bps], inc=(kt == KT - 1))
                    P.op("scalar", lambda e, ps=ps, m=m, blk=blk: e.copy(VS[:m, blk, :], ps[:m, :512]), reads=[bps], writes=[BVS])
                if pas == 1:
                    P.dma("sync", ht[:, :, :n], hd2_v[:, :, c0:c0 + n], reads=[Bh2[si], Bxn], writes=[Bht])
                attention(n, c0)
                for o in range(KT):
                    ps, bps = next_ps()
                    for hl in range(8):
                        P.op("tensor", lambda e, hl=hl, o=o, ps=ps: e.matmul(ps[:, :n], wo[:, hl, o * 128:(o + 1) * 128], OA[:, hl, :n],
                                                                            start=(hl == 0), stop=(hl == 7)),
                             reads=[Bwo, BOA], writes=[bps], inc=(hl == 7))
                    P.op("vector", lambda e, o=o, ps=ps: e.tensor_tensor(ht[:, o, :n], ht[:, o, :n], ps[:, :n], ALU.add),
                         reads=[bps, Bht], writes=[Bht])
                P.dma("sync", dst_v[:, :, c0:c0 + n], ht[:, :, :n], reads=[Bht], writes=[dst_B[si]])

            for si in range(len(SUBS)):
                sb_chunk(si)

        @contextlib.contextmanager
        def sub_scope():
            prev = cur[0]
            with contextlib.ExitStack() as sst:
                cur[0] = sst
                yield
                P.barrier()
                P.replay()
                cur[0] = prev

        def bufs_of(c0, n):
            return [Bh[i] for i, (a, m) in enumerate(SUBS) if a < c0 + n and c0 < a + m]

        def sincos(sin_o, cos_o, ut, Fw, tmp, Btmp, rd, wr):
            K, V, S1, S2 = tmp
            P.op("vector", lambda e: e.tensor_copy(K[:, :Fw], ut), reads=rd, writes=[Btmp])
            P.op("vector", lambda e: e.tensor_tensor(V[:, :Fw], ut, K[:, :Fw], ALU.subtract), reads=rd + [Btmp], writes=[Btmp])
            P.op("scalar", lambda e: e.activation(S1[:, :Fw], V[:, :Fw], AF.Sin, scale=float(np.pi)), reads=[Btmp], writes=[Btmp])
            P.op("scalar", lambda e: e.activation(S2[:, :Fw], V[:, :Fw], AF.Sin, scale=float(np.pi / 2)), reads=[Btmp], writes=[Btmp])
            P.op("vector", lambda e: e.tensor_tensor(S2[:, :Fw], S2[:, :Fw], S2[:, :Fw], ALU.mult), reads=[Btmp], writes=[Btmp])
            P.op("vector", lambda e: e.tensor_scalar(S2[:, :Fw], S2[:, :Fw], -2.0, 1.0, ALU.mult, ALU.add), reads=[Btmp], writes=[Btmp])
            P.op("vector", lambda e: e.scalar_tensor_tensor(sin_o, S1[:, :Fw], 2.0, S2[:, :Fw], ALU.mult, ALU.mult),
                 reads=[Btmp], writes=wr)
            P.op("vector", lambda e: e.tensor_tensor(S1[:, :Fw], S1[:, :Fw], S1[:, :Fw], ALU.mult), reads=[Btmp], writes=[Btmp])
            P.op("vector", lambda e: e.tensor_scalar(cos_o, S1[:, :Fw], -2.0, 1.0, ALU.mult, ALU.add), reads=[Btmp], writes=wr)

        def s5_stage(sidx):
            with stage_scope(256):
                s5_stage_(sidx)

        def s5_stage_(sidx):
            NT = 256
            TWO_PI_INV = float(1.0 / (2.0 * np.pi))
            ht, xn = C.ht, C.xn
            win = sb("win", [128, KT, D], BF16); Bwin = Buf("win")
            P.dma("gpsimd", win[:], W["ws5in"], writes=[Bwin])
            wgl = sb("wgl", [128, KT, 2 * D], BF16); Bwgl = Buf("wgl")
            P.dma("gpsimd", wgl[:], W["ws5glu"], writes=[Bwgl])
            COS = sb("COS", [128, 32, NT], F32); SIN = sb("SIN", [128, 32, NT], F32); Btab = Buf("s5tab")
            VRE = sb("VRE", [128, 32, 128], BF16); VIM = sb("VIM", [128, 32, 128], BF16); BV = Buf("s5V")
            CRE = sb("CRE", [128, 32, 128], BF16); CIN = sb("CIN", [128, 32, 128], BF16); BCm = Buf("s5C")
            P.dma("gpsimd", CRE[:], W["s5c"][0], writes=[BCm])
            P.dma("gpsimd", CIN[:], W["s5c"][1], writes=[BCm])
            P.op("vector", lambda e: e.tensor_scalar(CIN[:], CIN[:], -1.0, None, ALU.mult), reads=[BCm], writes=[BCm])
            rdec = sb("rdec", [128, 32], F32); thn = sb("thn", [128, 32], F32); Blane = Buf("s5lane")
            dsk = sb("dsk", [128, KT], F32)
            P.dma("sync", dsk[:], W["s5d"], writes=[Blane])
            CR = sb("CR", [128, 32], F32); CI = sb("CI", [128, 32], F32); BCR = Buf("CR"); BCI = Buf("CI")
            P.op("vector", lambda e: e.memset(CR[:], 0.0), writes=[BCR])
            P.op("vector", lambda e: e.memset(CI[:], 0.0), writes=[BCI])
            with sub_scope():
                LP = sb("LP", [128, 3, 32], F32)
                P.dma("sync", LP[:], W["s5lane"], writes=[Blane])
                P.op("scalar", lambda e: e.activation(LP[:, 2, :], LP[:, 2, :], AF.Exp), reads=[Blane], writes=[Blane])
                P.op("vector", lambda e: e.tensor_tensor(rdec[:], LP[:, 0, :], LP[:, 2, :], ALU.mult), reads=[Blane], writes=[Blane])
                P.op("scalar", lambda e: e.activation(rdec[:], rdec[:], AF.Exp), reads=[Blane], writes=[Blane])
                P.op("vector", lambda e: e.tensor_tensor(thn[:], LP[:, 1, :], LP[:, 2, :], ALU.mult), reads=[Blane], writes=[Blane])
                P.op("vector", lambda e: e.tensor_scalar(thn[:], thn[:], TWO_PI_INV, None, ALU.mult), reads=[Blane], writes=[Blane])
                tau1 = sb("tau1", [128, NT], F32)
                P.dma("sync", tau1[:], W["tau1"], writes=[Blane])
                Ut = sb("Ut", [128, 512], F32); BUt = Buf("Ut")
                tmp = (sb("tK", [128, 512], I32), sb("tV", [128, 512], F32), sb("tS1", [128, 512], F32), sb("tS2", [128, 512], F32))
                Btmp = Buf("sctmp")
                for pr in range(32):
                    P.op("vector", lambda e, pr=pr: e.tensor_scalar(Ut[:, :NT], tau1[:], thn[:, pr:pr + 1], None, ALU.mult),
                         reads=[Blane], writes=[BUt])
                    sincos(SIN[:, pr, :], COS[:, pr, :], Ut[:, :NT], NT, tmp, Btmp, [BUt], [Btab])
                RW_ = sb("RWt", [128, 3, 512], F32); BRW = Buf("RWt")
                nm = ["dt", "mag", "sn", "cs", "abr", "abi", "den", "cr", "ci", "t1"]
                Z = {k: sb("z" + k, [128, 512], F32) for k in nm}
                BZ = Buf("zz")
                BP = sb("BP", [128, 2, 4, 128], F32); BBP = Buf("BP")
                Vt = sb("Vt", [128, 512], F32)
                for qt in range(8):
                    l0 = qt * 512
                    P.dma("sync", RW_[:], W["s5row"][:, :, l0:l0 + 512], writes=[BRW])
                    P.dma("sync", BP[:, 0], W["s5b"][0][:, qt * 4:(qt + 1) * 4, :], writes=[BBP])
                    P.dma("sync", BP[:, 1], W["s5b"][1][:, qt * 4:(qt + 1) * 4, :], writes=[BBP])
                    lr_, li_ = RW_[:, 0, :], RW_[:, 1, :]
                    P.op("scalar", lambda e: e.activation(Z["dt"][:], RW_[:, 2, :], AF.Exp), reads=[BRW], writes=[BZ])
                    P.op("vector", lambda e: e.tensor_tensor(Z["mag"][:], lr_, Z["dt"][:], ALU.mult), reads=[BRW, BZ], writes=[BZ])
                    P.op("scalar", lambda e: e.activation(Z["mag"][:], Z["mag"][:], AF.Exp), reads=[BZ], writes=[BZ])
                    P.op("vector", lambda e: e.tensor_tensor(Ut[:], li_, Z["dt"][:], ALU.mult), reads=[BRW, BZ], writes=[BUt])
                    P.op("vector", lambda e: e.tensor_scalar(Ut[:], Ut[:], TWO_PI_INV, None, ALU.mult), reads=[BUt], writes=[BUt])
                    sincos(Z["sn"][:], Z["cs"][:], Ut[:], 512, tmp, Btmp, [BUt], [BZ])
                    P.op("vector", lambda e: e.tensor_tensor(Z["abr"][:], Z["mag"][:], Z["cs"][:], ALU.mult), reads=[BZ], writes=[BZ])
                    P.op("vector", lambda e: e.tensor_scalar(Z["abr"][:], Z["abr"][:], -1.0, None, ALU.add), reads=[BZ], writes=[BZ])
                    P.op("vector", lambda e: e.tensor_tensor(Z["abi"][:], Z["mag"][:], Z["sn"][:], ALU.mult), reads=[BZ], writes=[BZ])
                    P.op("vector", lambda e: e.tensor_tensor(Z["den"][:], lr_, lr_, ALU.mult), reads=[BRW], writes=[BZ])
                    P.op("vector", lambda e: e.tensor_tensor(Z["t1"][:], li_, li_, ALU.mult), reads=[BRW], writes=[BZ])
                    P.op("vector", lambda e: e.tensor_tensor(Z["den"][:], Z["den"][:], Z["t1"][:], ALU.add), reads=[BZ], writes=[BZ])
                    P.op("vector", lambda e: e.reciprocal(Z["den"][:], Z["den"][:]), reads=[BZ], writes=[BZ])
                    P.op("vector", lambda e: e.tensor_tensor(Z["cr"][:], Z["abr"][:], lr_, ALU.mult), reads=[BZ, BRW], writes=[BZ])
                    P.op("vector", lambda e: e.tensor_tensor(Z["t1"][:], Z["abi"][:], li_, ALU.mult), reads=[BZ, BRW], writes=[BZ])
                    P.op("vector", lambda e: e.tensor_tensor(Z["cr"][:], Z["cr"][:], Z["t1"][:], ALU.add), reads=[BZ], writes=[BZ])
                    P.op("vector", lambda e: e.tensor_tensor(Z["cr"][:], Z["cr"][:], Z["den"][:], ALU.mult), reads=[BZ], writes=[BZ])
                    P.op("vector", lambda e: e.tensor_tensor(Z["ci"][:], Z["abi"][:], lr_, ALU.mult), reads=[BZ, BRW], writes=[BZ])
                    P.op("vector", lambda e: e.tensor_tensor(Z["t1"][:], Z["abr"][:], li_, ALU.mult), reads=[BZ, BRW], writes=[BZ])
                    P.op("vector", lambda e: e.tensor_tensor(Z["ci"][:], Z["ci"][:], Z["t1"][:], ALU.subtract), reads=[BZ], writes=[BZ])
                    P.op("vector", lambda e: e.tensor_tensor(Z["ci"][:], Z["ci"][:], Z["den"][:], ALU.mult), reads=[BZ], writes=[BZ])
                    bre = BP[:, 0].rearrange("p a l -> p (a l)")
                    bim = BP[:, 1].rearrange("p a l -> p (a l)")
                    vre = VRE[:, qt * 4:(qt + 1) * 4, :].rearrange("p a l -> p (a l)")
                    vim = VIM[:, qt * 4:(qt + 1) * 4, :].rearrange("p a l -> p (a l)")
                    P.op("vector", lambda e, bre=bre: e.tensor_tensor(Vt[:], Z["cr"][:], bre, ALU.mult), reads=[BZ, BBP], writes=[BUt])
                    P.op("vector", lambda e, bim=bim: e.tensor_tensor(Z["t1"][:], Z["ci"][:], bim, ALU.mult), reads=[BZ, BBP], writes=[BZ])
                    P.op("vector", lambda e, vre=vre: e.tensor_tensor(vre, Vt[:], Z["t1"][:], ALU.subtract), reads=[BZ, BUt], writes=[BV])
                    P.op("vector", lambda e, bim=bim: e.tensor_tensor(Vt[:], Z["cr"][:], bim, ALU.mult), reads=[BZ, BBP], writes=[BUt])
                    P.op("vector", lambda e, bre=bre: e.tensor_tensor(Z["t1"][:], Z["ci"][:], bre, ALU.mult), reads=[BZ, BBP], writes=[BZ])
                    P.op("vector", lambda e, vim=vim: e.tensor_tensor(vim, Vt[:], Z["t1"][:], ALU.add), reads=[BZ, BUt], writes=[BV])
            UF = sb("UF", [128, KT, NT], F32); BUF = Buf("UF")
            UB = sb("UB", [128, KT, NT], BF16); BUB = Buf("UB")
            YG = sb("YG", [128, KT, NT], BF16); BYG = Buf("YG")

            def ring(name, depth, dt=F32):
                return [sb(f"s5{name}{i}", [128, NT], dt) for i in range(depth)], [Buf(f"s5{name}{i}") for i in range(depth)]

            XR, BXR = ring("XR", 2); XI, BXI = ring("XI", 2)
            XPR, BXPR = ring("XPR", 2); XPI, BXPI = ring("XPI", 2)
            WR, BWR = ring("WR", 2); WI, BWI = ring("WI", 2)
            SR, BSR = ring("SR", 2); SI, BSI = ring("SI", 2)
            TA, BTA = ring("TA", 8)
            SRb, BSRb = ring("SRb", 2, BF16); SIb, BSIb = ring("SIb", 2, BF16)
            Yt_, BYt = sb("s5Yt", [128, NT], F32), Buf("s5Yt")
            Tg_, BTg = sb("s5Tg", [128, NT], F32), Buf("s5Tg")
            T2_, BT2 = sb("s5T2", [128, NT], F32), Buf("s5T2")

            def tt(eng, o, a, b, op, rd, wr):
                P.op(eng, lambda e: e.tensor_tensor(o, a, b, op), reads=rd, writes=wr)

            def s5_pairs(n):
                PR = [(PS[i], BPS[i]) for i in (0, 1)]
                PI = [(PS[i], BPS[i]) for i in (2, 3)]
                PY = [(PS[i], BPS[i]) for i in (4, 5)]

                def ph0(t):
                    ft = t // 4
                    (psr, bpsr), (psi, bpsi) = PR[t % 2], PI[t % 2]
                    P.op("tensor", lambda e: e.matmul(psr[:, :n], VRE[:, t, :], UB[:, ft, :n], start=True, stop=True),
                         reads=[BV, BUB], writes=[bpsr])
                    P.op("tensor", lambda e: e.matmul(psi[:, :n], VIM[:, t, :], UB[:, ft, :n], start=True, stop=True),
                         reads=[BV, BUB], writes=[bpsi])

                def ph1(t):
                    (psr, bpsr), (psi, bpsi) = PR[t % 2], PI[t % 2]
                    P.op("scalar", lambda e: e.copy(XR[t % 2][:, :n], psr[:, :n]), reads=[bpsr], writes=[BXR[t % 2]])
                    P.op("scalar", lambda e: e.copy(XI[t % 2][:, :n], psi[:, :n]), reads=[bpsi], writes=[BXI[t % 2]])

                def ph2(t):
                    cs, sn = COS[:, t, :n], SIN[:, t, :n]
                    xr, xi = XR[t % 2][:, :n], XI[t % 2][:, :n]
                    a1, a2, a3, a4 = (TA[i][:, :n] for i in range(4))
                    tt("gpsimd", a1, cs, xr, ALU.mult, [Btab, BXR[t % 2]], [BTA[0]])
                    tt("gpsimd", a2, sn, xi, ALU.mult, [Btab, BXI[t % 2]], [BTA[1]])
                    tt("gpsimd", XPR[t % 2][:, :n], a1, a2, ALU.add, [BTA[0], BTA[1]], [BXPR[t % 2]])
                    tt("vector", a3, cs, xi, ALU.mult, [Btab, BXI[t % 2]], [BTA[2]])
                    tt("vector", a4, sn, xr, ALU.mult, [Btab, BXR[t % 2]], [BTA[3]])
                    tt("vector", XPI[t % 2][:, :n], a3, a4, ALU.subtract, [BTA[2], BTA[3]], [BXPI[t % 2]])

                def ph3(t):
                    rb = rdec[:, t:t + 1].to_broadcast([128, n])
                    P.op("vector", lambda e: e.tensor_tensor_scan(WR[t % 2][:, :n], rb, XPR[t % 2][:, :n], CR[:, t:t + 1], ALU.mult, ALU.add),
                         reads=[Blane, BXPR[t % 2], BCR], writes=[BWR[t % 2]])
                    P.op("vector", lambda e: e.tensor_tensor_scan(WI[t % 2][:, :n], rb, XPI[t % 2][:, :n], CI[:, t:t + 1], ALU.mult, ALU.add),
                         reads=[Blane, BXPI[t % 2], BCI], writes=[BWI[t % 2]])

                def ph4(t):
                    cs, sn = COS[:, t, :n], SIN[:, t, :n]
                    wr, wi = WR[t % 2][:, :n], WI[t % 2][:, :n]
                    a1, a2, a3, a4 = (TA[i][:, :n] for i in range(4, 8))
                    tt("gpsimd", a1, cs, wr, ALU.mult, [Btab, BWR[t % 2]], [BTA[4]])
                    tt("gpsimd", a2, sn, wi, ALU.mult, [Btab, BWI[t % 2]], [BTA[5]])
                    tt("gpsimd", SR[t % 2][:, :n], a1, a2, ALU.subtract, [BTA[4], BTA[5]], [BSR[t % 2]])
                    tt("vector", a3, cs, wi, ALU.mult, [Btab, BWI[t % 2]], [BTA[6]])
                    tt("vector", a4, sn, wr, ALU.mult, [Btab, BWR[t % 2]], [BTA[7]])
                    tt("vector", SI[t % 2][:, :n], a3, a4, ALU.add, [BTA[6], BTA[7]], [BSI[t % 2]])
                    P.op("gpsimd", lambda e: e.tensor_copy(CR[:, t:t + 1], SR[t % 2][:, n - 1:n]), reads=[BSR[t % 2]], writes=[BCR])
                    P.op("vector", lambda e: e.tensor_copy(CI[:, t:t + 1], SI[t % 2][:, n - 1:n]), reads=[BSI[t % 2]], writes=[BCI])

                def ph5(t):
                    P.op("scalar", lambda e: e.copy(SRb[t % 2][:, :n], SR[t % 2][:, :n]), reads=[BSR[t % 2]], writes=[BSRb[t % 2]])
                    P.op("scalar", lambda e: e.copy(SIb[t % 2][:, :n], SI[t % 2][:, :n]), reads=[BSI[t % 2]], writes=[BSIb[t % 2]])

                def ph6(t):
                    ft, q = t // 4, t % 4
                    yps, byps = PY[ft % 2]
                    P.op("tensor", lambda e: e.matmul(yps[:, :n], CRE[:, t, :], SRb[t % 2][:, :n], start=(q == 0), stop=False),
                         reads=[BCm, BSRb[t % 2]], writes=[byps])
                    P.op("tensor", lambda e: e.matmul(yps[:, :n], CIN[:, t, :], SIb[t % 2][:, :n], start=False, stop=(q == 3)),
                         reads=[BCm, BSIb[t % 2]], writes=[byps])
                    if q == 3:
                        Yt, Tg = Yt_[:, :n], Tg_[:, :n]
                        P.op("vector", lambda e: e.scalar_tensor_tensor(Yt, UF[:, ft, :n], dsk[:, ft:ft + 1], yps[:, :n], ALU.mult, ALU.add),
                             reads=[BUF, Blane, byps], writes=[BYt])
                        tt("gpsimd", Tg, Yt, Yt, ALU.mult, [BYt], [BTg])
                        P.op("gpsimd", lambda e: e.tensor_scalar(Tg, Tg, 0.044715, 1.0, ALU.mult, ALU.add), reads=[BTg], writes=[BTg])
                        tt("gpsimd", Tg, Tg, Yt, ALU.mult, [BTg, BYt], [BTg])
                        P.op("scalar", lambda e: e.activation(Tg, Tg, AF.Sigmoid, scale=1.5957691216), reads=[BTg], writes=[BTg])
                        P.op("gpsimd", lambda e: e.tensor_tensor(YG[:, ft, :n], Tg, Yt, ALU.mult), reads=[BTg, BYt], writes=[BYG])

                pipeline(32, [ph0, ph1, ph2, ph3, ph4, ph5, ph6])

            def s5_chunk(c0, n):
                hb = bufs_of(c0, n)
                P.dma("sync", ht[:, :, :n], hd_v[:, :, c0:c0 + n], reads=hb, writes=[Bht])
                norm_n(n, sidx)
                for ft in range(KT):
                    ps, bps = next_ps()
                    for kt in range(KT):
                        P.op("tensor", lambda e, kt=kt, ps=ps, ft=ft: e.matmul(ps[:, :n], win[:, kt, ft * 128:(ft + 1) * 128], xn[:, kt, :n],
                                                                           start=(kt == 0), stop=(kt == KT - 1)),
                             reads=[Bwin, Bxn], writes=[bps], inc=(kt == KT - 1))
                    P.op("scalar", lambda e, ps=ps, ft=ft: e.copy(UF[:, ft, :n], ps[:, :n]), reads=[bps], writes=[BUF])
                    P.op("gpsimd", lambda e, ft=ft: e.tensor_copy(UB[:, ft, :n], UF[:, ft, :n]), reads=[BUF], writes=[BUB])
                s5_pairs(n)
                T2 = T2_[:, :n]
                for o in range(KT):
                    psa, bpsa = next_ps()
                    psg, bpsg = next_ps()
                    for (pp, bpp, cb) in ((psa, bpsa, o * 128), (psg, bpsg, D + o * 128)):
                        for kt in range(KT):
                            P.op("tensor", lambda e, kt=kt, pp=pp, cb=cb: e.matmul(pp[:, :n], wgl[:, kt, cb:cb + 128], YG[:, kt, :n],
                                                                               start=(kt == 0), stop=(kt == KT - 1)),
                                 reads=[Bwgl, BYG], writes=[bpp], inc=(kt == KT - 1))
                    P.op("scalar", lambda e, psg=psg: e.activation(T2, psg[:, :n], AF.Sigmoid), reads=[bpsg], writes=[BT2])
                    P.op("vector", lambda e, psa=psa: e.tensor_tensor(T2, T2, psa[:, :n], ALU.mult), reads=[BT2, bpsa], writes=[BT2])
                    P.op("gpsimd", lambda e, o=o: e.tensor_tensor(ht[:, o, :n], ht[:, o, :n], T2, ALU.add), reads=[BT2, Bht], writes=[Bht])
                P.dma("sync", hd_v[:, :, c0:c0 + n], ht[:, :, :n], reads=[Bht], writes=hb)

            for c0 in range(0, L, NT):
                s5_chunk(c0, min(NT, L - c0))

        for s in stages:
            if s[0] == "s5":
                s5_stage(0)
            if s[0] == "sb":
                sb_stage(4)
            if s[0] == "hg":
                hg_stage(6)
            if s[0] == "ffn":
                ffn_stage(s[1], 2 * s[1] + 1)
            elif s[0] == "rg":
                rg_stage(2)

        P.dma("sync", out, hd[:, NMETA:], reads=Bh, key="final")
        Bfin = Buf("fin")
        P.wait_all("sync", Bh)
        P.q["sync"].append((None, [("final", P.semval["final"])], None))
        P.replay()
        print("ops:", P.nops, {k: len(v) for k, v in P.q.items()}, "sems:", len(P.sems))
    return nc


ALL_STAGES = [("s5",), ("ffn", 0), ("rg",), ("ffn", 1), ("sb",), ("ffn", 2), ("hg",), ("ffn", 3)]


def kernel(**inputs):
    inp = {k: np.asarray(v) for k, v in inputs.items()}
    wl = host_layout(inp)
    wshapes = {k: v.shape for k, v in wl.items()}
    nc = build(ALL_STAGES, wshapes)
    B = inp["x"].shape[0]
    in_maps = []
    for b in range(B):
        h0 = np.concatenate([inp["meta_tokens"], inp["x"][b]], axis=0)
        m = dict(wl)
        m["h0"] = np.ascontiguousarray(h0.T)
        in_maps.append(m)
    res = run_bass_kernel_spmd(nc, in_maps, core_ids=list(range(B)))
    outs = [np.ascontiguousarray(r["out"].T) for r in res.results]
    return np.stack(outs, 0).astype(np.float32)
```

```python
import contextlib
import numpy as np
import concourse.bass as bass
import concourse.mybir as mybir
from concourse.bass_utils import run_bass_kernel_spmd

F32 = mybir.dt.float32
BF16 = mybir.dt.bfloat16
I32 = mybir.dt.int32
AF = mybir.ActivationFunctionType
ALU = mybir.AluOpType

ENGS = ["tensor", "vector", "scalar", "gpsimd", "sync"]

L = 4112
D = 1024
KT = 8
DFF = 2816
NMETA = 16
EPS = 1e-6
SUBS = [(i * 512, 512) for i in range(8)] + [(4096, 16)]


class Buf:
    def __init__(self, name):
        self.name = name
        self.w = None
        self.r = []


class Prog:
    def __init__(self, nc, stack):
        self.nc = nc
        self.stack = stack
        self.q = {e: [] for e in ENGS}
        self.waited = {e: {} for e in ENGS}
        self.sems = {}
        self.semval = {}
        self.nops = 0
        for e in ENGS:
            self._sem("eng_" + e)

    def _sem(self, key):
        if key not in self.sems:
            self.sems[key] = self.stack.enter_context(self.nc.semaphore("s_" + key))
            self.semval[key] = 0
        return self.sems[key]

    def _collect(self, eng, reads, writes):
        need = {}

        def add(ev):
            k, v = ev
            if k == "eng_tensor" and eng == "tensor":
                return
            if v > need.get(k, 0):
                need[k] = v

        for b in reads:
            if b.w is not None:
                add(b.w)
        for b in writes:
            if b.w is not None:
                add(b.w)
            for ev in b.r:
                add(ev)
        out = []
        wd = self.waited[eng]
        for k, v in need.items():
            if wd.get(k, 0) >= v:
                continue
            wd[k] = v
            out.append((k, v))
        return out

    def _mark(self, ev, reads, writes):
        for b in writes:
            b.w = ev
            b.r = []
        for b in reads:
            if b not in writes:
                if len(b.r) > 24:
                    m = {}
                    for k, v in b.r:
                        if v > m.get(k, 0):
                            m[k] = v
                    b.r = list(m.items())
                b.r.append(ev)

    _cap = None

    def capture(self, fn):
        self._cap = []
        fn()
        c, self._cap = self._cap, None
        return c

    def emit_interleaved(self, lists):
        def items(l):
            out, curi = [], []
            for c in l:
                curi.append(c)
                if not (c[0] == "op" and c[2].get("inc") is False):
                    out.append(curi)
                    curi = []
            if curi:
                out.append(curi)
            return out

        lists = [items(l) for l in lists]
        for j in range(max(len(l) for l in lists)):
            for l in lists:
                if j < len(l):
                    for kind, a, kw in l[j]:
                        (self.op if kind == "op" else self.dma)(*a, **kw)

    def op(self, eng, fn, reads=(), writes=(), inc=True):
        if self._cap is not None:
            self._cap.append(("op", (eng, fn), dict(reads=list(reads), writes=list(writes), inc=inc)))
            return
        waits = self._collect(eng, reads, writes)
        if inc:
            k = "eng_" + eng
            self.semval[k] += 1
            self._mark((k, self.semval[k]), reads, writes)
        self.q[eng].append((fn, waits, ("eng_" + eng, 1) if inc else None))
        self.nops += 1

    def dma(self, eng, out, in_, reads=(), writes=(), key=None, **kw):
        if self._cap is not None:
            self._cap.append(("dma", (eng, out, in_), dict(reads=list(reads), writes=list(writes), key=key, **kw)))
            return
        waits = self._collect(eng, reads, writes)
        if key is None:
            key = "d_" + (writes[0].name if writes else reads[0].name + "_o")
        self._sem(key)
        self.semval[key] += 16
        self._mark((key, self.semval[key]), reads, writes)
        self.q[eng].append((lambda e: e.dma_start(out=out, in_=in_, **kw), waits, (key, 16)))
        self.nops += 1

    def wait_all(self, eng, bufs):
        waits = self._collect(eng, bufs, bufs)
        self.q[eng].append((None, waits, None))

    def barrier(self):
        for e in ENGS:
            wd = self.waited[e]
            waits = []
            for k, v in self.semval.items():
                if v > wd.get(k, 0):
                    wd[k] = v
                    waits.append((k, v))
            self.q[e].append((None, waits, None))

    def replay(self):
        self._replay()
        self.q = {e: [] for e in ENGS}

    def _replay(self):
        with self.nc.Block() as block:
            for ename in ENGS:
                ops = self.q[ename]
                if not ops:
                    continue

                def body(e, ops=ops):
                    for fn, waits, inc in ops:
                        for k, v in waits:
                            e.wait_ge(self.sems[k], v)
                        if fn is None:
                            continue
                        ins = fn(e)
                        if inc is not None:
                            ins.then_inc(self.sems[inc[0]], inc[1])

                getattr(block, ename)(body)


def host_layout(inp):
    f = np.float32
    o = {}
    g = np.stack([inp["norm_mix"][0], inp["norm_ffn"][0], inp["norm_mix"][1], inp["norm_ffn"][1],
                  inp["norm_mix"][2], inp["norm_ffn"][2], inp["norm_mix"][3], inp["norm_ffn"][3]], 0)
    o["gn"] = np.ascontiguousarray(g.reshape(8, KT, 128).transpose(2, 0, 1)).astype(f)
    wu = inp["ffn_w_up"].reshape(4, KT, 128, 2, 11, 256)
    o["wup"] = np.ascontiguousarray(wu.transpose(0, 4, 2, 1, 3, 5)).reshape(4, 11, 128, KT * 512).astype(f)
    wd = inp["ffn_w_down"].reshape(4, 22, 128, 8, 128)
    o["wdn"] = np.ascontiguousarray(wd.transpose(0, 3, 2, 1, 4)).reshape(4, 8, 128, 22 * 128).astype(f)
    cw = np.concatenate([inp["ffn_conv_w"], inp["ffn_conv_b"][:, None, :]], 1)
    o["fcw"] = np.ascontiguousarray(cw.reshape(4, 4, 44, 128).transpose(3, 0, 2, 1)).reshape(128, 4 * 44 * 4).astype(f)
    o["wrgin"] = np.ascontiguousarray(inp["rg_w_in"][0].reshape(KT, 128, 2688).transpose(1, 0, 2)).astype(f)
    o["wrgout"] = np.ascontiguousarray(inp["rg_w_out"][0].reshape(16, 84, D).transpose(1, 0, 2)).astype(f)
    o["wrga"] = np.ascontiguousarray(inp["rg_w_a"][0].transpose(1, 0, 2)).astype(f)
    o["wrgi"] = np.ascontiguousarray(inp["rg_w_i"][0].transpose(1, 0, 2)).astype(f)
    pr = np.concatenate([inp["rg_conv_w"][0], inp["rg_conv_b"], inp["rg_b_a"], inp["rg_b_i"], inp["rg_lambda"]], 0)
    o["rgp"] = np.ascontiguousarray(pr.reshape(8, 16, 84).transpose(2, 1, 0)).astype(f)
    o["whgin"] = np.ascontiguousarray(inp["hg_w_in"][0].reshape(KT, 128, 4096).transpose(1, 0, 2)).astype(f)
    o["whgout"] = np.ascontiguousarray(inp["hg_w_out"][0].reshape(KT, 128, D).transpose(1, 0, 2)).astype(f)
    o["hgg"] = np.ascontiguousarray(inp["hg_gamma"].reshape(4, KT, 128).transpose(2, 0, 1)).astype(f)
    o["hgon"] = np.ascontiguousarray(inp["hg_o_norm"][0].reshape(KT, 128).T).astype(f)
    wq = inp["sb_w_qkv"][0].reshape(KT, 128, 3, 2, 512)
    o["wqkv"] = np.ascontiguousarray(wq.transpose(3, 1, 0, 2, 4)).reshape(2, 128, KT, 1536).astype(f)
    wo = inp["sb_w_out"][0].reshape(2, 8, 64, D)
    o["wsbo"] = np.ascontiguousarray(wo.transpose(0, 2, 1, 3)).astype(f)
    o["qkn"] = np.ascontiguousarray(np.stack([np.tile(inp["sb_q_norm"][0], 2), np.tile(inp["sb_k_norm"][0], 2)], 1)).astype(f)
    bo = np.zeros((128, 128), f)
    bo[:64, :64] = 1.0
    bo[64:, 64:] = 1.0
    o["blockones"] = bo
    o["ones32"] = np.ones((128, 128), f)
    o["lstrict"] = np.tril(np.ones((128, 128), f), -1)
    jj = np.arange(128)[:, None]
    tt = np.arange(512)[None, :]
    am = np.zeros((2, 4, 128, 512), f)
    for r in range(4):
        past = (128 * r + jj) < tt
        am[0, r] = past.astype(f)
        am[1, r] = np.where(past, 0.0, -30000.0)
    o["amask"] = am
    o["ws5in"] = np.ascontiguousarray(inp["s5_w_in"][0].reshape(KT, 128, D).transpose(1, 0, 2)).astype(f)
    o["ws5glu"] = np.ascontiguousarray(inp["s5_w_glu"][0].reshape(KT, 128, 2 * D).transpose(1, 0, 2)).astype(f)
    lr, li, ldt = inp["s5_lam_re"][0], inp["s5_lam_im"][0], inp["s5_log_dt"][0]
    ldt2 = np.repeat(ldt[:, None], 64, 1)
    lane = lambda a: a.reshape(32, 2, 64).transpose(1, 2, 0).reshape(128, 32)
    o["s5lane"] = np.ascontiguousarray(np.stack([lane(lr), lane(li), lane(ldt2)], 1)).astype(f)
    row = np.stack([lr.reshape(-1), li.reshape(-1), ldt2.reshape(-1)], 0)
    o["s5row"] = np.ascontiguousarray(np.broadcast_to(row[None], (128, 3, 4096))).astype(f)
    gg = np.arange(64)
    bpad = np.zeros((2, 128, 32, 128), f)
    cpad = np.zeros((2, 128, 32, 128), f)
    for g in range(64):
        r0, pr, l0 = (g % 8) * 16, g // 2, (g % 2) * 64
        bpad[0, r0:r0 + 16, pr, l0:l0 + 64] = inp["s5_b_re"][0][g].T
        bpad[1, r0:r0 + 16, pr, l0:l0 + 64] = inp["s5_b_im"][0][g].T
        cpad[0, l0:l0 + 64, pr, r0:r0 + 16] = inp["s5_c_re"][0][g].T
        cpad[1, l0:l0 + 64, pr, r0:r0 + 16] = inp["s5_c_im"][0][g].T
    o["s5b"] = bpad
    o["s5c"] = cpad
    o["s5d"] = np.ascontiguousarray(inp["s5_d"][0].reshape(KT, 128).T).astype(f)
    o["tau1"] = np.ascontiguousarray(np.broadcast_to(np.arange(1, 257, dtype=f)[None], (128, 256)))
    o["ident"] = np.eye(128, dtype=f)
    rm = np.ones((128, 512), f)
    rm[:, ::128] = 0.0
    o["rmask"] = rm
    o["triu"] = np.triu(np.ones((128, 128), f))
    return o


def build(stages, wshapes, debug_out=False):
    nc = bass.Bass("TRN2", target_bir_lowering=False)
    st = contextlib.ExitStack()
    with st:
        P = Prog(nc, st)

        def din(name, shape, dt=F32):
            return nc.dram_tensor(name, list(shape), dt, kind="ExternalInput").ap()

        def dscr(name, shape, dt):
            return nc.dram_tensor(name, list(shape), dt).ap()

        cur = [st]

        uid = [0]

        def sb(name, shape, dt):
            uid[0] += 1
            return cur[0].enter_context(nc.sbuf_tensor(f"sb{uid[0]}_{name}", list(shape), dt))

        class NS:
            pass

        C = NS()

        @contextlib.contextmanager
        def stage_scope(NT):
            with contextlib.ExitStack() as sst:
                cur[0] = sst
                C.ht = sb("ht", [128, KT, NT], F32)
                C.Bht = Bht
                C.Bxn = Bxn
                C.xn = sb("xn", [128, KT, NT], BF16)
                C.sq = sb("sq", [128, 2, NT], BF16)
                C.rs = sb("rs", [128, NT], F32)
                yield
                P.barrier()
                P.replay()
                cur[0] = st

        h0 = din("h0", [D, L])
        W = {k: din(k, s) for k, s in wshapes.items()}
        out = nc.dram_tensor("out", [D, L - NMETA], F32, kind="ExternalOutput").ap()
        hd = dscr("hd", [D, L], F32)
        Bh = [Buf(f"hd{i}") for i in range(len(SUBS))]
        hd_v = hd.rearrange("(kt p) t -> p kt t", p=128)

        PS = [st.enter_context(nc.psum_tensor(f"ps{i}", [128, 512], F32)) for i in range(8)]
        BPS = [Buf(f"ps{i}") for i in range(8)]
        ps_rr = [0]

        def pipeline(ntiles, phases):
            k = len(phases)
            for step in range(ntiles + k - 1):
                for p in range(k):
                    t = step - p
                    if 0 <= t < ntiles:
                        phases[p](t)

        ps_reserved = set()

        def next_ps():
            while True:
                i = ps_rr[0]
                ps_rr[0] = (i + 1) % 8
                if i not in ps_reserved:
                    return PS[i], BPS[i]

        ones_bf = sb("ones_bf", [128, 128], BF16)
        Bones = Buf("ones")
        P.op("vector", lambda e: e.memset(ones_bf[:], 1.0), writes=[Bones])
        gn = sb("gn", [128, 8, KT], F32)
        Bgn = Buf("gn")
        P.dma("sync", gn[:], W["gn"], writes=[Bgn])

        for i, (c0, n) in enumerate(SUBS):
            P.dma("sync", hd[:, c0:c0 + n], h0[:, c0:c0 + n], writes=[Bh[i]], key="hinit")

        W16 = {}
        BW16 = {}

        def precast(name, lead):
            src = W[name]
            dst = dscr(name + "16", src.shape, BF16)
            W16[name] = dst
            BW16[name] = Buf(name + "16")
            import itertools
            for idx in itertools.product(*[range(s) for s in src.shape[:lead]]):
                s_ap, d_ap = src, dst
                for i in idx:
                    s_ap, d_ap = s_ap[i], d_ap[i]
                P.dma("gpsimd", d_ap, s_ap, writes=[BW16[name]], key="precast")

        NTM = 1040
        Bht = Buf("ht")
        Bxn = Buf("xn")
        Bsq = Buf("sq")
        Brs = Buf("rs")

        def subs_of(ch):
            c0 = SUBS[ch[0]][0]
            lst = [(SUBS[i][0] - c0, SUBS[i][1]) for i in ch]
            return c0, sum(x[1] for x in lst), lst

        def load_h(ch):
            ht = C.ht
            c0, n, _ = subs_of(ch)
            P.dma("sync", ht[:, :, :n], hd_v[:, :, c0:c0 + n], reads=[Bh[i] for i in ch], writes=[C.Bht])

        def store_h(ch):
            ht = C.ht
            c0, n, _ = subs_of(ch)
            P.dma("sync", hd_v[:, :, c0:c0 + n], ht[:, :, :n], reads=[C.Bht], writes=[Bh[i] for i in ch])

        Bsq2 = [Buf("sq_a"), Buf("sq_b")]

        def norm_impl(n, lst, sidx):
            ht, xn, sq, rs = C.ht, C.xn, C.sq, C.rs
            banks = [next_ps() for _ in lst]
            for kt in range(KT):
                P.op("scalar", lambda e, kt=kt: e.activation(sq[:, kt % 2, :n], ht[:, kt, :n], AF.Square),
                     reads=[C.Bht], writes=[Bsq2[kt % 2]])
                for (off, nsb), (ps, bps) in zip(lst, banks):
                    P.op("tensor", lambda e, ps=ps, kt=kt, off=off, nsb=nsb: e.matmul(
                        ps[:, :nsb], ones_bf[:], sq[:, kt % 2, off:off + nsb], start=(kt == 0), stop=(kt == KT - 1)),
                        reads=[Bones, Bsq2[kt % 2]], writes=[bps])
            for (off, nsb), (ps, bps) in zip(lst, banks):
                P.op("scalar", lambda e, ps=ps, off=off, nsb=nsb: e.activation(
                    rs[:, off:off + nsb], ps[:, :nsb], AF.Sqrt, bias=eps_t[:, 0:1], scale=1.0 / D),
                    reads=[bps, Beps], writes=[Brs])
            P.op("vector", lambda e: e.reciprocal(rs[:, :n], rs[:, :n]), reads=[Brs], writes=[Brs])
            for kt in range(KT):
                P.op("vector", lambda e, kt=kt: e.scalar_tensor_tensor(
                    xn[:, kt, :n], ht[:, kt, :n], gn[:, sidx, kt:kt + 1], rs[:, :n], ALU.mult, ALU.mult),
                    reads=[C.Bht, Bgn, Brs], writes=[C.Bxn])

        def norm(ch, sidx):
            c0, n, lst = subs_of(ch)
            norm_impl(n, lst, sidx)

        def norm_n(n, sidx):
            norm_impl(n, [(0, n)], sidx)

        eps_t = sb("eps_t", [128, 1], F32)
        Beps = Buf("eps")
        P.op("vector", lambda e: e.memset(eps_t[:], EPS), writes=[Beps])

        if any(s[0] == "ffn" for s in stages):
            precast("wup", 2)
            precast("wdn", 2)
        Bwu = [Buf(f"wu{i}") for i in range(2)]
        Bwd = [Buf(f"wd{i}") for i in range(2)]
        Bact = Buf("act")
        BU = [Buf(f"U{i}") for i in range(2)]
        BY = [Buf(f"Y{i}") for i in range(2)]
        Bhalo = Buf("halo")
        Bfcw = Buf("fcw")

        def ffn_stage(layer, sidx):
            with stage_scope(NTM):
                F = NS()
                F.fcw = sb("fcw", [128, 4 * 44 * 4], F32)
                P.dma("sync", F.fcw[:], W["fcw"], writes=[Bfcw])
                F.wu = [sb(f"wu{i}", [128, KT, 512], BF16) for i in range(2)]
                F.wdt = [sb(f"wd{i}", [128, 22, 128], BF16) for i in range(2)]
                F.act = sb("act", [128, 22, NTM], BF16)
                F.U = [sb(f"U{i}", [128, NTM + 2], F32) for i in range(2)]
                F.Y = [sb(f"Y{i}", [128, NTM], F32) for i in range(4)]
                F.BY = [Buf(f"Yb{i}") for i in range(4)]
                F.halo = sb("halo", [128, 44, 2], F32)
                halo = F.halo
                P.op("vector", lambda e: e.memset(halo[:], 0.0), writes=[Bhalo])
                HT = [C.ht, sb("ht2", [128, KT, NTM], F32)]
                BHT = [Bht, Buf("ht2")]
                XN = [C.xn, sb("xn2", [128, KT, NTM], BF16)]
                BXN = [Bxn, Buf("xn2")]
                F.Bact = [Buf(f"act{m}") for m in range(22)]
                F.first = True
                chs = [[0, 1], [2, 3], [4, 5], [6, 7, 8]]
                load_h(chs[0])
                norm(chs[0], sidx)
                for i, ch in enumerate(chs):
                    C.ht, C.Bht, C.xn, C.Bxn = HT[i % 2], BHT[i % 2], XN[i % 2], BXN[i % 2]
                    j = (i + 1) % 2
                    nxt = (chs[i + 1], HT[j], BHT[j], XN[j], BXN[j]) if i + 1 < len(chs) else None
                    ffn_chunk(F, layer, sidx, ch, nxt)
                C.ht, C.Bht, C.xn, C.Bxn = HT[0], BHT[0], XN[0], BXN[0]

        def ffn_chunk(F, layer, sidx, ch, nxt):
            fcw, wu, wdt, act, U, Y, halo = F.fcw, F.wu, F.wdt, F.act, F.U, F.Y, F.halo
            ht, xn = C.ht, C.xn
            Bht, Bxn, Bact = C.Bht, C.Bxn, F.Bact
            if True:
                c0, n, lst = subs_of(ch)
                if nxt is not None:
                    cur = (C.ht, C.Bht)
                    C.ht, C.Bht = nxt[1], nxt[2]
                    load_h(nxt[0])
                    C.ht, C.Bht = cur
                if F.first:
                    P.dma("sync", wu[0][:], W16["wup"][layer, 0].rearrange("p (k c) -> p k c", k=KT),
                          reads=[BW16["wup"]], writes=[Bwu[0]])
                    F.first = False
                for g in range(11):
                    if g == 8:
                        P.dma("sync", wdt[0][:], W16["wdn"][layer, 0].rearrange("p (k c) -> p k c", k=22),
                              reads=[BW16["wdn"]], writes=[Bwd[0]])
                    if g + 1 < 11:
                        P.dma("sync", wu[(g + 1) % 2][:], W16["wup"][layer, g + 1].rearrange("p (k c) -> p k c", k=KT),
                              reads=[BW16["wup"]], writes=[Bwu[(g + 1) % 2]])
                    wt, bwt = wu[g % 2], Bwu[g % 2]
                    for half in range(2):
                        m = 2 * g + half
                        for ab in range(2):
                            tidx = m + 22 * ab
                            col = 256 * ab + 128 * half
                            u, bu, y, by = U[ab], BU[ab], Y[2 * (m % 2) + ab], F.BY[2 * (m % 2) + ab]
                            P.op("gpsimd", lambda e, u=u, tidx=tidx: e.tensor_copy(u[:, 0:2], halo[:, tidx, :]),
                                 reads=[Bhalo], writes=[bu])
                            for (off, nsb) in lst:
                                ps, bps = next_ps()
                                for kt in range(KT):
                                    P.op("tensor", lambda e, ps=ps, wt=wt, kt=kt, col=col, off=off, nsb=nsb: e.matmul(
                                        ps[:, :nsb], wt[:, kt, col:col + 128], xn[:, kt, off:off + nsb],
                                        start=(kt == 0), stop=(kt == KT - 1)),
                                        reads=[bwt, Bxn], writes=[bps], inc=(kt == KT - 1))
                                P.op("scalar", lambda e, ps=ps, u=u, off=off, nsb=nsb: e.copy(
                                    u[:, 2 + off:2 + off + nsb], ps[:, :nsb]), reads=[bps], writes=[bu])
                            P.op("gpsimd", lambda e, u=u, tidx=tidx: e.tensor_copy(halo[:, tidx, :], u[:, n:n + 2]),
                                 reads=[bu], writes=[Bhalo])
                            cb = (layer * 44 + tidx) * 4
                            P.op("scalar", lambda e, u=u, y=y, cb=cb: e.activation(
                                y[:, :n], u[:, 2:2 + n], AF.Identity, bias=fcw[:, cb + 3:cb + 4], scale=fcw[:, cb + 2:cb + 3]),
                                reads=[bu, Bfcw], writes=[by])
                            P.op("vector", lambda e, u=u, y=y, cb=cb: e.scalar_tensor_tensor(
                                y[:, :n], u[:, 1:1 + n], fcw[:, cb + 1:cb + 2], y[:, :n], ALU.mult, ALU.add),
                                reads=[bu, Bfcw, by], writes=[by])
                            P.op("vector", lambda e, u=u, y=y, cb=cb: e.scalar_tensor_tensor(
                                y[:, :n], u[:, 0:n], fcw[:, cb:cb + 1], y[:, :n], ALU.mult, ALU.add),
                                reads=[bu, Bfcw, by], writes=[by])
                        def gate(m):
                            ya, yb, bya, byb = Y[2 * (m % 2)], Y[2 * (m % 2) + 1], F.BY[2 * (m % 2)], F.BY[2 * (m % 2) + 1]
                            P.op("scalar", lambda e: e.activation(ya[:, :n], ya[:, :n], AF.Silu), reads=[bya], writes=[bya])
                            P.op("gpsimd", lambda e: e.tensor_tensor(act[:, m, :n], ya[:, :n], yb[:, :n], ALU.mult),
                                 reads=[bya, byb], writes=[Bact[m]])
                        if m > 0:
                            gate(m - 1)
                gate(21)
                if nxt is not None:
                    cur = (C.ht, C.Bht, C.xn, C.Bxn)
                    C.ht, C.Bht, C.xn, C.Bxn = nxt[1], nxt[2], nxt[3], nxt[4]
                    norm(nxt[0], sidx)
                    C.ht, C.Bht, C.xn, C.Bxn = cur
                if nxt is not None:
                    P.dma("sync", wu[0][:], W16["wup"][layer, 0].rearrange("p (k c) -> p k c", k=KT),
                          reads=[BW16["wup"]], writes=[Bwu[0]])
                for j in range(8):
                    if j + 1 < 8:
                        P.dma("sync", wdt[(j + 1) % 2][:], W16["wdn"][layer, j + 1].rearrange("p (k c) -> p k c", k=22),
                              reads=[BW16["wdn"]], writes=[Bwd[(j + 1) % 2]])
                    wt, bwt = wdt[j % 2], Bwd[j % 2]
                    o = j
                    for (off, nsb) in lst:
                        ps, bps = next_ps()
                        for kt in range(22):
                            P.op("tensor", lambda e, ps=ps, wt=wt, kt=kt, off=off, nsb=nsb: e.matmul(
                                ps[:, :nsb], wt[:, kt, :], act[:, kt, off:off + nsb],
                                start=(kt == 0), stop=(kt == 21)),
                                reads=([bwt] + Bact) if kt == 21 else [bwt, Bact[kt]], writes=[bps], inc=(kt == 21))
                        P.op("vector", lambda e, ps=ps, o=o, off=off, nsb=nsb: e.tensor_tensor(
                            ht[:, o, off:off + nsb], ht[:, o, off:off + nsb], ps[:, :nsb], ALU.add),
                            reads=[bps, Bht], writes=[Bht])
                store_h(ch)

        one_t = sb("one_t", [128, 1], F32)
        Bone = Buf("one")
        P.op("vector", lambda e: e.memset(one_t[:], 1.0), writes=[Bone])

        def rg_stage(sidx):
            with stage_scope(512):
                rg_stage_(sidx)

        def rg_stage_(sidx):
            RW = 84
            ht, xn = C.ht, C.xn
            wrg = sb("wrg", [128, KT, 2688], BF16); Bwrg = Buf("wrg")
            P.dma("gpsimd", wrg[:], W["wrgin"], writes=[Bwrg])
            wro = sb("wro", [RW, 16, D], BF16); Bwro = Buf("wro")
            P.dma("gpsimd", wro[:], W["wrgout"], writes=[Bwro])
            wa = sb("wa", [RW, 16, RW], BF16); Bwa = Buf("wa")
            P.dma("gpsimd", wa[:], W["wrga"], writes=[Bwa])
            wi = sb("wi", [RW, 16, RW], BF16); Bwi = Buf("wi")
            P.dma("gpsimd", wi[:], W["wrgi"], writes=[Bwi])
            rgp = sb("rgp", [RW, 16, 8], F32); Brgp = Buf("rgp")
            P.dma("sync", rgp[:], W["rgp"], writes=[Brgp])
            cneg = sb("cneg", [RW, 16], F32); c2 = sb("c2", [RW, 16], F32); Bc = Buf("cneg")
            P.op("scalar", lambda e: e.activation(cneg[:], rgp[:, :, 7], AF.Exp, scale=-1.0), reads=[Brgp], writes=[Bc])
            P.op("scalar", lambda e: e.activation(cneg[:], cneg[:], AF.Ln, bias=one_t[:RW, 0:1]), reads=[Bc, Bone], writes=[Bc])
            P.op("vector", lambda e: e.tensor_scalar(cneg[:], cneg[:], -8.0, None, ALU.mult), reads=[Bc], writes=[Bc])
            P.op("vector", lambda e: e.tensor_scalar(c2[:], cneg[:], 2.0, None, ALU.mult), reads=[Bc], writes=[Bc])
            carry = sb("carry", [RW, 16], F32); Bcar = Buf("carry")
            P.op("vector", lambda e: e.memset(carry[:], 0.0), writes=[Bcar])
            rhalo = sb("rhalo", [RW, 16, 3], F32); Brh = Buf("rhalo")
            P.op("vector", lambda e: e.memset(rhalo[:], 0.0), writes=[Brh])
            YM = sb("YM", [RW, 16, 512], BF16); BYM = Buf("YM")
            def mkset(i):
                d = {"i": i}
                for nm_, shp, dt_ in [("T1", 512, F32), ("XB", 515, F32), ("XC", 512, F32), ("XCb", 512, BF16), ("Rr", 512, F32),
                                      ("Ii", 512, F32), ("Aa", 512, F32), ("HS", 512, F32), ("GG", 512, F32)]:
                    d[nm_] = sb(f"{nm_}_{i}", [RW, shp], dt_)
                    d["B" + nm_] = Buf(f"rg{nm_}_{i}")
                d["Mm"], d["BMm"] = d["Rr"], d["BRr"]
                d["Bt"], d["BBt"] = d["Ii"], d["BIi"]
                return d

            NSET = 4
            tsets = [mkset(i) for i in range(NSET)]

            def rg_tile(t, n, ts):
                T1, XB, XC, XCb, Rr, Ii, Aa, Mm, Bt, HS, GGt = (ts[k] for k in ["T1", "XB", "XC", "XCb", "Rr", "Ii", "Aa", "Mm", "Bt", "HS", "GG"])
                BT1, BXB, BXC, BXCb, BRr, BIi, BAa, BMm, BBt, BHS, BGG = (ts["B" + k] for k in
                                                                          ["T1", "XB", "XC", "XCb", "Rr", "Ii", "Aa", "Mm", "Bt", "HS", "GG"])
                ps, bps = PS[2 * ts['i']], BPS[2 * ts['i']]
                for kt in range(KT):
                    P.op("tensor", lambda e, kt=kt: e.matmul(ps[:RW, :n], wrg[:, kt, t * RW:(t + 1) * RW], xn[:, kt, :n],
                                                             start=(kt == 0), stop=(kt == KT - 1)),
                         reads=[Bwrg, Bxn], writes=[bps], inc=(kt == KT - 1))
                P.op("scalar", lambda e: e.activation(T1[:, :n], ps[:RW, :n], AF.Square), reads=[bps], writes=[BT1])
                P.op("gpsimd", lambda e: e.tensor_scalar(T1[:, :n], T1[:, :n], 0.044715, 1.0, ALU.mult, ALU.add),
                     reads=[BT1], writes=[BT1])
                P.op("vector", lambda e: e.tensor_tensor(T1[:, :n], T1[:, :n], ps[:RW, :n], ALU.mult),
                     reads=[BT1, bps], writes=[BT1])
                P.op("scalar", lambda e: e.activation(T1[:, :n], T1[:, :n], AF.Sigmoid, scale=1.5957691216), reads=[BT1], writes=[BT1])
                P.op("vector", lambda e: e.tensor_tensor(GGt[:, :n], T1[:, :n], ps[:RW, :n], ALU.mult),
                     reads=[BT1, bps], writes=[BGG])
                ps2, bps2 = PS[2 * ts['i'] + 1], BPS[2 * ts['i'] + 1]
                c0 = 1344 + t * RW
                for kt in range(KT):
                    P.op("tensor", lambda e, kt=kt: e.matmul(ps2[:RW, :n], wrg[:, kt, c0:c0 + RW], xn[:, kt, :n],
                                                             start=(kt == 0), stop=(kt == KT - 1)),
                         reads=[Bwrg, Bxn], writes=[bps2], inc=(kt == KT - 1))
                P.op("gpsimd", lambda e: e.tensor_copy(XB[:, 0:3], rhalo[:, t, :]), reads=[Brh], writes=[BXB])
                P.op("scalar", lambda e: e.copy(XB[:, 3:3 + n], ps2[:RW, :n]), reads=[bps2], writes=[BXB])
                P.op("gpsimd", lambda e: e.tensor_copy(rhalo[:, t, :], XB[:, n:n + 3]), reads=[BXB], writes=[Brh])
                P.op("scalar", lambda e: e.activation(XC[:, :n], XB[:, 3:3 + n], AF.Identity, bias=rgp[:, t, 4:5], scale=rgp[:, t, 3:4]),
                     reads=[BXB, Brgp], writes=[BXC])
                for k in range(3):
                    P.op("vector", lambda e, k=k: e.scalar_tensor_tensor(XC[:, :n], XB[:, k:k + n], rgp[:, t, k:k + 1], XC[:, :n],
                                                                         ALU.mult, ALU.add),
                         reads=[BXB, Brgp, BXC], writes=[BXC])
                P.op("gpsimd", lambda e: e.tensor_copy(XCb[:, :n], XC[:, :n]), reads=[BXC], writes=[BXCb])
                ps3, bps3 = PS[2 * ts['i']], BPS[2 * ts['i']]
                P.op("tensor", lambda e: e.matmul(ps3[:RW, :n], wa[:, t, :], XCb[:, :n], start=True, stop=True),
                     reads=[Bwa, BXCb], writes=[bps3])
                P.op("scalar", lambda e: e.activation(Rr[:, :n], ps3[:RW, :n], AF.Sigmoid, bias=rgp[:, t, 5:6]),
                     reads=[bps3, Brgp], writes=[BRr])
                ps4, bps4 = PS[2 * ts['i'] + 1], BPS[2 * ts['i'] + 1]
                P.op("tensor", lambda e: e.matmul(ps4[:RW, :n], wi[:, t, :], XCb[:, :n], start=True, stop=True),
                     reads=[Bwi, BXCb], writes=[bps4])
                P.op("scalar", lambda e: e.activation(Ii[:, :n], ps4[:RW, :n], AF.Sigmoid, bias=rgp[:, t, 6:7]),
                     reads=[bps4, Brgp], writes=[BIi])
                P.op("scalar", lambda e: e.activation(Aa[:, :n], Rr[:, :n], AF.Exp, scale=cneg[:, t:t + 1]),
                     reads=[BRr, Bc], writes=[BAa])
                P.op("scalar", lambda e: e.activation(Mm[:, :n], Rr[:, :n], AF.Exp, scale=c2[:, t:t + 1]),
                     reads=[BRr, Bc], writes=[BMm])
                P.op("vector", lambda e: e.tensor_scalar(Mm[:, :n], Mm[:, :n], 0.99999994, None, ALU.min), reads=[BMm], writes=[BMm])
                P.op("scalar", lambda e: e.activation(Mm[:, :n], Mm[:, :n], AF.Sqrt, scale=-1.0, bias=one_t[:RW, 0:1]),
                     reads=[BMm, Bone], writes=[BMm])
                P.op("gpsimd", lambda e: e.tensor_tensor(Bt[:, :n], Ii[:, :n], XC[:, :n], ALU.mult), reads=[BIi, BXC], writes=[BBt])
                P.op("gpsimd", lambda e: e.tensor_tensor(Bt[:, :n], Bt[:, :n], Mm[:, :n], ALU.mult), reads=[BBt, BMm], writes=[BBt])
                P.op("vector", lambda e: e.tensor_tensor_scan(HS[:, :n], Aa[:, :n], Bt[:, :n], carry[:, t:t + 1], ALU.mult, ALU.add),
                     reads=[BAa, BBt, Bcar], writes=[BHS])
                P.op("vector", lambda e: e.tensor_copy(carry[:, t:t + 1], HS[:, n - 1:n]), reads=[BHS], writes=[Bcar])
                P.op("gpsimd", lambda e: e.tensor_tensor(YM[:, t, :n], HS[:, :n], GGt[:, :n], ALU.mult),
                     reads=[BHS, BGG], writes=[BYM])

            def rg_chunk(si):
                c0, n, lst = subs_of([si])
                load_h([si])
                norm([si], sidx)
                for t0 in range(0, 16, NSET):
                    P.emit_interleaved([P.capture(lambda i=i: rg_tile(t0 + i, n, tsets[i])) for i in range(NSET)])
                for o in range(KT):
                    ps, bps = next_ps()
                    for t in range(16):
                        P.op("tensor", lambda e, t=t, o=o, ps=ps: e.matmul(ps[:, :n], wro[:, t, o * 128:(o + 1) * 128], YM[:, t, :n],
                                                                          start=(t == 0), stop=(t == 15)),
                             reads=[Bwro, BYM], writes=[bps], inc=(t == 15))
                    P.op("vector", lambda e, o=o, ps=ps: e.tensor_tensor(ht[:, o, :n], ht[:, o, :n], ps[:, :n], ALU.add),
                         reads=[bps, Bht], writes=[Bht])
                store_h([si])

            for si in range(len(SUBS)):
                rg_chunk(si)

        def hg_stage(sidx):
            with stage_scope(512):
                hg_stage_(sidx)

        def hg_stage_(sidx):
            ht, xn = C.ht, C.xn
            whg = sb("whg", [128, KT, 4096], BF16); Bwhg = Buf("whg")
            for kt in range(KT):
                P.dma("gpsimd", whg[:, kt, :], W["whgin"][:, kt, :], writes=[Bwhg])
            who = sb("who", [128, KT, D], BF16); Bwho = Buf("who")
            P.dma("gpsimd", who[:], W["whgout"], writes=[Bwho])
            ident = sb("ident", [128, 128], F32); rmask = sb("rmask", [128, 512], F32); triu = sb("triu", [128, 128], F32)
            Bcst = Buf("hgconst")
            P.dma("sync", ident[:], W["ident"], writes=[Bcst])
            P.dma("sync", rmask[:], W["rmask"], writes=[Bcst])
            P.dma("sync", triu[:], W["triu"], writes=[Bcst])
            gam = sb("gam", [128, 4, KT], F32); on = sb("on", [128, KT], F32); Bgam = Buf("gam")
            P.dma("sync", gam[:], W["hgg"], writes=[Bgam])
            P.dma("sync", on[:], W["hgon"], writes=[Bgam])
            lb = sb("lb", [128, KT], F32); oml = sb("oml", [128, KT], F32); s4 = sb("s4", [128, KT], F32); Blb = Buf("lb")
            P.op("scalar", lambda e: e.activation(gam[:], gam[:], AF.Exp), reads=[Bgam], writes=[Bgam])
            P.op("vector", lambda e: e.tensor_tensor(lb[:], gam[:, 0, :], gam[:, 1, :], ALU.add), reads=[Bgam], writes=[Blb])
            P.op("vector", lambda e: e.tensor_tensor(lb[:], lb[:], gam[:, 2, :], ALU.add), reads=[Bgam, Blb], writes=[Blb])
            P.op("vector", lambda e: e.tensor_tensor(s4[:], lb[:], gam[:, 3, :], ALU.add), reads=[Bgam, Blb], writes=[Blb])
            P.op("vector", lambda e: e.reciprocal(s4[:], s4[:]), reads=[Blb], writes=[Blb])
            P.op("vector", lambda e: e.tensor_tensor(lb[:], lb[:], s4[:], ALU.mult), reads=[Blb], writes=[Blb])
            P.op("vector", lambda e: e.tensor_tensor(oml[:], gam[:, 3, :], s4[:], ALU.mult), reads=[Bgam, Blb], writes=[Blb])
            S = sb("S", [128, KT, 128], F32)
            VT = sb("VT", [128, 4, D], BF16); BVT = Buf("VT")
            OG = sb("OG", [128, KT, 512], BF16); BOG = Buf("OG")
            BSl = [Buf(f"S{i}") for i in range(KT)]
            names = ["FH", "LF", "K1", "CUM", "E1", "E2", "E3", "KE", "SG", "O", "RS2", "ON"]

            def mkhs(i):
                h = NS()
                h.i = i
                h.rr = [0]
                h.Sb = sb(f"Sb{i}", [128, 128], BF16); h.BSb = Buf(f"Sb{i}")
                h.QD = sb(f"QD{i}", [128, 512], BF16); h.BQD = Buf(f"QD{i}")
                h.KI = sb(f"KI{i}", [128, 512], BF16); h.BKI = Buf(f"KI{i}")
                h.KET = sb(f"KET{i}", [128, 4, 128], BF16); h.BKET = Buf(f"KET{i}")
                h.SM = sb(f"SM{i}", [128, 128], BF16); h.BSM = Buf(f"SM{i}")
                h.T = {k: sb(f"hg{k}{i}", [128, 512], F32) for k in names}
                h.BT = {k: Buf(f"hg{k}{i}") for k in names}
                h.OSQ = sb(f"OSQ{i}", [128, 512], BF16); h.BOSQ = Buf(f"OSQ{i}")
                h.mids = sb(f"mids{i}", [128, 8], F32); h.Bmid = Buf(f"mids{i}")
                return h

            HS2 = [mkhs(0), mkhs(1)]
            P.op("vector", lambda e: e.memset(S[:], 0.0), writes=BSl)

            def hg_head(hd, n, subs, h):
                Sb, BSb, QD, BQD, KI, BKI, KET, BKET, SM, BSM = h.Sb, h.BSb, h.QD, h.BQD, h.KI, h.BKI, h.KET, h.BKET, h.SM, h.BSM
                T, BT, OSQ, BOSQ, mids, Bmid = h.T, h.BT, h.OSQ, h.BOSQ, h.mids, h.Bmid
                BS = BSl[hd]

                def next_ps():
                    i = 4 * h.i + h.rr[0]
                    h.rr[0] = (h.rr[0] + 1) % 4
                    return PS[i], BPS[i]

                FH, LF, K1, CUM, E1, E2, E3, KE, SG, O, RS2, ON = (T[k] for k in
                    ["FH", "LF", "K1", "CUM", "E1", "E2", "E3", "KE", "SG", "O", "RS2", "ON"])

                def proj(col0):
                    ps, bps = next_ps()
                    for kt in range(KT):
                        P.op("tensor", lambda e, kt=kt: e.matmul(ps[:, :n], whg[:, kt, col0:col0 + 128], xn[:, kt, :n],
                                                                 start=(kt == 0), stop=(kt == KT - 1)),
                             reads=[Bwhg, Bxn], writes=[bps], inc=(kt == KT - 1))
                    return ps, bps

                psf, bpsf = proj(1024 + hd * 128)
                P.op("scalar", lambda e: e.activation(FH[:, :n], psf[:, :n], AF.Sigmoid), reads=[bpsf], writes=[BT["FH"]])
                P.op("gpsimd", lambda e: e.tensor_scalar(FH[:, :n], FH[:, :n], oml[:, hd:hd + 1], lb[:, hd:hd + 1], ALU.mult, ALU.add),
                     reads=[BT["FH"], Blb], writes=[BT["FH"]])
                P.op("scalar", lambda e: e.activation(LF[:, :n], FH[:, :n], AF.Ln), reads=[BT["FH"]], writes=[BT["LF"]])
                P.op("gpsimd", lambda e: e.tensor_scalar(K1[:, :n], FH[:, :n], -1.0, 1.0, ALU.mult, ALU.add),
                     reads=[BT["FH"]], writes=[BT["K1"]])
                P.op("vector", lambda e: e.tensor_tensor_scan(CUM[:, :n], rmask[:, :n], LF[:, :n], 0.0, ALU.mult, ALU.add),
                     reads=[BT["LF"], Bcst], writes=[BT["CUM"]])
                for j, (o0, m) in enumerate(subs):
                    im, il = o0 + max(m // 2 - 1, 0), o0 + m - 1
                    P.op("scalar", lambda e, j=j, im=im: e.activation(mids[:, j:j + 1], CUM[:, im:im + 1], AF.Exp),
                         reads=[BT["CUM"]], writes=[Bmid])
                    P.op("scalar", lambda e, j=j, il=il: e.activation(mids[:, 4 + j:5 + j], CUM[:, il:il + 1], AF.Exp),
                         reads=[BT["CUM"]], writes=[Bmid])
                    P.op("scalar", lambda e, o0=o0, m=m, im=im: e.activation(E2[:, o0:o0 + m], CUM[:, o0:o0 + m], AF.Exp,
                                                                             scale=-1.0, bias=CUM[:, im:im + 1]),
                         reads=[BT["CUM"]], writes=[BT["E2"]])
                    P.op("scalar", lambda e, o0=o0, m=m, il=il: e.activation(E3[:, o0:o0 + m], CUM[:, o0:o0 + m], AF.Exp,
                                                                             scale=-1.0, bias=CUM[:, il:il + 1]),
                         reads=[BT["CUM"]], writes=[BT["E3"]])
                P.op("vector", lambda e: e.reciprocal(E1[:, :n], E2[:, :n]), reads=[BT["E2"]], writes=[BT["E1"]])
                psq, bpsq = proj(hd * 128)
                P.op("vector", lambda e: e.tensor_tensor(QD[:, :n], E1[:, :n], psq[:, :n], ALU.mult),
                     reads=[BT["E1"], bpsq], writes=[BQD])
                P.op("gpsimd", lambda e: e.tensor_tensor(KI[:, :n], K1[:, :n], E2[:, :n], ALU.mult),
                     reads=[BT["K1"], BT["E2"]], writes=[BKI])
                P.op("gpsimd", lambda e: e.tensor_tensor(KE[:, :n], K1[:, :n], E3[:, :n], ALU.mult),
                     reads=[BT["K1"], BT["E3"]], writes=[BT["KE"]])
                for j, (o0, m) in enumerate(subs):
                    pst, bpst = next_ps()
                    P.op("tensor", lambda e, o0=o0, m=m, pst=pst: e.transpose(pst[:m, :128], KE[:, o0:o0 + m], ident[:, :]),
                         reads=[BT["KE"], Bcst], writes=[bpst])
                    P.op("scalar", lambda e, j=j, m=m, pst=pst: e.copy(KET[:m, j, :], pst[:m, :128]), reads=[bpst], writes=[BKET])
                psg, bpsg = proj(3072 + hd * 128)
                P.op("scalar", lambda e: e.activation(SG[:, :n], psg[:, :n], AF.Silu), reads=[bpsg], writes=[BT["SG"]])
                for j, (o0, m) in enumerate(subs):
                    pss, bpss = next_ps()
                    P.op("tensor", lambda e, o0=o0, m=m, pss=pss: e.matmul(pss[:m, :m], KI[:, o0:o0 + m], QD[:, o0:o0 + m],
                                                                           start=True, stop=True),
                         reads=[BKI, BQD], writes=[bpss])
                    P.op("vector", lambda e, m=m, pss=pss: e.tensor_tensor(SM[:m, :m], pss[:m, :m], triu[:m, :m], ALU.mult),
                         reads=[bpss, Bcst], writes=[BSM])
                    P.op("gpsimd", lambda e, j=j: e.tensor_scalar(Sb[:], S[:, hd, :], mids[:, j:j + 1], None, ALU.mult),
                         reads=[BS, Bmid], writes=[BSb])
                    pso, bpso = next_ps()
                    P.op("tensor", lambda e, j=j, m=m, pso=pso: e.matmul(pso[:, :m], VT[:m, j, hd * 128:(hd + 1) * 128], SM[:m, :m],
                                                                         start=True, stop=False),
                         reads=[BVT, BSM], writes=[bpso], inc=False)
                    P.op("tensor", lambda e, o0=o0, m=m, pso=pso: e.matmul(pso[:, :m], Sb[:], QD[:, o0:o0 + m], start=False, stop=True),
                         reads=[BSb, BQD, BVT, BSM], writes=[bpso])
                    P.op("scalar", lambda e, o0=o0, m=m, pso=pso: e.copy(O[:, o0:o0 + m], pso[:, :m]), reads=[bpso], writes=[BT["O"]])
                    psu, bpsu = next_ps()
                    P.op("tensor", lambda e, j=j, m=m, psu=psu: e.matmul(psu[:, :128], KET[:m, j, :], VT[:m, j, hd * 128:(hd + 1) * 128],
                                                                         start=True, stop=True),
                         reads=[BKET, BVT], writes=[bpsu])
                    P.op("vector", lambda e, j=j, psu=psu: e.scalar_tensor_tensor(S[:, hd, :], S[:, hd, :], mids[:, 4 + j:5 + j], psu[:, :128],
                                                                                 ALU.mult, ALU.add),
                         reads=[BS, Bmid, bpsu], writes=[BS])
                P.op("gpsimd", lambda e: e.tensor_tensor(OSQ[:, :n], O[:, :n], O[:, :n], ALU.mult), reads=[BT["O"]], writes=[BOSQ])
                psn, bpsn = next_ps()
                P.op("tensor", lambda e: e.matmul(psn[:, :n], ones_bf[:], OSQ[:, :n], start=True, stop=True),
                     reads=[Bones, BOSQ], writes=[bpsn])
                P.op("scalar", lambda e: e.activation(RS2[:, :n], psn[:, :n], AF.Sqrt, bias=eps_t[:, 0:1], scale=1.0 / 128),
                     reads=[bpsn, Beps], writes=[BT["RS2"]])
                P.op("vector", lambda e: e.reciprocal(RS2[:, :n], RS2[:, :n]), reads=[BT["RS2"]], writes=[BT["RS2"]])
                P.op("vector", lambda e: e.scalar_tensor_tensor(ON[:, :n], O[:, :n], on[:, hd:hd + 1], RS2[:, :n], ALU.mult, ALU.mult),
                     reads=[BT["O"], Bgam, BT["RS2"]], writes=[BT["ON"]])
                P.op("gpsimd", lambda e: e.tensor_tensor(OG[:, hd, :n], ON[:, :n], SG[:, :n], ALU.mult),
                     reads=[BT["ON"], BT["SG"]], writes=[BOG])

            def hg_chunk(si):
                c0, n, lst = subs_of([si])
                subs = [(o0, min(128, n - o0)) for o0 in range(0, n, 128)]
                load_h([si])
                norm([si], sidx)
                for j, (o0, m) in enumerate(subs):
                    for half in range(2):
                        ps, bps = next_ps()
                        for kt in range(KT):
                            P.op("tensor", lambda e, kt=kt, ps=ps, o0=o0, m=m, half=half: e.matmul(
                                ps[:m, :512], xn[:, kt, o0:o0 + m], whg[:, kt, 2048 + 512 * half:2048 + 512 * (half + 1)],
                                start=(kt == 0), stop=(kt == KT - 1)),
                                reads=[Bwhg, Bxn], writes=[bps], inc=(kt == KT - 1))
                        P.op("scalar", lambda e, ps=ps, j=j, m=m, half=half: e.copy(VT[:m, j, 512 * half:512 * (half + 1)], ps[:m, :512]),
                             reads=[bps], writes=[BVT])
                for hd in range(0, KT, 2):
                    P.emit_interleaved([P.capture(lambda: hg_head(hd, n, subs, HS2[0])),
                                        P.capture(lambda: hg_head(hd + 1, n, subs, HS2[1]))])
                for o in range(KT):
                    ps, bps = next_ps()
                    for kt in range(KT):
                        P.op("tensor", lambda e, kt=kt, o=o, ps=ps: e.matmul(ps[:, :n], who[:, kt, o * 128:(o + 1) * 128], OG[:, kt, :n],
                                                                            start=(kt == 0), stop=(kt == KT - 1)),
                             reads=[Bwho, BOG], writes=[bps], inc=(kt == KT - 1))
                    P.op("vector", lambda e, o=o, ps=ps: e.tensor_tensor(ht[:, o, :n], ht[:, o, :n], ps[:, :n], ALU.add),
                         reads=[bps, Bht], writes=[Bht])
                store_h([si])

            for si in range(len(SUBS)):
                hg_chunk(si)

        hd2 = dscr("hd2", [D, L], F32)
        Bh2 = [Buf(f"hd2_{i}") for i in range(len(SUBS))]
        hd2_v = hd2.rearrange("(kt p) t -> p kt t", p=128)

        def sb_stage(sidx):
            for pas in range(2):
                with stage_scope(512):
                    sb_pass(sidx, pas)

        def sb_pass(sidx, pas):
            ht, xn = C.ht, C.xn
            wqkv = sb("wqkv", [128, KT, 1536], BF16); Bwqkv = Buf("wqkv")
            P.dma("gpsimd", wqkv[:], W["wqkv"][pas], writes=[Bwqkv])
            wo = sb("wo", [64, 8, D], BF16); Bwo = Buf("wo")
            P.dma("gpsimd", wo[:], W["wsbo"][pas], writes=[Bwo])
            Bcst = Buf("sbconst")
            qkn = sb("qkn", [128, 2], F32)
            P.dma("sync", qkn[:], W["qkn"], writes=[Bcst])
            P.op("vector", lambda e: e.tensor_scalar(qkn[:, 0:1], qkn[:, 0:1], 0.125, None, ALU.mult), reads=[Bcst], writes=[Bcst])
            bones = sb("bones", [128, 128], BF16)
            P.dma("gpsimd", bones[:], W["blockones"], writes=[Bcst])
            ones32 = sb("ones32", [128, 128], F32)
            P.dma("sync", ones32[:], W["ones32"], writes=[Bcst])
            lstr = sb("lstr", [128, 128], F32)
            P.dma("sync", lstr[:], W["lstrict"], writes=[Bcst])
            am = sb("am", [128, 2, 4, 512], F32)
            for a in range(2):
                for r in range(4):
                    P.dma("sync", am[:, a, r, :], W["amask"][a, r], writes=[Bcst])
            KS = sb("KS", [128, 4, L], BF16); BKS = Buf("KS")
            VS = sb("VS", [128, 33, 512], BF16); BVS = Buf("VS")
            Q = sb("Q", [128, 4, 512], BF16); BQ = Buf("Q")
            OA = sb("OA", [64, 8, 512], BF16); BOA = Buf("OA")
            SQ = sb("SQ", [128, 512], BF16); BSQ = Buf("SQ")
            RSq = sb("RSq", [128, 512], F32); BRSq = Buf("RSq")
            def ring(name, depth, dt=F32):
                return [sb(f"{name}{i}", [128, 512], dt) for i in range(depth)], [Buf(f"{name}{i}") for i in range(depth)]

            LD, BLD = ring("LD", 5)
            LK, BLK = ring("LK", 3)
            LSX, BLSX = ring("LSX", 2)
            NL, BNL = ring("NL", 3)
            ARG, BARG = ring("ARG", 2)
            WT, BWT = ring("WT", 2, BF16)
            zero_t = sb("zero_t", [128, 512], F32); Bzero = Buf("zero_t")
            P.op("vector", lambda e: e.memset(zero_t[:], 0.0), writes=[Bzero])
            src_v, src_B = (hd_v, Bh)
            dst_v, dst_B = (hd2_v, Bh2) if pas == 0 else (hd_v, Bh)

            def qk_norm(ps, bps, n, col, dst_fn, dB):
                P.op("scalar", lambda e: e.activation(SQ[:, :n], ps[:, :n], AF.Square), reads=[bps], writes=[BSQ])
                psn, bpsn = next_ps()
                P.op("tensor", lambda e: e.matmul(psn[:, :n], bones[:], SQ[:, :n], start=True, stop=True),
                     reads=[Bcst, BSQ], writes=[bpsn])
                P.op("scalar", lambda e: e.activation(RSq[:, :n], psn[:, :n], AF.Sqrt, bias=eps_t[:, 0:1], scale=1.0 / 64),
                     reads=[bpsn, Beps], writes=[BRSq])
                P.op("vector", lambda e: e.reciprocal(RSq[:, :n], RSq[:, :n]), reads=[BRSq], writes=[BRSq])
                P.op("vector", lambda e: e.scalar_tensor_tensor(dst_fn(), ps[:, :n], qkn[:, col:col + 1], RSq[:, :n], ALU.mult, ALU.mult),
                     reads=[bps, Bcst, BRSq], writes=[dB])

            def attention(n, c0):
                nkb = (c0 + n + 127) // 128
                tiles = [(hl, kb) for hl in range(8) for kb in range(nkb - 1, -1, -1)]

                def info(t):
                    hl, kb = tiles[t]
                    k0 = kb * 128
                    m = min(128, L - k0)
                    r = (k0 - c0) // 128 if k0 >= c0 else -1
                    return hl, kb, k0, m, r, (kb == nkb - 1), (kb == 0)

                PZ = [(PS[i], BPS[i]) for i in (0, 1, 2)]
                PB = [(PS[i], BPS[i]) for i in (3, 4)]
                PA = [(PS[i], BPS[i]) for i in (5, 6)]

                def ph0(t):
                    hl, kb, k0, m, r, first, last = info(t)
                    ft, half = hl // 2, hl % 2
                    psz, bpsz = PZ[t % 3]
                    P.op("tensor", lambda e: e.matmul(psz[:m, :n], KS[half * 64:(half + 1) * 64, ft, k0:k0 + m],
                                                      Q[half * 64:(half + 1) * 64, ft, :n], start=True, stop=True),
                         reads=[BKS, BQ], writes=[bpsz])

                def ph1(t):
                    hl, kb, k0, m, r, first, last = info(t)
                    psz, bpsz = PZ[t % 3]
                    ld, bld = LD[t % 5], BLD[t % 5]
                    P.op("scalar", lambda e: e.activation(ld[:m, :n], psz[:m, :n], AF.Exp, scale=-1.0), reads=[bpsz], writes=[bld])
                    P.op("scalar", lambda e: e.activation(ld[:m, :n], ld[:m, :n], AF.Ln, bias=one_t[:m, 0:1]), reads=[bld, Bone], writes=[bld])

                def ph2(t):
                    hl, kb, k0, m, r, first, last = info(t)
                    psz, bpsz = PZ[t % 3]
                    ld, bld = LD[t % 5], BLD[t % 5]
                    lk, blk = LK[t % 3], BLK[t % 3]
                    P.op("vector", lambda e: e.scalar_tensor_tensor(lk[:m, :n], psz[:m, :n], -1.0, ld[:m, :n], ALU.mult, ALU.subtract),
                         reads=[bpsz, bld], writes=[blk])
                    if r >= 0:
                        nl, bnl = NL[t % 3], BNL[t % 3]
                        P.op("gpsimd", lambda e: e.tensor_tensor(lk[:m, :n], lk[:m, :n], am[:m, 0, r, :n], ALU.mult),
                             reads=[blk, Bcst], writes=[blk])
                        P.op("gpsimd", lambda e: e.tensor_tensor(nl[:m, :n], am[:m, 1, r, :n], ld[:m, :n], ALU.subtract),
                             reads=[bld, Bcst], writes=[bnl])
                    if not first:
                        lsx, blsx = LSX[t % 2], BLSX[t % 2]
                        pl, bpl = LK[(t - 1) % 3], BLK[(t - 1) % 3]
                        mp = info(t - 1)[3]
                        if info(t - 1)[5]:
                            P.op("gpsimd", lambda e: e.tensor_copy(lsx[:, :n], zero_t[:, :n]), reads=[Bzero], writes=[blsx])
                            P.op("gpsimd", lambda e: e.tensor_copy(lsx[:mp, :n], pl[:mp, :n]), reads=[bpl], writes=[blsx])
                        else:
                            px, bpx = LSX[(t - 1) % 2], BLSX[(t - 1) % 2]
                            P.op("gpsimd", lambda e: e.tensor_tensor(lsx[:, :n], px[:, :n], pl[:, :n], ALU.add),
                                 reads=[bpx, bpl], writes=[blsx])

                def ph3(t):
                    hl, kb, k0, m, r, first, last = info(t)
                    lk, blk = LK[t % 3], BLK[t % 3]
                    psb, bpsb = PB[t % 2]
                    P.op("tensor", lambda e: e.matmul(psb[:m, :n], lstr[:m, :m], lk[:m, :n], start=True, stop=first),
                         reads=[Bcst, blk], writes=[bpsb], inc=first)
                    if not first:
                        lsx, blsx = LSX[t % 2], BLSX[t % 2]
                        P.op("tensor", lambda e: e.matmul(psb[:m, :n], ones32[:, :m], lsx[:, :n], start=False, stop=True),
                             reads=[Bcst, blk, blsx], writes=[bpsb])

                def ph4(t):
                    hl, kb, k0, m, r, first, last = info(t)
                    psb, bpsb = PB[t % 2]
                    arg, barg = ARG[t % 2], BARG[t % 2]
                    if r >= 0:
                        nl, bnl = NL[t % 3], BNL[t % 3]
                        P.op("vector", lambda e: e.tensor_tensor(arg[:m, :n], psb[:m, :n], nl[:m, :n], ALU.add),
                             reads=[bpsb, bnl], writes=[barg])
                    else:
                        ld, bld = LD[t % 5], BLD[t % 5]
                        P.op("vector", lambda e: e.tensor_tensor(arg[:m, :n], psb[:m, :n], ld[:m, :n], ALU.subtract),
                             reads=[bpsb, bld], writes=[barg])

                def ph5(t):
                    hl, kb, k0, m, r, first, last = info(t)
                    arg, barg = ARG[t % 2], BARG[t % 2]
                    wt, bwt = WT[t % 2], BWT[t % 2]
                    P.op("scalar", lambda e: e.activation(wt[:m, :n], arg[:m, :n], AF.Exp), reads=[barg], writes=[bwt])

                def ph6(t):
                    hl, kb, k0, m, r, first, last = info(t)
                    wt, bwt = WT[t % 2], BWT[t % 2]
                    acc, bacc = PA[hl % 2]
                    P.op("tensor", lambda e: e.matmul(acc[:64, :n], VS[:m, kb, hl * 64:(hl + 1) * 64], wt[:m, :n], start=first, stop=last),
                         reads=[BVS, bwt], writes=[bacc])
                    if last:
                        P.op("scalar", lambda e: e.copy(OA[:, hl, :n], acc[:64, :n]), reads=[bacc], writes=[BOA])

                pipeline(len(tiles), [ph0, ph1, ph2, ph3, ph4, ph5, ph6])

            def sb_chunk(si):
                c0, n, lst = subs_of([si])
                subs = [(o0, min(128, n - o0)) for o0 in range(0, n, 128)]
                P.dma("sync", ht[:, :, :n], src_v[:, :, c0:c0 + n], reads=[src_B[si]], writes=[Bht])
                norm([si], sidx)
                for ft in range(4):
                    for which in range(2):
                        ps, bps = next_ps()
                        cb = 512 * which + ft * 128
                        for kt in range(KT):
                            P.op("tensor", lambda e, kt=kt, ps=ps, cb=cb: e.matmul(ps[:, :n], wqkv[:, kt, cb:cb + 128], xn[:, kt, :n],
                                                                               start=(kt == 0), stop=(kt == KT - 1)),
                                 reads=[Bwqkv, Bxn], writes=[bps], inc=(kt == KT - 1))
                        if which == 0:
                            qk_norm(ps, bps, n, 0, lambda ft=ft: Q[:, ft, :n], BQ)
                        else:
                            qk_norm(ps, bps, n, 1, lambda ft=ft: KS[:, ft, c0:c0 + n], BKS)
                for (o0, m) in subs:
                    ps, bps = next_ps()
                    blk = (c0 + o0) // 128
                    for kt in range(KT):
                        P.op("tensor", lambda e, kt=kt, ps=ps, o0=o0, m=m: e.matmul(ps[:m, :512], xn[:, kt, o0:o0 + m], wqkv[:, kt, 1024:1536],
                                                                                 start=(kt == 0), stop=(kt == KT - 1)),
                             reads=[Bwqkv, Bxn], writes=[bps], inc=(kt == KT - 1))
                    P.op("scalar", lambda e, ps=ps, m=m, blk=blk: e.copy(VS[:m, blk, :], ps[:m, :512]), reads=[bps], writes=[BVS])
                if pas == 1:
                    P.dma("sync", ht[:, :, :n], hd2_v[:, :, c0:c0 + n], reads=[Bh2[si], Bxn], writes=[Bht])
                attention(n, c0)
                for o in range(KT):
                    ps, bps = next_ps()
                    for hl in range(8):
                        P.op("tensor", lambda e, hl=hl, o=o, ps=ps: e.matmul(ps[:, :n], wo[:, hl, o * 128:(o + 1) * 128], OA[:, hl, :n],
                                                                            start=(hl == 0), stop=(hl == 7)),
                             reads=[Bwo, BOA], writes=[bps], inc=(hl == 7))
                    P.op("vector", lambda e, o=o, ps=ps: e.tensor_tensor(ht[:, o, :n], ht[:, o, :n], ps[:, :n], ALU.add),
                         reads=[bps, Bht], writes=[Bht])
                P.dma("sync", dst_v[:, :, c0:c0 + n], ht[:, :, :n], reads=[Bht], writes=[dst_B[si]])

            for si in range(len(SUBS)):
                sb_chunk(si)

        @contextlib.contextmanager
        def sub_scope():
            prev = cur[0]
            with contextlib.ExitStack() as sst:
                cur[0] = sst
                yield
                P.barrier()
                P.replay()
                cur[0] = prev

        def bufs_of(c0, n):
            return [Bh[i] for i, (a, m) in enumerate(SUBS) if a < c0 + n and c0 < a + m]

        def sincos(sin_o, cos_o, ut, Fw, tmp, Btmp, rd, wr):
            K, V, S1, S2 = tmp
            P.op("vector", lambda e: e.tensor_copy(K[:, :Fw], ut), reads=rd, writes=[Btmp])
            P.op("vector", lambda e: e.tensor_tensor(V[:, :Fw], ut, K[:, :Fw], ALU.subtract), reads=rd + [Btmp], writes=[Btmp])
            P.op("scalar", lambda e: e.activation(S1[:, :Fw], V[:, :Fw], AF.Sin, scale=float(np.pi)), reads=[Btmp], writes=[Btmp])
            P.op("scalar", lambda e: e.activation(S2[:, :Fw], V[:, :Fw], AF.Sin, scale=float(np.pi / 2)), reads=[Btmp], writes=[Btmp])
            P.op("vector", lambda e: e.tensor_tensor(S2[:, :Fw], S2[:, :Fw], S2[:, :Fw], ALU.mult), reads=[Btmp], writes=[Btmp])
            P.op("vector", lambda e: e.tensor_scalar(S2[:, :Fw], S2[:, :Fw], -2.0, 1.0, ALU.mult, ALU.add), reads=[Btmp], writes=[Btmp])
            P.op("vector", lambda e: e.scalar_tensor_tensor(sin_o, S1[:, :Fw], 2.0, S2[:, :Fw], ALU.mult, ALU.mult),
                 reads=[Btmp], writes=wr)
            P.op("vector", lambda e: e.tensor_tensor(S1[:, :Fw], S1[:, :Fw], S1[:, :Fw], ALU.mult), reads=[Btmp], writes=[Btmp])
            P.op("vector", lambda e: e.tensor_scalar(cos_o, S1[:, :Fw], -2.0, 1.0, ALU.mult, ALU.add), reads=[Btmp], writes=wr)

        def s5_stage(sidx):
            with stage_scope(256):
                s5_stage_(sidx)

        def s5_stage_(sidx):
            NT = 256
            TWO_PI_INV = float(1.0 / (2.0 * np.pi))
            ht, xn = C.ht, C.xn
            win = sb("win", [128, KT, D], BF16); Bwin = Buf("win")
            P.dma("gpsimd", win[:], W["ws5in"], writes=[Bwin])
            wgl = sb("wgl", [128, KT, 2 * D], BF16); Bwgl = Buf("wgl")
            P.dma("gpsimd", wgl[:], W["ws5glu"], writes=[Bwgl])
            COS = sb("COS", [128, 32, NT], F32); SIN = sb("SIN", [128, 32, NT], F32); Btab = Buf("s5tab")
            VRE = sb("VRE", [128, 32, 128], BF16); VIM = sb("VIM", [128, 32, 128], BF16); BV = Buf("s5V")
            CRE = sb("CRE", [128, 32, 128], BF16); CIN = sb("CIN", [128, 32, 128], BF16); BCm = Buf("s5C")
            P.dma("gpsimd", CRE[:], W["s5c"][0], writes=[BCm])
            P.dma("gpsimd", CIN[:], W["s5c"][1], writes=[BCm])
            P.op("vector", lambda e: e.tensor_scalar(CIN[:], CIN[:], -1.0, None, ALU.mult), reads=[BCm], writes=[BCm])
            rdec = sb("rdec", [128, 32], F32); thn = sb("thn", [128, 32], F32); Blane = Buf("s5lane")
            dsk = sb("dsk", [128, KT], F32)
            P.dma("sync", dsk[:], W["s5d"], writes=[Blane])
            CR = sb("CR", [128, 32], F32); CI = sb("CI", [128, 32], F32); BCR = Buf("CR"); BCI = Buf("CI")
            P.op("vector", lambda e: e.memset(CR[:], 0.0), writes=[BCR])
            P.op("vector", lambda e: e.memset(CI[:], 0.0), writes=[BCI])
            with sub_scope():
                LP = sb("LP", [128, 3, 32], F32)
                P.dma("sync", LP[:], W["s5lane"], writes=[Blane])
                P.op("scalar", lambda e: e.activation(LP[:, 2, :], LP[:, 2, :], AF.Exp), reads=[Blane], writes=[Blane])
                P.op("vector", lambda e: e.tensor_tensor(rdec[:], LP[:, 0, :], LP[:, 2, :], ALU.mult), reads=[Blane], writes=[Blane])
                P.op("scalar", lambda e: e.activation(rdec[:], rdec[:], AF.Exp), reads=[Blane], writes=[Blane])
                P.op("vector", lambda e: e.tensor_tensor(thn[:], LP[:, 1, :], LP[:, 2, :], ALU.mult), reads=[Blane], writes=[Blane])
                P.op("vector", lambda e: e.tensor_scalar(thn[:], thn[:], TWO_PI_INV, None, ALU.mult), reads=[Blane], writes=[Blane])
                tau1 = sb("tau1", [128, NT], F32)
                P.dma("sync", tau1[:], W["tau1"], writes=[Blane])
                Ut = sb("Ut", [128, 512], F32); BUt = Buf("Ut")
                tmp = (sb("tK", [128, 512], I32), sb("tV", [128, 512], F32), sb("tS1", [128, 512], F32), sb("tS2", [128, 512], F32))
                Btmp = Buf("sctmp")
                for pr in range(32):
                    P.op("vector", lambda e, pr=pr: e.tensor_scalar(Ut[:, :NT], tau1[:], thn[:, pr:pr + 1], None, ALU.mult),
                         reads=[Blane], writes=[BUt])
                    sincos(SIN[:, pr, :], COS[:, pr, :], Ut[:, :NT], NT, tmp, Btmp, [BUt], [Btab])
                RW_ = sb("RWt", [128, 3, 512], F32); BRW = Buf("RWt")
                nm = ["dt", "mag", "sn", "cs", "abr", "abi", "den", "cr", "ci", "t1"]
                Z = {k: sb("z" + k, [128, 512], F32) for k in nm}
                BZ = Buf("zz")
                BP = sb("BP", [128, 2, 4, 128], F32); BBP = Buf("BP")
                Vt = sb("Vt", [128, 512], F32)
                for qt in range(8):
                    l0 = qt * 512
                    P.dma("sync", RW_[:], W["s5row"][:, :, l0:l0 + 512], writes=[BRW])
                    P.dma("sync", BP[:, 0], W["s5b"][0][:, qt * 4:(qt + 1) * 4, :], writes=[BBP])
                    P.dma("sync", BP[:, 1], W["s5b"][1][:, qt * 4:(qt + 1) * 4, :], writes=[BBP])
                    lr_, li_ = RW_[:, 0, :], RW_[:, 1, :]
                    P.op("scalar", lambda e: e.activation(Z["dt"][:], RW_[:, 2, :], AF.Exp), reads=[BRW], writes=[BZ])
                    P.op("vector", lambda e: e.tensor_tensor(Z["mag"][:], lr_, Z["dt"][:], ALU.mult), reads=[BRW, BZ], writes=[BZ])
                    P.op("scalar", lambda e: e.activation(Z["mag"][:], Z["mag"][:], AF.Exp), reads=[BZ], writes=[BZ])
                    P.op("vector", lambda e: e.tensor_tensor(Ut[:], li_, Z["dt"][:], ALU.mult), reads=[BRW, BZ], writes=[BUt])
                    P.op("vector", lambda e: e.tensor_scalar(Ut[:], Ut[:], TWO_PI_INV, None, ALU.mult), reads=[BUt], writes=[BUt])
                    sincos(Z["sn"][:], Z["cs"][:], Ut[:], 512, tmp, Btmp, [BUt], [BZ])
                    P.op("vector", lambda e: e.tensor_tensor(Z["abr"][:], Z["mag"][:], Z["cs"][:], ALU.mult), reads=[BZ], writes=[BZ])
                    P.op("vector", lambda e: e.tensor_scalar(Z["abr"][:], Z["abr"][:], -1.0, None, ALU.add), reads=[BZ], writes=[BZ])
                    P.op("vector", lambda e: e.tensor_tensor(Z["abi"][:], Z["mag"][:], Z["sn"][:], ALU.mult), reads=[BZ], writes=[BZ])
                    P.op("vector", lambda e: e.tensor_tensor(Z["den"][:], lr_, lr_, ALU.mult), reads=[BRW], writes=[BZ])
                    P.op("vector", lambda e: e.tensor_tensor(Z["t1"][:], li_, li_, ALU.mult), reads=[BRW], writes=[BZ])
                    P.op("vector", lambda e: e.tensor_tensor(Z["den"][:], Z["den"][:], Z["t1"][:], ALU.add), reads=[BZ], writes=[BZ])
                    P.op("vector", lambda e: e.reciprocal(Z["den"][:], Z["den"][:]), reads=[BZ], writes=[BZ])
                    P.op("vector", lambda e: e.tensor_tensor(Z["cr"][:], Z["abr"][:], lr_, ALU.mult), reads=[BZ, BRW], writes=[BZ])
                    P.op("vector", lambda e: e.tensor_tensor(Z["t1"][:], Z["abi"][:], li_, ALU.mult), reads=[BZ, BRW], writes=[BZ])
                    P.op("vector", lambda e: e.tensor_tensor(Z["cr"][:], Z["cr"][:], Z["t1"][:], ALU.add), reads=[BZ], writes=[BZ])
                    P.op("vector", lambda e: e.tensor_tensor(Z["cr"][:], Z["cr"][:], Z["den"][:], ALU.mult), reads=[BZ], writes=[BZ])
                    P.op("vector", lambda e: e.tensor_tensor(Z["ci"][:], Z["abi"][:], lr_, ALU.mult), reads=[BZ, BRW], writes=[BZ])
                    P.op("vector", lambda e: e.tensor_tensor(Z["t1"][:], Z["abr"][:], li_, ALU.mult), reads=[BZ, BRW], writes=[BZ])
                    P.op("vector", lambda e: e.tensor_tensor(Z["ci"][:], Z["ci"][:], Z["t1"][:], ALU.subtract), reads=[BZ], writes=[BZ])
                    P.op("vector", lambda e: e.tensor_tensor(Z["ci"][:], Z["ci"][:], Z["den"][:], ALU.mult), reads=[BZ], writes=[BZ])
                    bre = BP[:, 0].rearrange("p a l -> p (a l)")
                    bim = BP[:, 1].rearrange("p a l -> p (a l)")
                    vre = VRE[:, qt * 4:(qt + 1) * 4, :].rearrange("p a l -> p (a l)")
                    vim = VIM[:, qt * 4:(qt + 1) * 4, :].rearrange("p a l -> p (a l)")
                    P.op("vector", lambda e, bre=bre: e.tensor_tensor(Vt[:], Z["cr"][:], bre, ALU.mult), reads=[BZ, BBP], writes=[BUt])
                    P.op("vector", lambda e, bim=bim: e.tensor_tensor(Z["t1"][:], Z["ci"][:], bim, ALU.mult), reads=[BZ, BBP], writes=[BZ])
                    P.op("vector", lambda e, vre=vre: e.tensor_tensor(vre, Vt[:], Z["t1"][:], ALU.subtract), reads=[BZ, BUt], writes=[BV])
                    P.op("vector", lambda e, bim=bim: e.tensor_tensor(Vt[:], Z["cr"][:], bim, ALU.mult), reads=[BZ, BBP], writes=[BUt])
                    P.op("vector", lambda e, bre=bre: e.tensor_tensor(Z["t1"][:], Z["ci"][:], bre, ALU.mult), reads=[BZ, BBP], writes=[BZ])
                    P.op("vector", lambda e, vim=vim: e.tensor_tensor(vim, Vt[:], Z["t1"][:], ALU.add), reads=[BZ, BUt], writes=[BV])
            UF = sb("UF", [128, KT, NT], F32); BUF = Buf("UF")
            UB = sb("UB", [128, KT, NT], BF16); BUB = Buf("UB")
            YG = sb("YG", [128, KT, NT], BF16); BYG = Buf("YG")

            def ring(name, depth, dt=F32):
                return [sb(f"s5{name}{i}", [128, NT], dt) for i in range(depth)], [Buf(f"s5{name}{i}") for i in range(depth)]

            XR, BXR = ring("XR", 2); XI, BXI = ring("XI", 2)
            XPR, BXPR = ring("XPR", 2); XPI, BXPI = ring("XPI", 2)
            WR, BWR = ring("WR", 2); WI, BWI = ring("WI", 2)
            SR, BSR = ring("SR", 2); SI, BSI = ring("SI", 2)
            TA, BTA = ring("TA", 8)
            SRb, BSRb = ring("SRb", 2, BF16); SIb, BSIb = ring("SIb", 2, BF16)
            Yt_, BYt = sb("s5Yt", [128, NT], F32), Buf("s5Yt")
            Tg_, BTg = sb("s5Tg", [128, NT], F32), Buf("s5Tg")
            T2_, BT2 = sb("s5T2", [128, NT], F32), Buf("s5T2")

            def tt(eng, o, a, b, op, rd, wr):
                P.op(eng, lambda e: e.tensor_tensor(o, a, b, op), reads=rd, writes=wr)

            def s5_pairs(n):
                PR = [(PS[i], BPS[i]) for i in (0, 1)]
                PI = [(PS[i], BPS[i]) for i in (2, 3)]
                PY = [(PS[i], BPS[i]) for i in (4, 5)]

                def ph0(t):
                    ft = t // 4
                    (psr, bpsr), (psi, bpsi) = PR[t % 2], PI[t % 2]
                    P.op("tensor", lambda e: e.matmul(psr[:, :n], VRE[:, t, :], UB[:, ft, :n], start=True, stop=True),
                         reads=[BV, BUB], writes=[bpsr])
                    P.op("tensor", lambda e: e.matmul(psi[:, :n], VIM[:, t, :], UB[:, ft, :n], start=True, stop=True),
                         reads=[BV, BUB], writes=[bpsi])

                def ph1(t):
                    (psr, bpsr), (psi, bpsi) = PR[t % 2], PI[t % 2]
                    P.op("scalar", lambda e: e.copy(XR[t % 2][:, :n], psr[:, :n]), reads=[bpsr], writes=[BXR[t % 2]])
                    P.op("scalar", lambda e: e.copy(XI[t % 2][:, :n], psi[:, :n]), reads=[bpsi], writes=[BXI[t % 2]])

                def ph2(t):
                    cs, sn = COS[:, t, :n], SIN[:, t, :n]
                    xr, xi = XR[t % 2][:, :n], XI[t % 2][:, :n]
                    a1, a2, a3, a4 = (TA[i][:, :n] for i in range(4))
                    tt("gpsimd", a1, cs, xr, ALU.mult, [Btab, BXR[t % 2]], [BTA[0]])
                    tt("gpsimd", a2, sn, xi, ALU.mult, [Btab, BXI[t % 2]], [BTA[1]])
                    tt("gpsimd", XPR[t % 2][:, :n], a1, a2, ALU.add, [BTA[0], BTA[1]], [BXPR[t % 2]])
                    tt("vector", a3, cs, xi, ALU.mult, [Btab, BXI[t % 2]], [BTA[2]])
                    tt("vector", a4, sn, xr, ALU.mult, [Btab, BXR[t % 2]], [BTA[3]])
                    tt("vector", XPI[t % 2][:, :n], a3, a4, ALU.subtract, [BTA[2], BTA[3]], [BXPI[t % 2]])

                def ph3(t):
                    rb = rdec[:, t:t + 1].to_broadcast([128, n])
                    P.op("vector", lambda e: e.tensor_tensor_scan(WR[t % 2][:, :n], rb, XPR[t % 2][:, :n], CR[:, t:t + 1], ALU.mult, ALU.add),
                         reads=[Blane, BXPR[t % 2], BCR], writes=[BWR[t % 2]])
                    P.op("vector", lambda e: e.tensor_tensor_scan(WI[t % 2][:, :n], rb, XPI[t % 2][:, :n], CI[:, t:t + 1], ALU.mult, ALU.add),
                         reads=[Blane, BXPI[t % 2], BCI], writes=[BWI[t % 2]])

                def ph4(t):
                    cs, sn = COS[:, t, :n], SIN[:, t, :n]
                    wr, wi = WR[t % 2][:, :n], WI[t % 2][:, :n]
                    a1, a2, a3, a4 = (TA[i][:, :n] for i in range(4, 8))
                    tt("gpsimd", a1, cs, wr, ALU.mult, [Btab, BWR[t % 2]], [BTA[4]])
                    tt("gpsimd", a2, sn, wi, ALU.mult, [Btab, BWI[t % 2]], [BTA[5]])
                    tt("gpsimd", SR[t % 2][:, :n], a1, a2, ALU.subtract, [BTA[4], BTA[5]], [BSR[t % 2]])
                    tt("vector", a3, cs, wi, ALU.mult, [Btab, BWI[t % 2]], [BTA[6]])
                    tt("vector", a4, sn, wr, ALU.mult, [Btab, BWR[t % 2]], [BTA[7]])
                    tt("vector", SI[t % 2][:, :n], a3, a4, ALU.add, [BTA[6], BTA[7]], [BSI[t % 2]])
                    P.op("gpsimd", lambda e: e.tensor_copy(CR[:, t:t + 1], SR[t % 2][:, n - 1:n]), reads=[BSR[t % 2]], writes=[BCR])
                    P.op("vector", lambda e: e.tensor_copy(CI[:, t:t + 1], SI[t % 2][:, n - 1:n]), reads=[BSI[t % 2]], writes=[BCI])

                def ph5(t):
                    P.op("scalar", lambda e: e.copy(SRb[t % 2][:, :n], SR[t % 2][:, :n]), reads=[BSR[t % 2]], writes=[BSRb[t % 2]])
                    P.op("scalar", lambda e: e.copy(SIb[t % 2][:, :n], SI[t % 2][:, :n]), reads=[BSI[t % 2]], writes=[BSIb[t % 2]])

                def ph6(t):
                    ft, q = t // 4, t % 4
                    yps, byps = PY[ft % 2]
                    P.op("tensor", lambda e: e.matmul(yps[:, :n], CRE[:, t, :], SRb[t % 2][:, :n], start=(q == 0), stop=False),
                         reads=[BCm, BSRb[t % 2]], writes=[byps])
                    P.op("tensor", lambda e: e.matmul(yps[:, :n], CIN[:, t, :], SIb[t % 2][:, :n], start=False, stop=(q == 3)),
                         reads=[BCm, BSIb[t % 2]], writes=[byps])
                    if q == 3:
                        Yt, Tg = Yt_[:, :n], Tg_[:, :n]
                        P.op("vector", lambda e: e.scalar_tensor_tensor(Yt, UF[:, ft, :n], dsk[:, ft:ft + 1], yps[:, :n], ALU.mult, ALU.add),
                             reads=[BUF, Blane, byps], writes=[BYt])
                        tt("gpsimd", Tg, Yt, Yt, ALU.mult, [BYt], [BTg])
                        P.op("gpsimd", lambda e: e.tensor_scalar(Tg, Tg, 0.044715, 1.0, ALU.mult, ALU.add), reads=[BTg], writes=[BTg])
                        tt("gpsimd", Tg, Tg, Yt, ALU.mult, [BTg, BYt], [BTg])
                        P.op("scalar", lambda e: e.activation(Tg, Tg, AF.Sigmoid, scale=1.5957691216), reads=[BTg], writes=[BTg])
                        P.op("gpsimd", lambda e: e.tensor_tensor(YG[:, ft, :n], Tg, Yt, ALU.mult), reads=[BTg, BYt], writes=[BYG])

                pipeline(32, [ph0, ph1, ph2, ph3, ph4, ph5, ph6])

            def s5_chunk(c0, n):
                hb = bufs_of(c0, n)
                P.dma("sync", ht[:, :, :n], hd_v[:, :, c0:c0 + n], reads=hb, writes=[Bht])
                norm_n(n, sidx)
                for ft in range(KT):
                    ps, bps = next_ps()
                    for kt in range(KT):
                        P.op("tensor", lambda e, kt=kt, ps=ps, ft=ft: e.matmul(ps[:, :n], win[:, kt, ft * 128:(ft + 1) * 128], xn[:, kt, :n],
                                                                           start=(kt == 0), stop=(kt == KT - 1)),
                             reads=[Bwin, Bxn], writes=[bps], inc=(kt == KT - 1))
                    P.op("scalar", lambda e, ps=ps, ft=ft: e.copy(UF[:, ft, :n], ps[:, :n]), reads=[bps], writes=[BUF])
                    P.op("gpsimd", lambda e, ft=ft: e.tensor_copy(UB[:, ft, :n], UF[:, ft, :n]), reads=[BUF], writes=[BUB])
                s5_pairs(n)
                T2 = T2_[:, :n]
                for o in range(KT):
                    psa, bpsa = next_ps()
                    psg, bpsg = next_ps()
                    for (pp, bpp, cb) in ((psa, bpsa, o * 128), (psg, bpsg, D + o * 128)):
                        for kt in range(KT):
                            P.op("tensor", lambda e, kt=kt, pp=pp, cb=cb: e.matmul(pp[:, :n], wgl[:, kt, cb:cb + 128], YG[:, kt, :n],
                                                                               start=(kt == 0), stop=(kt == KT - 1)),
                                 reads=[Bwgl, BYG], writes=[bpp], inc=(kt == KT - 1))
                    P.op("scalar", lambda e, psg=psg: e.activation(T2, psg[:, :n], AF.Sigmoid), reads=[bpsg], writes=[BT2])
                    P.op("vector", lambda e, psa=psa: e.tensor_tensor(T2, T2, psa[:, :n], ALU.mult), reads=[BT2, bpsa], writes=[BT2])
                    P.op("gpsimd", lambda e, o=o: e.tensor_tensor(ht[:, o, :n], ht[:, o, :n], T2, ALU.add), reads=[BT2, Bht], writes=[Bht])
                P.dma("sync", hd_v[:, :, c0:c0 + n], ht[:, :, :n], reads=[Bht], writes=hb)

            for c0 in range(0, L, NT):
                s5_chunk(c0, min(NT, L - c0))

        for s in stages:
            if s[0] == "s5":
                s5_stage(0)
            if s[0] == "sb":
                sb_stage(4)
            if s[0] == "hg":
                hg_stage(6)
            if s[0] == "ffn":
                ffn_stage(s[1], 2 * s[1] + 1)
            elif s[0] == "rg":
                rg_stage(2)

        P.dma("sync", out, hd[:, NMETA:], reads=Bh, key="final")
        Bfin = Buf("fin")
        P.wait_all("sync", Bh)
        P.q["sync"].append((None, [("final", P.semval["final"])], None))
        P.replay()
        print("ops:", P.nops, {k: len(v) for k, v in P.q.items()}, "sems:", len(P.sems))
    return nc


ALL_STAGES = [("s5",), ("ffn", 0), ("rg",), ("ffn", 1), ("sb",), ("ffn", 2), ("hg",), ("ffn", 3)]


def kernel(**inputs):
    inp = {k: np.asarray(v) for k, v in inputs.items()}
    wl = host_layout(inp)
    wshapes = {k: v.shape for k, v in wl.items()}
    nc = build(ALL_STAGES, wshapes)
    B = inp["x"].shape[0]
    in_maps = []
    for b in range(B):
        h0 = np.concatenate([inp["meta_tokens"], inp["x"][b]], axis=0)
        m = dict(wl)
        m["h0"] = np.ascontiguousarray(h0.T)
        in_maps.append(m)
    res = run_bass_kernel_spmd(nc, in_maps, core_ids=list(range(B)))
    outs = [np.ascontiguousarray(r["out"].T) for r in res.results]
    return np.stack(outs, 0).astype(np.float32)
```

```python
import contextlib
import numpy as np
import concourse.bass as bass
import concourse.mybir as mybir
from concourse.bass_utils import run_bass_kernel_spmd

F32 = mybir.dt.float32
BF16 = mybir.dt.bfloat16
I32 = mybir.dt.int32
AF = mybir.ActivationFunctionType
ALU = mybir.AluOpType

ENGS = ["tensor", "vector", "scalar", "gpsimd", "sync"]

L = 4112
D = 1024
KT = 8
DFF = 2816
NMETA = 16
EPS = 1e-6
SUBS = [(i * 512, 512) for i in range(8)] + [(4096, 16)]


class Buf:
    def __init__(self, name):
        self.name = name
        self.w = None
        self.r = []


class Prog:
    def __init__(self, nc, stack):
        self.nc = nc
        self.stack = stack
        self.q = {e: [] for e in ENGS}
        self.waited = {e: {} for e in ENGS}
        self.sems = {}
        self.semval = {}
        self.nops = 0
        for e in ENGS:
            self._sem("eng_" + e)

    def _sem(self, key):
        if key not in self.sems:
            self.sems[key] = self.stack.enter_context(self.nc.semaphore("s_" + key))
            self.semval[key] = 0
        return self.sems[key]

    def _collect(self, eng, reads, writes):
        need = {}

        def add(ev):
            k, v = ev
            if k == "eng_tensor" and eng == "tensor":
                return
            if v > need.get(k, 0):
                need[k] = v

        for b in reads:
            if b.w is not None:
                add(b.w)
        for b in writes:
            if b.w is not None:
                add(b.w)
            for ev in b.r:
                add(ev)
        out = []
        wd = self.waited[eng]
        for k, v in need.items():
            if wd.get(k, 0) >= v:
                continue
            wd[k] = v
            out.append((k, v))
        return out

    def _mark(self, ev, reads, writes):
        for b in writes:
            b.w = ev
            b.r = []
        for b in reads:
            if b not in writes:
                if len(b.r) > 24:
                    m = {}
                    for k, v in b.r:
                        if v > m.get(k, 0):
                            m[k] = v
                    b.r = list(m.items())
                b.r.append(ev)

    _cap = None

    def capture(self, fn):
        self._cap = []
        fn()
        c, self._cap = self._cap, None
        return c

    def emit_interleaved(self, lists):
        def items(l):
            out, curi = [], []
            for c in l:
                curi.append(c)
                if not (c[0] == "op" and c[2].get("inc") is False):
                    out.append(curi)
                    curi = []
            if curi:
                out.append(curi)
            return out

        lists = [items(l) for l in lists]
        for j in range(max(len(l) for l in lists)):
            for l in lists:
                if j < len(l):
                    for kind, a, kw in l[j]:
                        (self.op if kind == "op" else self.dma)(*a, **kw)

    def op(self, eng, fn, reads=(), writes=(), inc=True):
        if self._cap is not None:
            self._cap.append(("op", (eng, fn), dict(reads=list(reads), writes=list(writes), inc=inc)))
            return
        waits = self._collect(eng, reads, writes)
        if inc:
            k = "eng_" + eng
            self.semval[k] += 1
            self._mark((k, self.semval[k]), reads, writes)
        self.q[eng].append((fn, waits, ("eng_" + eng, 1) if inc else None))
        self.nops += 1

    def dma(self, eng, out, in_, reads=(), writes=(), key=None, **kw):
        if self._cap is not None:
            self._cap.append(("dma", (eng, out, in_), dict(reads=list(reads), writes=list(writes), key=key, **kw)))
            return
        waits = self._collect(eng, reads, writes)
        if key is None:
            key = "d_" + (writes[0].name if writes else reads[0].name + "_o")
        self._sem(key)
        self.semval[key] += 16
        self._mark((key, self.semval[key]), reads, writes)
        self.q[eng].append((lambda e: e.dma_start(out=out, in_=in_, **kw), waits, (key, 16)))
        self.nops += 1

    def wait_all(self, eng, bufs):
        waits = self._collect(eng, bufs, bufs)
        self.q[eng].append((None, waits, None))

    def barrier(self):
        for e in ENGS:
            wd = self.waited[e]
            waits = []
            for k, v in self.semval.items():
                if v > wd.get(k, 0):
                    wd[k] = v
                    waits.append((k, v))
            self.q[e].append((None, waits, None))

    def replay(self):
        self._replay()
        self.q = {e: [] for e in ENGS}

    def _replay(self):
        with self.nc.Block() as block:
            for ename in ENGS:
                ops = self.q[ename]
                if not ops:
                    continue

                def body(e, ops=ops):
                    for fn, waits, inc in ops:
                        for k, v in waits:
                            e.wait_ge(self.sems[k], v)
                        if fn is None:
                            continue
                        ins = fn(e)
                        if inc is not None:
                            ins.then_inc(self.sems[inc[0]], inc[1])

                getattr(block, ename)(body)


def host_layout(inp):
    f = np.float32
    o = {}
    g = np.stack([inp["norm_mix"][0], inp["norm_ffn"][0], inp["norm_mix"][1], inp["norm_ffn"][1],
                  inp["norm_mix"][2], inp["norm_ffn"][2], inp["norm_mix"][3], inp["norm_ffn"][3]], 0)
    o["gn"] = np.ascontiguousarray(g.reshape(8, KT, 128).transpose(2, 0, 1)).astype(f)
    wu = inp["ffn_w_up"].reshape(4, KT, 128, 2, 11, 256)
    o["wup"] = np.ascontiguousarray(wu.transpose(0, 4, 2, 1, 3, 5)).reshape(4, 11, 128, KT * 512).astype(f)
    wd = inp["ffn_w_down"].reshape(4, 22, 128, 8, 128)
    o["wdn"] = np.ascontiguousarray(wd.transpose(0, 3, 2, 1, 4)).reshape(4, 8, 128, 22 * 128).astype(f)
    cw = np.concatenate([inp["ffn_conv_w"], inp["ffn_conv_b"][:, None, :]], 1)
    o["fcw"] = np.ascontiguousarray(cw.reshape(4, 4, 44, 128).transpose(3, 0, 2, 1)).reshape(128, 4 * 44 * 4).astype(f)
    o["wrgin"] = np.ascontiguousarray(inp["rg_w_in"][0].reshape(KT, 128, 2688).transpose(1, 0, 2)).astype(f)
    o["wrgout"] = np.ascontiguousarray(inp["rg_w_out"][0].reshape(16, 84, D).transpose(1, 0, 2)).astype(f)
    o["wrga"] = np.ascontiguousarray(inp["rg_w_a"][0].transpose(1, 0, 2)).astype(f)
    o["wrgi"] = np.ascontiguousarray(inp["rg_w_i"][0].transpose(1, 0, 2)).astype(f)
    pr = np.concatenate([inp["rg_conv_w"][0], inp["rg_conv_b"], inp["rg_b_a"], inp["rg_b_i"], inp["rg_lambda"]], 0)
    o["rgp"] = np.ascontiguousarray(pr.reshape(8, 16, 84).transpose(2, 1, 0)).astype(f)
    o["whgin"] = np.ascontiguousarray(inp["hg_w_in"][0].reshape(KT, 128, 4096).transpose(1, 0, 2)).astype(f)
    o["whgout"] = np.ascontiguousarray(inp["hg_w_out"][0].reshape(KT, 128, D).transpose(1, 0, 2)).astype(f)
    o["hgg"] = np.ascontiguousarray(inp["hg_gamma"].reshape(4, KT, 128).transpose(2, 0, 1)).astype(f)
    o["hgon"] = np.ascontiguousarray(inp["hg_o_norm"][0].reshape(KT, 128).T).astype(f)
    wq = inp["sb_w_qkv"][0].reshape(KT, 128, 3, 2, 512)
    o["wqkv"] = np.ascontiguousarray(wq.transpose(3, 1, 0, 2, 4)).reshape(2, 128, KT, 1536).astype(f)
    wo = inp["sb_w_out"][0].reshape(2, 8, 64, D)
    o["wsbo"] = np.ascontiguousarray(wo.transpose(0, 2, 1, 3)).astype(f)
    o["qkn"] = np.ascontiguousarray(np.stack([np.tile(inp["sb_q_norm"][0], 2), np.tile(inp["sb_k_norm"][0], 2)], 1)).astype(f)
    bo = np.zeros((128, 128), f)
    bo[:64, :64] = 1.0
    bo[64:, 64:] = 1.0
    o["blockones"] = bo
    o["ones32"] = np.ones((128, 128), f)
    o["lstrict"] = np.tril(np.ones((128, 128), f), -1)
    jj = np.arange(128)[:, None]
    tt = np.arange(512)[None, :]
    am = np.zeros((2, 4, 128, 512), f)
    for r in range(4):
        past = (128 * r + jj) < tt
        am[0, r] = past.astype(f)
        am[1, r] = np.where(past, 0.0, -30000.0)
    o["amask"] = am
    o["ws5in"] = np.ascontiguousarray(inp["s5_w_in"][0].reshape(KT, 128, D).transpose(1, 0, 2)).astype(f)
    o["ws5glu"] = np.ascontiguousarray(inp["s5_w_glu"][0].reshape(KT, 128, 2 * D).transpose(1, 0, 2)).astype(f)
    lr, li, ldt = inp["s5_lam_re"][0], inp["s5_lam_im"][0], inp["s5_log_dt"][0]
    ldt2 = np.repeat(ldt[:, None], 64, 1)
    lane = lambda a: a.reshape(32, 2, 64).transpose(1, 2, 0).reshape(128, 32)
    o["s5lane"] = np.ascontiguousarray(np.stack([lane(lr), lane(li), lane(ldt2)], 1)).astype(f)
    row = np.stack([lr.reshape(-1), li.reshape(-1), ldt2.reshape(-1)], 0)
    o["s5row"] = np.ascontiguousarray(np.broadcast_to(row[None], (128, 3, 4096))).astype(f)
    gg = np.arange(64)
    bpad = np.zeros((2, 128, 32, 128), f)
    cpad = np.zeros((2, 128, 32, 128), f)
    for g in range(64):
        r0, pr, l0 = (g % 8) * 16, g // 2, (g % 2) * 64
        bpad[0, r0:r0 + 16, pr, l0:l0 + 64] = inp["s5_b_re"][0][g].T
        bpad[1, r0:r0 + 16, pr, l0:l0 + 64] = inp["s5_b_im"][0][g].T
        cpad[0, l0:l0 + 64, pr, r0:r0 + 16] = inp["s5_c_re"][0][g].T
        cpad[1, l0:l0 + 64, pr, r0:r0 + 16] = inp["s5_c_im"][0][g].T
    o["s5b"] = bpad
    o["s5c"] = cpad
    o["s5d"] = np.ascontiguousarray(inp["s5_d"][0].reshape(KT, 128).T).astype(f)
    o["tau1"] = np.ascontiguousarray(np.broadcast_to(np.arange(1, 257, dtype=f)[None], (128, 256)))
    o["ident"] = np.eye(128, dtype=f)
    rm = np.ones((128, 512), f)
    rm[:, ::128] = 0.0
    o["rmask"] = rm
    o["triu"] = np.triu(np.ones((128, 128), f))
    return o


def build(stages, wshapes, debug_out=False):
    nc = bass.Bass("TRN2", target_bir_lowering=False)
    st = contextlib.ExitStack()
    with st:
        P = Prog(nc, st)

        def din(name, shape, dt=F32):
            return nc.dram_tensor(name, list(shape), dt, kind="ExternalInput").ap()

        def dscr(name, shape, dt):
            return nc.dram_tensor(name, list(shape), dt).ap()

        cur = [st]

        uid = [0]

        def sb(name, shape, dt):
            uid[0] += 1
            return cur[0].enter_context(nc.sbuf_tensor(f"sb{uid[0]}_{name}", list(shape), dt))

        class NS:
            pass

        C = NS()

        @contextlib.contextmanager
        def stage_scope(NT):
            with contextlib.ExitStack() as sst:
                cur[0] = sst
                C.ht = sb("ht", [128, KT, NT], F32)
                C.Bht = Bht
                C.Bxn = Bxn
                C.xn = sb("xn", [128, KT, NT], BF16)
                C.sq = sb("sq", [128, 2, NT], BF16)
                C.rs = sb("rs", [128, NT], F32)
                yield
                P.barrier()
                P.replay()
                cur[0] = st

        h0 = din("h0", [D, L])
        W = {k: din(k, s) for k, s in wshapes.items()}
        out = nc.dram_tensor("out", [D, L - NMETA], F32, kind="ExternalOutput").ap()
        hd = dscr("hd", [D, L], F32)
        Bh = [Buf(f"hd{i}") for i in range(len(SUBS))]
        hd_v = hd.rearrange("(kt p) t -> p kt t", p=128)

        PS = [st.enter_context(nc.psum_tensor(f"ps{i}", [128, 512], F32)) for i in range(8)]
        BPS = [Buf(f"ps{i}") for i in range(8)]
        ps_rr = [0]

        def pipeline(ntiles, phases):
            k = len(phases)
            for step in range(ntiles + k - 1):
                for p in range(k):
                    t = step - p
                    if 0 <= t < ntiles:
                        phases[p](t)

        ps_reserved = set()

        def next_ps():
            while True:
                i = ps_rr[0]
                ps_rr[0] = (i + 1) % 8
                if i not in ps_reserved:
                    return PS[i], BPS[i]

        ones_bf = sb("ones_bf", [128, 128], BF16)
        Bones = Buf("ones")
        P.op("vector", lambda e: e.memset(ones_bf[:], 1.0), writes=[Bones])
        gn = sb("gn", [128, 8, KT], F32)
        Bgn = Buf("gn")
        P.dma("sync", gn[:], W["gn"], writes=[Bgn])

        for i, (c0, n) in enumerate(SUBS):
            P.dma("sync", hd[:, c0:c0 + n], h0[:, c0:c0 + n], writes=[Bh[i]], key="hinit")

        W16 = {}
        BW16 = {}

        def precast(name, lead):
            src = W[name]
            dst = dscr(name + "16", src.shape, BF16)
            W16[name] = dst
            BW16[name] = Buf(name + "16")
            import itertools
            for idx in itertools.product(*[range(s) for s in src.shape[:lead]]):
                s_ap, d_ap = src, dst
                for i in idx:
                    s_ap, d_ap = s_ap[i], d_ap[i]
                P.dma("gpsimd", d_ap, s_ap, writes=[BW16[name]], key="precast")

        NTM = 1040
        Bht = Buf("ht")
        Bxn = Buf("xn")
        Bsq = Buf("sq")
        Brs = Buf("rs")

        def subs_of(ch):
            c0 = SUBS[ch[0]][0]
            lst = [(SUBS[i][0] - c0, SUBS[i][1]) for i in ch]
            return c0, sum(x[1] for x in lst), lst

        def load_h(ch):
            ht = C.ht
            c0, n, _ = subs_of(ch)
            P.dma("sync", ht[:, :, :n], hd_v[:, :, c0:c0 + n], reads=[Bh[i] for i in ch], writes=[C.Bht])

        def store_h(ch):
            ht = C.ht
            c0, n, _ = subs_of(ch)
            P.dma("sync", hd_v[:, :, c0:c0 + n], ht[:, :, :n], reads=[C.Bht], writes=[Bh[i] for i in ch])

        Bsq2 = [Buf("sq_a"), Buf("sq_b")]

        def norm_impl(n, lst, sidx):
            ht, xn, sq, rs = C.ht, C.xn, C.sq, C.rs
            banks = [next_ps() for _ in lst]
            for kt in range(KT):
                P.op("scalar", lambda e, kt=kt: e.activation(sq[:, kt % 2, :n], ht[:, kt, :n], AF.Square),
                     reads=[C.Bht], writes=[Bsq2[kt % 2]])
                for (off, nsb), (ps, bps) in zip(lst, banks):
                    P.op("tensor", lambda e, ps=ps, kt=kt, off=off, nsb=nsb: e.matmul(
                        ps[:, :nsb], ones_bf[:], sq[:, kt % 2, off:off + nsb], start=(kt == 0), stop=(kt == KT - 1)),
                        reads=[Bones, Bsq2[kt % 2]], writes=[bps])
            for (off, nsb), (ps, bps) in zip(lst, banks):
                P.op("scalar", lambda e, ps=ps, off=off, nsb=nsb: e.activation(
                    rs[:, off:off + nsb], ps[:, :nsb], AF.Sqrt, bias=eps_t[:, 0:1], scale=1.0 / D),
                    reads=[bps, Beps], writes=[Brs])
            P.op("vector", lambda e: e.reciprocal(rs[:, :n], rs[:, :n]), reads=[Brs], writes=[Brs])
            for kt in range(KT):
                P.op("vector", lambda e, kt=kt: e.scalar_tensor_tensor(
                    xn[:, kt, :n], ht[:, kt, :n], gn[:, sidx, kt:kt + 1], rs[:, :n], ALU.mult, ALU.mult),
                    reads=[C.Bht, Bgn, Brs], writes=[C.Bxn])

        def norm(ch, sidx):
            c0, n, lst = subs_of(ch)
            norm_impl(n, lst, sidx)

        def norm_n(n, sidx):
            norm_impl(n, [(0, n)], sidx)

        eps_t = sb("eps_t", [128, 1], F32)
        Beps = Buf("eps")
        P.op("vector", lambda e: e.memset(eps_t[:], EPS), writes=[Beps])

        if any(s[0] == "ffn" for s in stages):
            precast("wup", 2)
            precast("wdn", 2)
        Bwu = [Buf(f"wu{i}") for i in range(2)]
        Bwd = [Buf(f"wd{i}") for i in range(2)]
        Bact = Buf("act")
        BU = [Buf(f"U{i}") for i in range(2)]
        BY = [Buf(f"Y{i}") for i in range(2)]
        Bhalo = Buf("halo")
        Bfcw = Buf("fcw")

        def ffn_stage(layer, sidx):
            with stage_scope(NTM):
                F = NS()
                F.fcw = sb("fcw", [128, 4 * 44 * 4], F32)
                P.dma("sync", F.fcw[:], W["fcw"], writes=[Bfcw])
                F.wu = [sb(f"wu{i}", [128, KT, 512], BF16) for i in range(2)]
                F.wdt = [sb(f"wd{i}", [128, 22, 128], BF16) for i in range(2)]
                F.act = sb("act", [128, 22, NTM], BF16)
                F.U = [sb(f"U{i}", [128, NTM + 2], F32) for i in range(2)]
                F.Y = [sb(f"Y{i}", [128, NTM], F32) for i in range(4)]
                F.BY = [Buf(f"Yb{i}") for i in range(4)]
                F.halo = sb("halo", [128, 44, 2], F32)
                halo = F.halo
                P.op("vector", lambda e: e.memset(halo[:], 0.0), writes=[Bhalo])
                HT = [C.ht, sb("ht2", [128, KT, NTM], F32)]
                BHT = [Bht, Buf("ht2")]
                XN = [C.xn, sb("xn2", [128, KT, NTM], BF16)]
                BXN = [Bxn, Buf("xn2")]
                F.Bact = [Buf(f"act{m}") for m in range(22)]
                F.first = True
                chs = [[0, 1], [2, 3], [4, 5], [6, 7, 8]]
                load_h(chs[0])
                norm(chs[0], sidx)
                for i, ch in enumerate(chs):
                    C.ht, C.Bht, C.xn, C.Bxn = HT[i % 2], BHT[i % 2], XN[i % 2], BXN[i % 2]
                    j = (i + 1) % 2
                    nxt = (chs[i + 1], HT[j], BHT[j], XN[j], BXN[j]) if i + 1 < len(chs) else None
                    ffn_chunk(F, layer, sidx, ch, nxt)
                C.ht, C.Bht, C.xn, C.Bxn = HT[0], BHT[0], XN[0], BXN[0]

        def ffn_chunk(F, layer, sidx, ch, nxt):
            fcw, wu, wdt, act, U, Y, halo = F.fcw, F.wu, F.wdt, F.act, F.U, F.Y, F.halo
            ht, xn = C.ht, C.xn
            Bht, Bxn, Bact = C.Bht, C.Bxn, F.Bact
            if True:
                c0, n, lst = subs_of(ch)
                if nxt is not None:
                    cur = (C.ht, C.Bht)
                    C.ht, C.Bht = nxt[1], nxt[2]
                    load_h(nxt[0])
                    C.ht, C.Bht = cur
                if F.first:
                    P.dma("sync", wu[0][:], W16["wup"][layer, 0].rearrange("p (k c) -> p k c", k=KT),
                          reads=[BW16["wup"]], writes=[Bwu[0]])
                    F.first = False
                for g in range(11):
                    if g == 8:
                        P.dma("sync", wdt[0][:], W16["wdn"][layer, 0].rearrange("p (k c) -> p k c", k=22),
                              reads=[BW16["wdn"]], writes=[Bwd[0]])
                    if g + 1 < 11:
                        P.dma("sync", wu[(g + 1) % 2][:], W16["wup"][layer, g + 1].rearrange("p (k c) -> p k c", k=KT),
                              reads=[BW16["wup"]], writes=[Bwu[(g + 1) % 2]])
                    wt, bwt = wu[g % 2], Bwu[g % 2]
                    for half in range(2):
                        m = 2 * g + half
                        for ab in range(2):
                            tidx = m + 22 * ab
                            col = 256 * ab + 128 * half
                            u, bu, y, by = U[ab], BU[ab], Y[2 * (m % 2) + ab], F.BY[2 * (m % 2) + ab]
                            P.op("gpsimd", lambda e, u=u, tidx=tidx: e.tensor_copy(u[:, 0:2], halo[:, tidx, :]),
                                 reads=[Bhalo], writes=[bu])
                            for (off, nsb) in lst:
                                ps, bps = next_ps()
                                for kt in range(KT):
                                    P.op("tensor", lambda e, ps=ps, wt=wt, kt=kt, col=col, off=off, nsb=nsb: e.matmul(
                                        ps[:, :nsb], wt[:, kt, col:col + 128], xn[:, kt, off:off + nsb],
                                        start=(kt == 0), stop=(kt == KT - 1)),
                                        reads=[bwt, Bxn], writes=[bps], inc=(kt == KT - 1))
                                P.op("scalar", lambda e, ps=ps, u=u, off=off, nsb=nsb: e.copy(
                                    u[:, 2 + off:2 + off + nsb], ps[:, :nsb]), reads=[bps], writes=[bu])
                            P.op("gpsimd", lambda e, u=u, tidx=tidx: e.tensor_copy(halo[:, tidx, :], u[:, n:n + 2]),
                                 reads=[bu], writes=[Bhalo])
                            cb = (layer * 44 + tidx) * 4
                            P.op("scalar", lambda e, u=u, y=y, cb=cb: e.activation(
                                y[:, :n], u[:, 2:2 + n], AF.Identity, bias=fcw[:, cb + 3:cb + 4], scale=fcw[:, cb + 2:cb + 3]),
                                reads=[bu, Bfcw], writes=[by])
                            P.op("vector", lambda e, u=u, y=y, cb=cb: e.scalar_tensor_tensor(
                                y[:, :n], u[:, 1:1 + n], fcw[:, cb + 1:cb + 2], y[:, :n], ALU.mult, ALU.add),
                                reads=[bu, Bfcw, by], writes=[by])
                            P.op("vector", lambda e, u=u, y=y, cb=cb: e.scalar_tensor_tensor(
                                y[:, :n], u[:, 0:n], fcw[:, cb:cb + 1], y[:, :n], ALU.mult, ALU.add),
                                reads=[bu, Bfcw, by], writes=[by])
                        def gate(m):
                            ya, yb, bya, byb = Y[2 * (m % 2)], Y[2 * (m % 2) + 1], F.BY[2 * (m % 2)], F.BY[2 * (m % 2) + 1]
                            P.op("scalar", lambda e: e.activation(ya[:, :n], ya[:, :n], AF.Silu), reads=[bya], writes=[bya])
                            P.op("gpsimd", lambda e: e.tensor_tensor(act[:, m, :n], ya[:, :n], yb[:, :n], ALU.mult),
                                 reads=[bya, byb], writes=[Bact[m]])
                        if m > 0:
                            gate(m - 1)
                gate(21)
                if nxt is not None:
                    cur = (C.ht, C.Bht, C.xn, C.Bxn)
                    C.ht, C.Bht, C.xn, C.Bxn = nxt[1], nxt[2], nxt[3], nxt[4]
                    norm(nxt[0], sidx)
                    C.ht, C.Bht, C.xn, C.Bxn = cur
                if nxt is not None:
                    P.dma("sync", wu[0][:], W16["wup"][layer, 0].rearrange("p (k c) -> p k c", k=KT),
                          reads=[BW16["wup"]], writes=[Bwu[0]])
                for j in range(8):
                    if j + 1 < 8:
                        P.dma("sync", wdt[(j + 1) % 2][:], W16["wdn"][layer, j + 1].rearrange("p (k c) -> p k c", k=22),
                              reads=[BW16["wdn"]], writes=[Bwd[(j + 1) % 2]])
                    wt, bwt = wdt[j % 2], Bwd[j % 2]
                    o = j
                    for (off, nsb) in lst:
                        ps, bps = next_ps()
                        for kt in range(22):
                            P.op("tensor", lambda e, ps=ps, wt=wt, kt=kt, off=off, nsb=nsb: e.matmul(
                                ps[:, :nsb], wt[:, kt, :], act[:, kt, off:off + nsb],
                                start=(kt == 0), stop=(kt == 21)),
                                reads=([bwt] + Bact) if kt == 21 else [bwt, Bact[kt]], writes=[bps], inc=(kt == 21))
                        P.op("vector", lambda e, ps=ps, o=o, off=off, nsb=nsb: e.tensor_tensor(
                            ht[:, o, off:off + nsb], ht[:, o, off:off + nsb], ps[:, :nsb], ALU.add),
                            reads=[bps, Bht], writes=[Bht])
                store_h(ch)

        one_t = sb("one_t", [128, 1], F32)
        Bone = Buf("one")
        P.op("vector", lambda e: e.memset(one_t[:], 1.0), writes=[Bone])

        def rg_stage(sidx):
            with stage_scope(512):
                rg_stage_(sidx)

        def rg_stage_(sidx):
            RW = 84
            ht, xn = C.ht, C.xn
            wrg = sb("wrg", [128, KT, 2688], BF16); Bwrg = Buf("wrg")
            P.dma("gpsimd", wrg[:], W["wrgin"], writes=[Bwrg])
            wro = sb("wro", [RW, 16, D], BF16); Bwro = Buf("wro")
            P.dma("gpsimd", wro[:], W["wrgout"], writes=[Bwro])
            wa = sb("wa", [RW, 16, RW], BF16); Bwa = Buf("wa")
            P.dma("gpsimd", wa[:], W["wrga"], writes=[Bwa])
            wi = sb("wi", [RW, 16, RW], BF16); Bwi = Buf("wi")
            P.dma("gpsimd", wi[:], W["wrgi"], writes=[Bwi])
            rgp = sb("rgp", [RW, 16, 8], F32); Brgp = Buf("rgp")
            P.dma("sync", rgp[:], W["rgp"], writes=[Brgp])
            cneg = sb("cneg", [RW, 16], F32); c2 = sb("c2", [RW, 16], F32); Bc = Buf("cneg")
            P.op("scalar", lambda e: e.activation(cneg[:], rgp[:, :, 7], AF.Exp, scale=-1.0), reads=[Brgp], writes=[Bc])
            P.op("scalar", lambda e: e.activation(cneg[:], cneg[:], AF.Ln, bias=one_t[:RW, 0:1]), reads=[Bc, Bone], writes=[Bc])
            P.op("vector", lambda e: e.tensor_scalar(cneg[:], cneg[:], -8.0, None, ALU.mult), reads=[Bc], writes=[Bc])
            P.op("vector", lambda e: e.tensor_scalar(c2[:], cneg[:], 2.0, None, ALU.mult), reads=[Bc], writes=[Bc])
            carry = sb("carry", [RW, 16], F32); Bcar = Buf("carry")
            P.op("vector", lambda e: e.memset(carry[:], 0.0), writes=[Bcar])
            rhalo = sb("rhalo", [RW, 16, 3], F32); Brh = Buf("rhalo")
            P.op("vector", lambda e: e.memset(rhalo[:], 0.0), writes=[Brh])
            YM = sb("YM", [RW, 16, 512], BF16); BYM = Buf("YM")
            def mkset(i):
                d = {"i": i}
                for nm_, shp, dt_ in [("T1", 512, F32), ("XB", 515, F32), ("XC", 512, F32), ("XCb", 512, BF16), ("Rr", 512, F32),
                                      ("Ii", 512, F32), ("Aa", 512, F32), ("HS", 512, F32), ("GG", 512, F32)]:
                    d[nm_] = sb(f"{nm_}_{i}", [RW, shp], dt_)
                    d["B" + nm_] = Buf(f"rg{nm_}_{i}")
                d["Mm"], d["BMm"] = d["Rr"], d["BRr"]
                d["Bt"], d["BBt"] = d["Ii"], d["BIi"]
                return d

            NSET = 4
            tsets = [mkset(i) for i in range(NSET)]

            def rg_tile(t, n, ts):
                T1, XB, XC, XCb, Rr, Ii, Aa, Mm, Bt, HS, GGt = (ts[k] for k in ["T1", "XB", "XC", "XCb", "Rr", "Ii", "Aa", "Mm", "Bt", "HS", "GG"])
                BT1, BXB, BXC, BXCb, BRr, BIi, BAa, BMm, BBt, BHS, BGG = (ts["B" + k] for k in
                                                                          ["T1", "XB", "XC", "XCb", "Rr", "Ii", "Aa", "Mm", "Bt", "HS", "GG"])
                ps, bps = PS[2 * ts['i']], BPS[2 * ts['i']]
                for kt in range(KT):
                    P.op("tensor", lambda e, kt=kt: e.matmul(ps[:RW, :n], wrg[:, kt, t * RW:(t + 1) * RW], xn[:, kt, :n],
                                                             start=(kt == 0), stop=(kt == KT - 1)),
                         reads=[Bwrg, Bxn], writes=[bps], inc=(kt == KT - 1))
                P.op("scalar", lambda e: e.activation(T1[:, :n], ps[:RW, :n], AF.Square), reads=[bps], writes=[BT1])
                P.op("scalar", lambda e: e.activation(T1[:, :n], T1[:, :n], AF.Identity, bias=one_t[:RW, 0:1], scale=0.044715),
                     reads=[BT1, Bone], writes=[BT1])
                P.op("vector", lambda e: e.tensor_tensor(T1[:, :n], T1[:, :n], ps[:RW, :n], ALU.mult),
                     reads=[BT1, bps], writes=[BT1])
                P.op("scalar", lambda e: e.activation(T1[:, :n], T1[:, :n], AF.Sigmoid, scale=1.5957691216), reads=[BT1], writes=[BT1])
                P.op("vector", lambda e: e.tensor_tensor(GGt[:, :n], T1[:, :n], ps[:RW, :n], ALU.mult),
                     reads=[BT1, bps], writes=[BGG])
                ps2, bps2 = PS[2 * ts['i'] + 1], BPS[2 * ts['i'] + 1]
                c0 = 1344 + t * RW
                for kt in range(KT):
                    P.op("tensor", lambda e, kt=kt: e.matmul(ps2[:RW, :n], wrg[:, kt, c0:c0 + RW], xn[:, kt, :n],
                                                             start=(kt == 0), stop=(kt == KT - 1)),
                         reads=[Bwrg, Bxn], writes=[bps2], inc=(kt == KT - 1))
                P.op("gpsimd", lambda e: e.tensor_copy(XB[:, 0:3], rhalo[:, t, :]), reads=[Brh], writes=[BXB])
                P.op("scalar", lambda e: e.copy(XB[:, 3:3 + n], ps2[:RW, :n]), reads=[bps2], writes=[BXB])
                P.op("gpsimd", lambda e: e.tensor_copy(rhalo[:, t, :], XB[:, n:n + 3]), reads=[BXB], writes=[Brh])
                P.op("scalar", lambda e: e.activation(XC[:, :n], XB[:, 3:3 + n], AF.Identity, bias=rgp[:, t, 4:5], scale=rgp[:, t, 3:4]),
                     reads=[BXB, Brgp], writes=[BXC])
                for k in range(3):
                    P.op("vector", lambda e, k=k: e.scalar_tensor_tensor(XC[:, :n], XB[:, k:k + n], rgp[:, t, k:k + 1], XC[:, :n],
                                                                         ALU.mult, ALU.add),
                         reads=[BXB, Brgp, BXC], writes=[BXC])
                P.op("gpsimd", lambda e: e.tensor_copy(XCb[:, :n], XC[:, :n]), reads=[BXC], writes=[BXCb])
                ps3, bps3 = PS[2 * ts['i']], BPS[2 * ts['i']]
                P.op("tensor", lambda e: e.matmul(ps3[:RW, :n], wa[:, t, :], XCb[:, :n], start=True, stop=True),
                     reads=[Bwa, BXCb], writes=[bps3])
                P.op("scalar", lambda e: e.activation(Rr[:, :n], ps3[:RW, :n], AF.Sigmoid, bias=rgp[:, t, 5:6]),
                     reads=[bps3, Brgp], writes=[BRr])
                ps4, bps4 = PS[2 * ts['i'] + 1], BPS[2 * ts['i'] + 1]
                P.op("tensor", lambda e: e.matmul(ps4[:RW, :n], wi[:, t, :], XCb[:, :n], start=True, stop=True),
                     reads=[Bwi, BXCb], writes=[bps4])
                P.op("scalar", lambda e: e.activation(Ii[:, :n], ps4[:RW, :n], AF.Sigmoid, bias=rgp[:, t, 6:7]),
                     reads=[bps4, Brgp], writes=[BIi])
                P.op("scalar", lambda e: e.activation(Aa[:, :n], Rr[:, :n], AF.Exp, scale=cneg[:, t:t + 1]),
                     reads=[BRr, Bc], writes=[BAa])
                P.op("scalar", lambda e: e.activation(Mm[:, :n], Rr[:, :n], AF.Exp, scale=c2[:, t:t + 1]),
                     reads=[BRr, Bc], writes=[BMm])
                P.op("vector", lambda e: e.tensor_scalar(Mm[:, :n], Mm[:, :n], 0.99999994, None, ALU.min), reads=[BMm], writes=[BMm])
                P.op("scalar", lambda e: e.activation(Mm[:, :n], Mm[:, :n], AF.Sqrt, scale=-1.0, bias=one_t[:RW, 0:1]),
                     reads=[BMm, Bone], writes=[BMm])
                P.op("gpsimd", lambda e: e.tensor_tensor(Bt[:, :n], Ii[:, :n], XC[:, :n], ALU.mult), reads=[BIi, BXC], writes=[BBt])
                P.op("gpsimd", lambda e: e.tensor_tensor(Bt[:, :n], Bt[:, :n], Mm[:, :n], ALU.mult), reads=[BBt, BMm], writes=[BBt])
                P.op("vector", lambda e: e.tensor_tensor_scan(HS[:, :n], Aa[:, :n], Bt[:, :n], carry[:, t:t + 1], ALU.mult, ALU.add),
                     reads=[BAa, BBt, Bcar], writes=[BHS])
                P.op("vector", lambda e: e.tensor_copy(carry[:, t:t + 1], HS[:, n - 1:n]), reads=[BHS], writes=[Bcar])
                P.op("gpsimd", lambda e: e.tensor_tensor(YM[:, t, :n], HS[:, :n], GGt[:, :n], ALU.mult),
                     reads=[BHS, BGG], writes=[BYM])

            def rg_chunk(si):
                c0, n, lst = subs_of([si])
                load_h([si])
                norm([si], sidx)
                for t0 in range(0, 16, NSET):
                    P.emit_interleaved([P.capture(lambda i=i: rg_tile(t0 + i, n, tsets[i])) for i in range(NSET)])
                for o in range(KT):
                    ps, bps = next_ps()
                    for t in range(16):
                        P.op("tensor", lambda e, t=t, o=o, ps=ps: e.matmul(ps[:, :n], wro[:, t, o * 128:(o + 1) * 128], YM[:, t, :n],
                                                                          start=(t == 0), stop=(t == 15)),
                             reads=[Bwro, BYM], writes=[bps], inc=(t == 15))
                    P.op("vector", lambda e, o=o, ps=ps: e.tensor_tensor(ht[:, o, :n], ht[:, o, :n], ps[:, :n], ALU.add),
                         reads=[bps, Bht], writes=[Bht])
                store_h([si])

            for si in range(len(SUBS)):
                rg_chunk(si)

        def hg_stage(sidx):
            with stage_scope(512):
                hg_stage_(sidx)

        def hg_stage_(sidx):
            ht, xn = C.ht, C.xn
            whg = sb("whg", [128, KT, 4096], BF16); Bwhg = Buf("whg")
            for kt in range(KT):
                P.dma("gpsimd", whg[:, kt, :], W["whgin"][:, kt, :], writes=[Bwhg])
            who = sb("who", [128, KT, D], BF16); Bwho = Buf("who")
            P.dma("gpsimd", who[:], W["whgout"], writes=[Bwho])
            ident = sb("ident", [128, 128], F32); rmask = sb("rmask", [128, 512], F32); triu = sb("triu", [128, 128], F32)
            Bcst = Buf("hgconst")
            P.dma("sync", ident[:], W["ident"], writes=[Bcst])
            P.dma("sync", rmask[:], W["rmask"], writes=[Bcst])
            P.dma("sync", triu[:], W["triu"], writes=[Bcst])
            gam = sb("gam", [128, 4, KT], F32); on = sb("on", [128, KT], F32); Bgam = Buf("gam")
            P.dma("sync", gam[:], W["hgg"], writes=[Bgam])
            P.dma("sync", on[:], W["hgon"], writes=[Bgam])
            lb = sb("lb", [128, KT], F32); oml = sb("oml", [128, KT], F32); s4 = sb("s4", [128, KT], F32); Blb = Buf("lb")
            P.op("scalar", lambda e: e.activation(gam[:], gam[:], AF.Exp), reads=[Bgam], writes=[Bgam])
            P.op("vector", lambda e: e.tensor_tensor(lb[:], gam[:, 0, :], gam[:, 1, :], ALU.add), reads=[Bgam], writes=[Blb])
            P.op("vector", lambda e: e.tensor_tensor(lb[:], lb[:], gam[:, 2, :], ALU.add), reads=[Bgam, Blb], writes=[Blb])
            P.op("vector", lambda e: e.tensor_tensor(s4[:], lb[:], gam[:, 3, :], ALU.add), reads=[Bgam, Blb], writes=[Blb])
            P.op("vector", lambda e: e.reciprocal(s4[:], s4[:]), reads=[Blb], writes=[Blb])
            P.op("vector", lambda e: e.tensor_tensor(lb[:], lb[:], s4[:], ALU.mult), reads=[Blb], writes=[Blb])
            P.op("vector", lambda e: e.tensor_tensor(oml[:], gam[:, 3, :], s4[:], ALU.mult), reads=[Bgam, Blb], writes=[Blb])
            S = sb("S", [128, KT, 128], F32)
            VT = sb("VT", [128, 4, D], BF16); BVT = Buf("VT")
            OG = sb("OG", [128, KT, 512], BF16); BOG = Buf("OG")
            BSl = [Buf(f"S{i}") for i in range(KT)]
            names = ["FH", "LF", "K1", "CUM", "E1", "E2", "E3", "KE", "SG", "O", "RS2", "ON"]

            def mkhs(i):
                h = NS()
                h.i = i
                h.rr = [0]
                h.Sb = sb(f"Sb{i}", [128, 128], BF16); h.BSb = Buf(f"Sb{i}")
                h.QD = sb(f"QD{i}", [128, 512], BF16); h.BQD = Buf(f"QD{i}")
                h.KI = sb(f"KI{i}", [128, 512], BF16); h.BKI = Buf(f"KI{i}")
                h.KET = sb(f"KET{i}", [128, 4, 128], BF16); h.BKET = Buf(f"KET{i}")
                h.SM = sb(f"SM{i}", [128, 128], BF16); h.BSM = Buf(f"SM{i}")
                h.T = {k: sb(f"hg{k}{i}", [128, 512], F32) for k in names}
                h.BT = {k: Buf(f"hg{k}{i}") for k in names}
                h.OSQ = sb(f"OSQ{i}", [128, 512], BF16); h.BOSQ = Buf(f"OSQ{i}")
                h.mids = sb(f"mids{i}", [128, 8], F32); h.Bmid = Buf(f"mids{i}")
                return h

            HS2 = [mkhs(0), mkhs(1)]
            P.op("vector", lambda e: e.memset(S[:], 0.0), writes=BSl)

            def hg_head(hd, n, subs, h):
                Sb, BSb, QD, BQD, KI, BKI, KET, BKET, SM, BSM = h.Sb, h.BSb, h.QD, h.BQD, h.KI, h.BKI, h.KET, h.BKET, h.SM, h.BSM
                T, BT, OSQ, BOSQ, mids, Bmid = h.T, h.BT, h.OSQ, h.BOSQ, h.mids, h.Bmid
                BS = BSl[hd]

                def next_ps():
                    i = 4 * h.i + h.rr[0]
                    h.rr[0] = (h.rr[0] + 1) % 4
                    return PS[i], BPS[i]

                FH, LF, K1, CUM, E1, E2, E3, KE, SG, O, RS2, ON = (T[k] for k in
                    ["FH", "LF", "K1", "CUM", "E1", "E2", "E3", "KE", "SG", "O", "RS2", "ON"])

                def proj(col0):
                    ps, bps = next_ps()
                    for kt in range(KT):
                        P.op("tensor", lambda e, kt=kt: e.matmul(ps[:, :n], whg[:, kt, col0:col0 + 128], xn[:, kt, :n],
                                                                 start=(kt == 0), stop=(kt == KT - 1)),
                             reads=[Bwhg, Bxn], writes=[bps], inc=(kt == KT - 1))
                    return ps, bps

                psf, bpsf = proj(1024 + hd * 128)
                P.op("scalar", lambda e: e.activation(FH[:, :n], psf[:, :n], AF.Sigmoid), reads=[bpsf], writes=[BT["FH"]])
                P.op("scalar", lambda e: e.activation(FH[:, :n], FH[:, :n], AF.Identity, bias=lb[:, hd:hd + 1], scale=oml[:, hd:hd + 1]),
                     reads=[BT["FH"], Blb], writes=[BT["FH"]])
                P.op("scalar", lambda e: e.activation(LF[:, :n], FH[:, :n], AF.Ln), reads=[BT["FH"]], writes=[BT["LF"]])
                P.op("scalar", lambda e: e.activation(K1[:, :n], FH[:, :n], AF.Identity, bias=one_t[:, 0:1], scale=-1.0),
                     reads=[BT["FH"], Bone], writes=[BT["K1"]])
                P.op("vector", lambda e: e.tensor_tensor_scan(CUM[:, :n], rmask[:, :n], LF[:, :n], 0.0, ALU.mult, ALU.add),
                     reads=[BT["LF"], Bcst], writes=[BT["CUM"]])
                for j, (o0, m) in enumerate(subs):
                    im, il = o0 + max(m // 2 - 1, 0), o0 + m - 1
                    P.op("scalar", lambda e, j=j, im=im: e.activation(mids[:, j:j + 1], CUM[:, im:im + 1], AF.Exp),
                         reads=[BT["CUM"]], writes=[Bmid])
                    P.op("scalar", lambda e, j=j, il=il: e.activation(mids[:, 4 + j:5 + j], CUM[:, il:il + 1], AF.Exp),
                         reads=[BT["CUM"]], writes=[Bmid])
                    P.op("scalar", lambda e, o0=o0, m=m, im=im: e.activation(E2[:, o0:o0 + m], CUM[:, o0:o0 + m], AF.Exp,
                                                                             scale=-1.0, bias=CUM[:, im:im + 1]),
                         reads=[BT["CUM"]], writes=[BT["E2"]])
                    P.op("scalar", lambda e, o0=o0, m=m, il=il: e.activation(E3[:, o0:o0 + m], CUM[:, o0:o0 + m], AF.Exp,
                                                                             scale=-1.0, bias=CUM[:, il:il + 1]),
                         reads=[BT["CUM"]], writes=[BT["E3"]])
                P.op("vector", lambda e: e.reciprocal(E1[:, :n], E2[:, :n]), reads=[BT["E2"]], writes=[BT["E1"]])
                psq, bpsq = proj(hd * 128)
                P.op("vector", lambda e: e.tensor_tensor(QD[:, :n], E1[:, :n], psq[:, :n], ALU.mult),
                     reads=[BT["E1"], bpsq], writes=[BQD])
                P.op("gpsimd", lambda e: e.tensor_tensor(KI[:, :n], K1[:, :n], E2[:, :n], ALU.mult),
                     reads=[BT["K1"], BT["E2"]], writes=[BKI])
                P.op("gpsimd", lambda e: e.tensor_tensor(KE[:, :n], K1[:, :n], E3[:, :n], ALU.mult),
                     reads=[BT["K1"], BT["E3"]], writes=[BT["KE"]])
                for j, (o0, m) in enumerate(subs):
                    pst, bpst = next_ps()
                    P.op("tensor", lambda e, o0=o0, m=m, pst=pst: e.transpose(pst[:m, :128], KE[:, o0:o0 + m], ident[:, :]),
                         reads=[BT["KE"], Bcst], writes=[bpst])
                    P.op("scalar", lambda e, j=j, m=m, pst=pst: e.copy(KET[:m, j, :], pst[:m, :128]), reads=[bpst], writes=[BKET])
                psg, bpsg = proj(3072 + hd * 128)
                P.op("scalar", lambda e: e.activation(SG[:, :n], psg[:, :n], AF.Silu), reads=[bpsg], writes=[BT["SG"]])
                for j, (o0, m) in enumerate(subs):
                    pss, bpss = next_ps()
                    P.op("tensor", lambda e, o0=o0, m=m, pss=pss: e.matmul(pss[:m, :m], KI[:, o0:o0 + m], QD[:, o0:o0 + m],
                                                                           start=True, stop=True),
                         reads=[BKI, BQD], writes=[bpss])
                    P.op("vector", lambda e, m=m, pss=pss: e.tensor_tensor(SM[:m, :m], pss[:m, :m], triu[:m, :m], ALU.mult),
                         reads=[bpss, Bcst], writes=[BSM])
                    P.op("gpsimd", lambda e, j=j: e.tensor_scalar(Sb[:], S[:, hd, :], mids[:, j:j + 1], None, ALU.mult),
                         reads=[BS, Bmid], writes=[BSb])
                    pso, bpso = next_ps()
                    P.op("tensor", lambda e, j=j, m=m, pso=pso: e.matmul(pso[:, :m], VT[:m, j, hd * 128:(hd + 1) * 128], SM[:m, :m],
                                                                         start=True, stop=False),
                         reads=[BVT, BSM], writes=[bpso], inc=False)
                    P.op("tensor", lambda e, o0=o0, m=m, pso=pso: e.matmul(pso[:, :m], Sb[:], QD[:, o0:o0 + m], start=False, stop=True),
                         reads=[BSb, BQD, BVT, BSM], writes=[bpso])
                    P.op("scalar", lambda e, o0=o0, m=m, pso=pso: e.copy(O[:, o0:o0 + m], pso[:, :m]), reads=[bpso], writes=[BT["O"]])
                    psu, bpsu = next_ps()
                    P.op("tensor", lambda e, j=j, m=m, psu=psu: e.matmul(psu[:, :128], KET[:m, j, :], VT[:m, j, hd * 128:(hd + 1) * 128],
                                                                         start=True, stop=True),
                         reads=[BKET, BVT], writes=[bpsu])
                    P.op("vector", lambda e, j=j, psu=psu: e.scalar_tensor_tensor(S[:, hd, :], S[:, hd, :], mids[:, 4 + j:5 + j], psu[:, :128],
                                                                                 ALU.mult, ALU.add),
                         reads=[BS, Bmid, bpsu], writes=[BS])
                P.op("gpsimd", lambda e: e.tensor_tensor(OSQ[:, :n], O[:, :n], O[:, :n], ALU.mult), reads=[BT["O"]], writes=[BOSQ])
                psn, bpsn = next_ps()
                P.op("tensor", lambda e: e.matmul(psn[:, :n], ones_bf[:], OSQ[:, :n], start=True, stop=True),
                     reads=[Bones, BOSQ], writes=[bpsn])
                P.op("scalar", lambda e: e.activation(RS2[:, :n], psn[:, :n], AF.Sqrt, bias=eps_t[:, 0:1], scale=1.0 / 128),
                     reads=[bpsn, Beps], writes=[BT["RS2"]])
                P.op("vector", lambda e: e.reciprocal(RS2[:, :n], RS2[:, :n]), reads=[BT["RS2"]], writes=[BT["RS2"]])
                P.op("vector", lambda e: e.scalar_tensor_tensor(ON[:, :n], O[:, :n], on[:, hd:hd + 1], RS2[:, :n], ALU.mult, ALU.mult),
                     reads=[BT["O"], Bgam, BT["RS2"]], writes=[BT["ON"]])
                P.op("gpsimd", lambda e: e.tensor_tensor(OG[:, hd, :n], ON[:, :n], SG[:, :n], ALU.mult),
                     reads=[BT["ON"], BT["SG"]], writes=[BOG])

            def hg_chunk(si):
                c0, n, lst = subs_of([si])
                subs = [(o0, min(128, n - o0)) for o0 in range(0, n, 128)]
                load_h([si])
                norm([si], sidx)
                for j, (o0, m) in enumerate(subs):
                    for half in range(2):
                        ps, bps = next_ps()
                        for kt in range(KT):
                            P.op("tensor", lambda e, kt=kt, ps=ps, o0=o0, m=m, half=half: e.matmul(
                                ps[:m, :512], xn[:, kt, o0:o0 + m], whg[:, kt, 2048 + 512 * half:2048 + 512 * (half + 1)],
                                start=(kt == 0), stop=(kt == KT - 1)),
                                reads=[Bwhg, Bxn], writes=[bps], inc=(kt == KT - 1))
                        P.op("scalar", lambda e, ps=ps, j=j, m=m, half=half: e.copy(VT[:m, j, 512 * half:512 * (half + 1)], ps[:m, :512]),
                             reads=[bps], writes=[BVT])
                for hd in range(0, KT, 2):
                    P.emit_interleaved([P.capture(lambda: hg_head(hd, n, subs, HS2[0])),
                                        P.capture(lambda: hg_head(hd + 1, n, subs, HS2[1]))])
                for o in range(KT):
                    ps, bps = next_ps()
                    for kt in range(KT):
                        P.op("tensor", lambda e, kt=kt, o=o, ps=ps: e.matmul(ps[:, :n], who[:, kt, o * 128:(o + 1) * 128], OG[:, kt, :n],
                                                                            start=(kt == 0), stop=(kt == KT - 1)),
                             reads=[Bwho, BOG], writes=[bps], inc=(kt == KT - 1))
                    P.op("vector", lambda e, o=o, ps=ps: e.tensor_tensor(ht[:, o, :n], ht[:, o, :n], ps[:, :n], ALU.add),
                         reads=[bps, Bht], writes=[Bht])
                store_h([si])

            for si in range(len(SUBS)):
                hg_chunk(si)

        hd2 = dscr("hd2", [D, L], F32)
        Bh2 = [Buf(f"hd2_{i}") for i in range(len(SUBS))]
        hd2_v = hd2.rearrange("(kt p) t -> p kt t", p=128)

        def sb_stage(sidx):
            for pas in range(2):
                with stage_scope(512):
                    sb_pass(sidx, pas)

        def sb_pass(sidx, pas):
            ht, xn = C.ht, C.xn
            wqkv = sb("wqkv", [128, KT, 1536], BF16); Bwqkv = Buf("wqkv")
            P.dma("gpsimd", wqkv[:], W["wqkv"][pas], writes=[Bwqkv])
            wo = sb("wo", [64, 8, D], BF16); Bwo = Buf("wo")
            P.dma("gpsimd", wo[:], W["wsbo"][pas], writes=[Bwo])
            Bcst = Buf("sbconst")
            qkn = sb("qkn", [128, 2], F32)
            P.dma("sync", qkn[:], W["qkn"], writes=[Bcst])
            P.op("vector", lambda e: e.tensor_scalar(qkn[:, 0:1], qkn[:, 0:1], 0.125, None, ALU.mult), reads=[Bcst], writes=[Bcst])
            bones = sb("bones", [128, 128], BF16)
            P.dma("gpsimd", bones[:], W["blockones"], writes=[Bcst])
            ones32 = sb("ones32", [128, 128], F32)
            P.dma("sync", ones32[:], W["ones32"], writes=[Bcst])
            lstr = sb("lstr", [128, 128], F32)
            P.dma("sync", lstr[:], W["lstrict"], writes=[Bcst])
            am = sb("am", [128, 2, 4, 512], F32)
            for a in range(2):
                for r in range(4):
                    P.dma("sync", am[:, a, r, :], W["amask"][a, r], writes=[Bcst])
            KS = sb("KS", [128, 4, L], BF16); BKS = Buf("KS")
            VS = sb("VS", [128, 33, 512], BF16); BVS = Buf("VS")
            Q = sb("Q", [128, 4, 512], BF16); BQ = Buf("Q")
            OA = sb("OA", [64, 8, 512], BF16); BOA = Buf("OA")
            SQ = sb("SQ", [128, 512], BF16); BSQ = Buf("SQ")
            RSq = sb("RSq", [128, 512], F32); BRSq = Buf("RSq")
            def ring(name, depth, dt=F32):
                return [sb(f"{name}{i}", [128, 512], dt) for i in range(depth)], [Buf(f"{name}{i}") for i in range(depth)]

            LD, BLD = ring("LD", 5)
            LK, BLK = ring("LK", 3)
            LSX, BLSX = ring("LSX", 2)
            NL, BNL = ring("NL", 3)
            ARG, BARG = ring("ARG", 2)
            WT, BWT = ring("WT", 2, BF16)
            zero_t = sb("zero_t", [128, 512], F32); Bzero = Buf("zero_t")
            P.op("vector", lambda e: e.memset(zero_t[:], 0.0), writes=[Bzero])
            src_v, src_B = (hd_v, Bh)
            dst_v, dst_B = (hd2_v, Bh2) if pas == 0 else (hd_v, Bh)

            def qk_norm(ps, bps, n, col, dst_fn, dB):
                P.op("scalar", lambda e: e.activation(SQ[:, :n], ps[:, :n], AF.Square), reads=[bps], writes=[BSQ])
                psn, bpsn = next_ps()
                P.op("tensor", lambda e: e.matmul(psn[:, :n], bones[:], SQ[:, :n], start=True, stop=True),
                     reads=[Bcst, BSQ], writes=[bpsn])
                P.op("scalar", lambda e: e.activation(RSq[:, :n], psn[:, :n], AF.Sqrt, bias=eps_t[:, 0:1], scale=1.0 / 64),
                     reads=[bpsn, Beps], writes=[BRSq])
                P.op("vector", lambda e: e.reciprocal(RSq[:, :n], RSq[:, :n]), reads=[BRSq], writes=[BRSq])
                P.op("vector", lambda e: e.scalar_tensor_tensor(dst_fn(), ps[:, :n], qkn[:, col:col + 1], RSq[:, :n], ALU.mult, ALU.mult),
                     reads=[bps, Bcst, BRSq], writes=[dB])

            def attention(n, c0):
                nkb = (c0 + n + 127) // 128
                tiles = [(hl, kb) for hl in range(8) for kb in range(nkb - 1, -1, -1)]

                def info(t):
                    hl, kb = tiles[t]
                    k0 = kb * 128
                    m = min(128, L - k0)
                    r = (k0 - c0) // 128 if k0 >= c0 else -1
                    return hl, kb, k0, m, r, (kb == nkb - 1), (kb == 0)

                PZ = [(PS[i], BPS[i]) for i in (0, 1, 2)]
                PB = [(PS[i], BPS[i]) for i in (3, 4)]
                PA = [(PS[i], BPS[i]) for i in (5, 6)]

                def ph0(t):
                    hl, kb, k0, m, r, first, last = info(t)
                    ft, half = hl // 2, hl % 2
                    psz, bpsz = PZ[t % 3]
                    P.op("tensor", lambda e: e.matmul(psz[:m, :n], KS[half * 64:(half + 1) * 64, ft, k0:k0 + m],
                                                      Q[half * 64:(half + 1) * 64, ft, :n], start=True, stop=True),
                         reads=[BKS, BQ], writes=[bpsz])

                def ph1(t):
                    hl, kb, k0, m, r, first, last = info(t)
                    psz, bpsz = PZ[t % 3]
                    ld, bld = LD[t % 5], BLD[t % 5]
                    P.op("scalar", lambda e: e.activation(ld[:m, :n], psz[:m, :n], AF.Exp, scale=-1.0), reads=[bpsz], writes=[bld])
                    P.op("scalar", lambda e: e.activation(ld[:m, :n], ld[:m, :n], AF.Ln, bias=one_t[:m, 0:1]), reads=[bld, Bone], writes=[bld])

                def ph2(t):
                    hl, kb, k0, m, r, first, last = info(t)
                    psz, bpsz = PZ[t % 3]
                    ld, bld = LD[t % 5], BLD[t % 5]
                    lk, blk = LK[t % 3], BLK[t % 3]
                    P.op("vector", lambda e: e.scalar_tensor_tensor(lk[:m, :n], psz[:m, :n], -1.0, ld[:m, :n], ALU.mult, ALU.subtract),
                         reads=[bpsz, bld], writes=[blk])
                    if r >= 0:
                        nl, bnl = NL[t % 3], BNL[t % 3]
                        P.op("gpsimd", lambda e: e.tensor_tensor(lk[:m, :n], lk[:m, :n], am[:m, 0, r, :n], ALU.mult),
                             reads=[blk, Bcst], writes=[blk])
                        P.op("gpsimd", lambda e: e.tensor_tensor(nl[:m, :n], am[:m, 1, r, :n], ld[:m, :n], ALU.subtract),
                             reads=[bld, Bcst], writes=[bnl])
                    if not first:
                        lsx, blsx = LSX[t % 2], BLSX[t % 2]
                        pl, bpl = LK[(t - 1) % 3], BLK[(t - 1) % 3]
                        mp = info(t - 1)[3]
                        if info(t - 1)[5]:
                            P.op("gpsimd", lambda e: e.tensor_copy(lsx[:, :n], zero_t[:, :n]), reads=[Bzero], writes=[blsx])
                            P.op("gpsimd", lambda e: e.tensor_copy(lsx[:mp, :n], pl[:mp, :n]), reads=[bpl], writes=[blsx])
                        else:
                            px, bpx = LSX[(t - 1) % 2], BLSX[(t - 1) % 2]
                            P.op("gpsimd", lambda e: e.tensor_tensor(lsx[:, :n], px[:, :n], pl[:, :n], ALU.add),
                                 reads=[bpx, bpl], writes=[blsx])

                def ph3(t):
                    hl, kb, k0, m, r, first, last = info(t)
                    lk, blk = LK[t % 3], BLK[t % 3]
                    psb, bpsb = PB[t % 2]
                    P.op("tensor", lambda e: e.matmul(psb[:m, :n], lstr[:m, :m], lk[:m, :n], start=True, stop=first),
                         reads=[Bcst, blk], writes=[bpsb], inc=first)
                    if not first:
                        lsx, blsx = LSX[t % 2], BLSX[t % 2]
                        P.op("tensor", lambda e: e.matmul(psb[:m, :n], ones32[:, :m], lsx[:, :n], start=False, stop=True),
                             reads=[Bcst, blk, blsx], writes=[bpsb])

                def ph4(t):
                    hl, kb, k0, m, r, first, last = info(t)
                    psb, bpsb = PB[t % 2]
                    arg, barg = ARG[t % 2], BARG[t % 2]
                    if r >= 0:
                        nl, bnl = NL[t % 3], BNL[t % 3]
                        P.op("vector", lambda e: e.tensor_tensor(arg[:m, :n], psb[:m, :n], nl[:m, :n], ALU.add),
                             reads=[bpsb, bnl], writes=[barg])
                    else:
                        ld, bld = LD[t % 5], BLD[t % 5]
                        P.op("vector", lambda e: e.tensor_tensor(arg[:m, :n], psb[:m, :n], ld[:m, :n], ALU.subtract),
                             reads=[bpsb, bld], writes=[barg])

                def ph5(t):
                    hl, kb, k0, m, r, first, last = info(t)
                    arg, barg = ARG[t % 2], BARG[t % 2]
                    wt, bwt = WT[t % 2], BWT[t % 2]
                    P.op("scalar", lambda e: e.activation(wt[:m, :n], arg[:m, :n], AF.Exp), reads=[barg], writes=[bwt])

                def ph6(t):
                    hl, kb, k0, m, r, first, last = info(t)
                    wt, bwt = WT[t % 2], BWT[t % 2]
                    acc, bacc = PA[hl % 2]
                    P.op("tensor", lambda e: e.matmul(acc[:64, :n], VS[:m, kb, hl * 64:(hl + 1) * 64], wt[:m, :n], start=first, stop=last),
                         reads=[BVS, bwt], writes=[bacc])
                    if last:
                        P.op("scalar", lambda e: e.copy(OA[:, hl, :n], acc[:64, :n]), reads=[bacc], writes=[BOA])

                pipeline(len(tiles), [ph0, ph1, ph2, ph3, ph4, ph5, ph6])

            def sb_chunk(si):
                c0, n, lst = subs_of([si])
                subs = [(o0, min(128, n - o0)) for o0 in range(0, n, 128)]
                P.dma("sync", ht[:, :, :n], src_v[:, :, c0:c0 + n], reads=[src_B[si]], writes=[Bht])
                norm([si], sidx)
                for ft in range(4):
                    for which in range(2):
                        ps, bps = next_ps()
                        cb = 512 * which + ft * 128
                        for kt in range(KT):
                            P.op("tensor", lambda e, kt=kt, ps=ps, cb=cb: e.matmul(ps[:, :n], wqkv[:, kt, cb:cb + 128], xn[:, kt, :n],
                                                                               start=(kt == 0), stop=(kt == KT - 1)),
                                 reads=[Bwqkv, Bxn], writes=[bps], inc=(kt == KT - 1))
                        if which == 0:
                            qk_norm(ps, bps, n, 0, lambda ft=ft: Q[:, ft, :n], BQ)
                        else:
                            qk_norm(ps, bps, n, 1, lambda ft=ft: KS[:, ft, c0:c0 + n], BKS)
                for (o0, m) in subs:
                    ps, bps = next_ps()
                    blk = (c0 + o0) // 128
                    for kt in range(KT):
                        P.op("tensor", lambda e, kt=kt, ps=ps, o0=o0, m=m: e.matmul(ps[:m, :512], xn[:, kt, o0:o0 + m], wqkv[:, kt, 1024:1536],
                                                                                 start=(kt == 0), stop=(kt == KT - 1)),
                             reads=[Bwqkv, Bxn], writes=[bps], inc=(kt == KT - 1))
                    P.op("scalar", lambda e, ps=ps, m=m, blk=blk: e.copy(VS[:m, blk, :], ps[:m, :512]), reads=[bps], writes=[BVS])
                if pas == 1:
                    P.dma("sync", ht[:, :, :n], hd2_v[:, :, c0:c0 + n], reads=[Bh2[si], Bxn], writes=[Bht])
                attention(n, c0)
                for o in range(KT):
                    ps, bps = next_ps()
                    for hl in range(8):
                        P.op("tensor", lambda e, hl=hl, o=o, ps=ps: e.matmul(ps[:, :n], wo[:, hl, o * 128:(o + 1) * 128], OA[:, hl, :n],
                                                                            start=(hl == 0), stop=(hl == 7)),
                             reads=[Bwo, BOA], writes=[bps], inc=(hl == 7))
                    P.op("vector", lambda e, o=o, ps=ps: e.tensor_tensor(ht[:, o, :n], ht[:, o, :n], ps[:, :n], ALU.add),
                         reads=[bps, Bht], writes=[Bht])
                P.dma("sync", dst_v[:, :, c0:c0 + n], ht[:, :, :n], reads=[Bht], writes=[dst_B[si]])

            for si in range(len(SUBS)):
                sb_chunk(si)

        @contextlib.contextmanager
        def sub_scope():
            prev = cur[0]
            with contextlib.ExitStack() as sst:
                cur[0] = sst
                yield
                P.barrier()
                P.replay()
                cur[0] = prev

        def bufs_of(c0, n):
            return [Bh[i] for i, (a, m) in enumerate(SUBS) if a < c0 + n and c0 < a + m]

        def sincos(sin_o, cos_o, ut, Fw, tmp, Btmp, rd, wr):
            K, V, S1, S2 = tmp
            P.op("vector", lambda e: e.tensor_copy(K[:, :Fw], ut), reads=rd, writes=[Btmp])
            P.op("vector", lambda e: e.tensor_tensor(V[:, :Fw], ut, K[:, :Fw], ALU.subtract), reads=rd + [Btmp], writes=[Btmp])
            P.op("scalar", lambda e: e.activation(S1[:, :Fw], V[:, :Fw], AF.Sin, scale=float(np.pi)), reads=[Btmp], writes=[Btmp])
            P.op("scalar", lambda e: e.activation(S2[:, :Fw], V[:, :Fw], AF.Sin, scale=float(np.pi / 2)), reads=[Btmp], writes=[Btmp])
            P.op("vector", lambda e: e.tensor_tensor(S2[:, :Fw], S2[:, :Fw], S2[:, :Fw], ALU.mult), reads=[Btmp], writes=[Btmp])
            P.op("vector", lambda e: e.tensor_scalar(S2[:, :Fw], S2[:, :Fw], -2.0, 1.0, ALU.mult, ALU.add), reads=[Btmp], writes=[Btmp])
            P.op("vector", lambda e: e.scalar_tensor_tensor(sin_o, S1[:, :Fw], 2.0, S2[:, :Fw], ALU.mult, ALU.mult),
                 reads=[Btmp], writes=wr)
            P.op("vector", lambda e: e.tensor_tensor(S1[:, :Fw], S1[:, :Fw], S1[:, :Fw], ALU.mult), reads=[Btmp], writes=[Btmp])
            P.op("vector", lambda e: e.tensor_scalar(cos_o, S1[:, :Fw], -2.0, 1.0, ALU.mult, ALU.add), reads=[Btmp], writes=wr)

        def s5_stage(sidx):
            with stage_scope(256):
                s5_stage_(sidx)

        def s5_stage_(sidx):
            NT = 256
            TWO_PI_INV = float(1.0 / (2.0 * np.pi))
            ht, xn = C.ht, C.xn
            win = sb("win", [128, KT, D], BF16); Bwin = Buf("win")
            P.dma("gpsimd", win[:], W["ws5in"], writes=[Bwin])
            wgl = sb("wgl", [128, KT, 2 * D], BF16); Bwgl = Buf("wgl")
            P.dma("gpsimd", wgl[:], W["ws5glu"], writes=[Bwgl])
            COS = sb("COS", [128, 32, NT], F32); SIN = sb("SIN", [128, 32, NT], F32); Btab = Buf("s5tab")
            VRE = sb("VRE", [128, 32, 128], BF16); VIM = sb("VIM", [128, 32, 128], BF16); BV = Buf("s5V")
            CRE = sb("CRE", [128, 32, 128], BF16); CIN = sb("CIN", [128, 32, 128], BF16); BCm = Buf("s5C")
            P.dma("gpsimd", CRE[:], W["s5c"][0], writes=[BCm])
            P.dma("gpsimd", CIN[:], W["s5c"][1], writes=[BCm])
            P.op("vector", lambda e: e.tensor_scalar(CIN[:], CIN[:], -1.0, None, ALU.mult), reads=[BCm], writes=[BCm])
            rdec = sb("rdec", [128, 32], F32); thn = sb("thn", [128, 32], F32); Blane = Buf("s5lane")
            dsk = sb("dsk", [128, KT], F32)
            P.dma("sync", dsk[:], W["s5d"], writes=[Blane])
            CR = sb("CR", [128, 32], F32); CI = sb("CI", [128, 32], F32); BCR = Buf("CR"); BCI = Buf("CI")
            P.op("vector", lambda e: e.memset(CR[:], 0.0), writes=[BCR])
            P.op("vector", lambda e: e.memset(CI[:], 0.0), writes=[BCI])
            with sub_scope():
                LP = sb("LP", [128, 3, 32], F32)
                P.dma("sync", LP[:], W["s5lane"], writes=[Blane])
                P.op("scalar", lambda e: e.activation(LP[:, 2, :], LP[:, 2, :], AF.Exp), reads=[Blane], writes=[Blane])
                P.op("vector", lambda e: e.tensor_tensor(rdec[:], LP[:, 0, :], LP[:, 2, :], ALU.mult), reads=[Blane], writes=[Blane])
                P.op("scalar", lambda e: e.activation(rdec[:], rdec[:], AF.Exp), reads=[Blane], writes=[Blane])
                P.op("vector", lambda e: e.tensor_tensor(thn[:], LP[:, 1, :], LP[:, 2, :], ALU.mult), reads=[Blane], writes=[Blane])
                P.op("vector", lambda e: e.tensor_scalar(thn[:], thn[:], TWO_PI_INV, None, ALU.mult), reads=[Blane], writes=[Blane])
                tau1 = sb("tau1", [128, NT], F32)
                P.dma("sync", tau1[:], W["tau1"], writes=[Blane])
                Ut = sb("Ut", [128, 512], F32); BUt = Buf("Ut")
                tmp = (sb("tK", [128, 512], I32), sb("tV", [128, 512], F32), sb("tS1", [128, 512], F32), sb("tS2", [128, 512], F32))
                Btmp = Buf("sctmp")
                for pr in range(32):
                    P.op("vector", lambda e, pr=pr: e.tensor_scalar(Ut[:, :NT], tau1[:], thn[:, pr:pr + 1], None, ALU.mult),
                         reads=[Blane], writes=[BUt])
                    sincos(SIN[:, pr, :], COS[:, pr, :], Ut[:, :NT], NT, tmp, Btmp, [BUt], [Btab])
                RW_ = sb("RWt", [128, 3, 512], F32); BRW = Buf("RWt")
                nm = ["dt", "mag", "sn", "cs", "abr", "abi", "den", "cr", "ci", "t1"]
                Z = {k: sb("z" + k, [128, 512], F32) for k in nm}
                BZ = Buf("zz")
                BP = sb("BP", [128, 2, 4, 128], F32); BBP = Buf("BP")
                Vt = sb("Vt", [128, 512], F32)
                for qt in range(8):
                    l0 = qt * 512
                    P.dma("sync", RW_[:], W["s5row"][:, :, l0:l0 + 512], writes=[BRW])
                    P.dma("sync", BP[:, 0], W["s5b"][0][:, qt * 4:(qt + 1) * 4, :], writes=[BBP])
                    P.dma("sync", BP[:, 1], W["s5b"][1][:, qt * 4:(qt + 1) * 4, :], writes=[BBP])
                    lr_, li_ = RW_[:, 0, :], RW_[:, 1, :]
                    P.op("scalar", lambda e: e.activation(Z["dt"][:], RW_[:, 2, :], AF.Exp), reads=[BRW], writes=[BZ])
                    P.op("vector", lambda e: e.tensor_tensor(Z["mag"][:], lr_, Z["dt"][:], ALU.mult), reads=[BRW, BZ], writes=[BZ])
                    P.op("scalar", lambda e: e.activation(Z["mag"][:], Z["mag"][:], AF.Exp), reads=[BZ], writes=[BZ])
                    P.op("vector", lambda e: e.tensor_tensor(Ut[:], li_, Z["dt"][:], ALU.mult), reads=[BRW, BZ], writes=[BUt])
                    P.op("vector", lambda e: e.tensor_scalar(Ut[:], Ut[:], TWO_PI_INV, None, ALU.mult), reads=[BUt], writes=[BUt])
                    sincos(Z["sn"][:], Z["cs"][:], Ut[:], 512, tmp, Btmp, [BUt], [BZ])
                    P.op("vector", lambda e: e.tensor_tensor(Z["abr"][:], Z["mag"][:], Z["cs"][:], ALU.mult), reads=[BZ], writes=[BZ])
                    P.op("vector", lambda e: e.tensor_scalar(Z["abr"][:], Z["abr"][:], -1.0, None, ALU.add), reads=[BZ], writes=[BZ])
                    P.op("vector", lambda e: e.tensor_tensor(Z["abi"][:], Z["mag"][:], Z["sn"][:], ALU.mult), reads=[BZ], writes=[BZ])
                    P.op("vector", lambda e: e.tensor_tensor(Z["den"][:], lr_, lr_, ALU.mult), reads=[BRW], writes=[BZ])
                    P.op("vector", lambda e: e.tensor_tensor(Z["t1"][:], li_, li_, ALU.mult), reads=[BRW], writes=[BZ])
                    P.op("vector", lambda e: e.tensor_tensor(Z["den"][:], Z["den"][:], Z["t1"][:], ALU.add), reads=[BZ], writes=[BZ])
                    P.op("vector", lambda e: e.reciprocal(Z["den"][:], Z["den"][:]), reads=[BZ], writes=[BZ])
                    P.op("vector", lambda e: e.tensor_tensor(Z["cr"][:], Z["abr"][:], lr_, ALU.mult), reads=[BZ, BRW], writes=[BZ])
                    P.op("vector", lambda e: e.tensor_tensor(Z["t1"][:], Z["abi"][:], li_, ALU.mult), reads=[BZ, BRW], writes=[BZ])
                    P.op("vector", lambda e: e.tensor_tensor(Z["cr"][:], Z["cr"][:], Z["t1"][:], ALU.add), reads=[BZ], writes=[BZ])
                    P.op("vector", lambda e: e.tensor_tensor(Z["cr"][:], Z["cr"][:], Z["den"][:], ALU.mult), reads=[BZ], writes=[BZ])
                    P.op("vector", lambda e: e.tensor_tensor(Z["ci"][:], Z["abi"][:], lr_, ALU.mult), reads=[BZ, BRW], writes=[BZ])
                    P.op("vector", lambda e: e.tensor_tensor(Z["t1"][:], Z["abr"][:], li_, ALU.mult), reads=[BZ, BRW], writes=[BZ])
                    P.op("vector", lambda e: e.tensor_tensor(Z["ci"][:], Z["ci"][:], Z["t1"][:], ALU.subtract), reads=[BZ], writes=[BZ])
                    P.op("vector", lambda e: e.tensor_tensor(Z["ci"][:], Z["ci"][:], Z["den"][:], ALU.mult), reads=[BZ], writes=[BZ])
                    bre = BP[:, 0].rearrange("p a l -> p (a l)")
                    bim = BP[:, 1].rearrange("p a l -> p (a l)")
                    vre = VRE[:, qt * 4:(qt + 1) * 4, :].rearrange("p a l -> p (a l)")
                    vim = VIM[:, qt * 4:(qt + 1) * 4, :].rearrange("p a l -> p (a l)")
                    P.op("vector", lambda e, bre=bre: e.tensor_tensor(Vt[:], Z["cr"][:], bre, ALU.mult), reads=[BZ, BBP], writes=[BUt])
                    P.op("vector", lambda e, bim=bim: e.tensor_tensor(Z["t1"][:], Z["ci"][:], bim, ALU.mult), reads=[BZ, BBP], writes=[BZ])
                    P.op("vector", lambda e, vre=vre: e.tensor_tensor(vre, Vt[:], Z["t1"][:], ALU.subtract), reads=[BZ, BUt], writes=[BV])
                    P.op("vector", lambda e, bim=bim: e.tensor_tensor(Vt[:], Z["cr"][:], bim, ALU.mult), reads=[BZ, BBP], writes=[BUt])
                    P.op("vector", lambda e, bre=bre: e.tensor_tensor(Z["t1"][:], Z["ci"][:], bre, ALU.mult), reads=[BZ, BBP], writes=[BZ])
                    P.op("vector", lambda e, vim=vim: e.tensor_tensor(vim, Vt[:], Z["t1"][:], ALU.add), reads=[BZ, BUt], writes=[BV])
            UF = sb("UF", [128, KT, NT], F32); BUF = Buf("UF")
            UB = sb("UB", [128, KT, NT], BF16); BUB = Buf("UB")
            YG = sb("YG", [128, KT, NT], BF16); BYG = Buf("YG")

            def ring(name, depth, dt=F32):
                return [sb(f"s5{name}{i}", [128, NT], dt) for i in range(depth)], [Buf(f"s5{name}{i}") for i in range(depth)]

            XR, BXR = ring("XR", 2); XI, BXI = ring("XI", 2)
            XPR, BXPR = ring("XPR", 2); XPI, BXPI = ring("XPI", 2)
            WR, BWR = ring("WR", 2); WI, BWI = ring("WI", 2)
            SR, BSR = ring("SR", 2); SI, BSI = ring("SI", 2)
            TA, BTA = ring("TA", 8)
            SRb, BSRb = ring("SRb", 2, BF16); SIb, BSIb = ring("SIb", 2, BF16)
            Yt_, BYt = sb("s5Yt", [128, NT], F32), Buf("s5Yt")
            Tg_, BTg = sb("s5Tg", [128, NT], F32), Buf("s5Tg")
            T2_, BT2 = sb("s5T2", [128, NT], F32), Buf("s5T2")

            def tt(eng, o, a, b, op, rd, wr):
                P.op(eng, lambda e: e.tensor_tensor(o, a, b, op), reads=rd, writes=wr)

            def s5_pairs(n):
                PR = [(PS[i], BPS[i]) for i in (0, 1)]
                PI = [(PS[i], BPS[i]) for i in (2, 3)]
                PY = [(PS[i], BPS[i]) for i in (4, 5)]

                def ph0(t):
                    ft = t // 4
                    (psr, bpsr), (psi, bpsi) = PR[t % 2], PI[t % 2]
                    P.op("tensor", lambda e: e.matmul(psr[:, :n], VRE[:, t, :], UB[:, ft, :n], start=True, stop=True),
                         reads=[BV, BUB], writes=[bpsr])
                    P.op("tensor", lambda e: e.matmul(psi[:, :n], VIM[:, t, :], UB[:, ft, :n], start=True, stop=True),
                         reads=[BV, BUB], writes=[bpsi])

                def ph1(t):
                    (psr, bpsr), (psi, bpsi) = PR[t % 2], PI[t % 2]
                    P.op("scalar", lambda e: e.copy(XR[t % 2][:, :n], psr[:, :n]), reads=[bpsr], writes=[BXR[t % 2]])
                    P.op("scalar", lambda e: e.copy(XI[t % 2][:, :n], psi[:, :n]), reads=[bpsi], writes=[BXI[t % 2]])

                def ph2(t):
                    cs, sn = COS[:, t, :n], SIN[:, t, :n]
                    xr, xi = XR[t % 2][:, :n], XI[t % 2][:, :n]
                    a1, a2, a3, a4 = (TA[i][:, :n] for i in range(4))
                    tt("gpsimd", a1, cs, xr, ALU.mult, [Btab, BXR[t % 2]], [BTA[0]])
                    tt("gpsimd", a2, sn, xi, ALU.mult, [Btab, BXI[t % 2]], [BTA[1]])
                    tt("gpsimd", XPR[t % 2][:, :n], a1, a2, ALU.add, [BTA[0], BTA[1]], [BXPR[t % 2]])
                    tt("vector", a3, cs, xi, ALU.mult, [Btab, BXI[t % 2]], [BTA[2]])
                    tt("vector", a4, sn, xr, ALU.mult, [Btab, BXR[t % 2]], [BTA[3]])
                    tt("vector", XPI[t % 2][:, :n], a3, a4, ALU.subtract, [BTA[2], BTA[3]], [BXPI[t % 2]])

                def ph3(t):
                    rb = rdec[:, t:t + 1].to_broadcast([128, n])
                    P.op("vector", lambda e: e.tensor_tensor_scan(WR[t % 2][:, :n], rb, XPR[t % 2][:, :n], CR[:, t:t + 1], ALU.mult, ALU.add),
                         reads=[Blane, BXPR[t % 2], BCR], writes=[BWR[t % 2]])
                    P.op("vector", lambda e: e.tensor_tensor_scan(WI[t % 2][:, :n], rb, XPI[t % 2][:, :n], CI[:, t:t + 1], ALU.mult, ALU.add),
                         reads=[Blane, BXPI[t % 2], BCI], writes=[BWI[t % 2]])

                def ph4(t):
                    cs, sn = COS[:, t, :n], SIN[:, t, :n]
                    wr, wi = WR[t % 2][:, :n], WI[t % 2][:, :n]
                    a1, a2, a3, a4 = (TA[i][:, :n] for i in range(4, 8))
                    tt("gpsimd", a1, cs, wr, ALU.mult, [Btab, BWR[t % 2]], [BTA[4]])
                    tt("gpsimd", a2, sn, wi, ALU.mult, [Btab, BWI[t % 2]], [BTA[5]])
                    tt("gpsimd", SR[t % 2][:, :n], a1, a2, ALU.subtract, [BTA[4], BTA[5]], [BSR[t % 2]])
                    tt("vector", a3, cs, wi, ALU.mult, [Btab, BWI[t % 2]], [BTA[6]])
                    tt("vector", a4, sn, wr, ALU.mult, [Btab, BWR[t % 2]], [BTA[7]])
                    tt("vector", SI[t % 2][:, :n], a3, a4, ALU.add, [BTA[6], BTA[7]], [BSI[t % 2]])
                    P.op("gpsimd", lambda e: e.tensor_copy(CR[:, t:t + 1], SR[t % 2][:, n - 1:n]), reads=[BSR[t % 2]], writes=[BCR])
                    P.op("vector", lambda e: e.tensor_copy(CI[:, t:t + 1], SI[t % 2][:, n - 1:n]), reads=[BSI[t % 2]], writes=[BCI])

                def ph5(t):
                    P.op("scalar", lambda e: e.copy(SRb[t % 2][:, :n], SR[t % 2][:, :n]), reads=[BSR[t % 2]], writes=[BSRb[t % 2]])
                    P.op("scalar", lambda e: e.copy(SIb[t % 2][:, :n], SI[t % 2][:, :n]), reads=[BSI[t % 2]], writes=[BSIb[t % 2]])

                def ph6(t):
                    ft, q = t // 4, t % 4
                    yps, byps = PY[ft % 2]
                    P.op("tensor", lambda e: e.matmul(yps[:, :n], CRE[:, t, :], SRb[t % 2][:, :n], start=(q == 0), stop=False),
                         reads=[BCm, BSRb[t % 2]], writes=[byps])
                    P.op("tensor", lambda e: e.matmul(yps[:, :n], CIN[:, t, :], SIb[t % 2][:, :n], start=False, stop=(q == 3)),
                         reads=[BCm, BSIb[t % 2]], writes=[byps])
                    if q == 3:
                        Yt, Tg = Yt_[:, :n], Tg_[:, :n]
                        P.op("vector", lambda e: e.scalar_tensor_tensor(Yt, UF[:, ft, :n], dsk[:, ft:ft + 1], yps[:, :n], ALU.mult, ALU.add),
                             reads=[BUF, Blane, byps], writes=[BYt])
                        tt("gpsimd", Tg, Yt, Yt, ALU.mult, [BYt], [BTg])
                        P.op("gpsimd", lambda e: e.tensor_scalar(Tg, Tg, 0.044715, 1.0, ALU.mult, ALU.add), reads=[BTg], writes=[BTg])
                        tt("gpsimd", Tg, Tg, Yt, ALU.mult, [BTg, BYt], [BTg])
                        P.op("scalar", lambda e: e.activation(Tg, Tg, AF.Sigmoid, scale=1.5957691216), reads=[BTg], writes=[BTg])
                        P.op("gpsimd", lambda e: e.tensor_tensor(YG[:, ft, :n], Tg, Yt, ALU.mult), reads=[BTg, BYt], writes=[BYG])

                pipeline(32, [ph0, ph1, ph2, ph3, ph4, ph5, ph6])

            def s5_chunk(c0, n):
                hb = bufs_of(c0, n)
                P.dma("sync", ht[:, :, :n], hd_v[:, :, c0:c0 + n], reads=hb, writes=[Bht])
                norm_n(n, sidx)
                for ft in range(KT):
                    ps, bps = next_ps()
                    for kt in range(KT):
                        P.op("tensor", lambda e, kt=kt, ps=ps, ft=ft: e.matmul(ps[:, :n], win[:, kt, ft * 128:(ft + 1) * 128], xn[:, kt, :n],
                                                                           start=(kt == 0), stop=(kt == KT - 1)),
                             reads=[Bwin, Bxn], writes=[bps], inc=(kt == KT - 1))
                    P.op("scalar", lambda e, ps=ps, ft=ft: e.copy(UF[:, ft, :n], ps[:, :n]), reads=[bps], writes=[BUF])
                    P.op("gpsimd", lambda e, ft=ft: e.tensor_copy(UB[:, ft, :n], UF[:, ft, :n]), reads=[BUF], writes=[BUB])
                s5_pairs(n)
                T2 = T2_[:, :n]
                for o in range(KT):
                    psa, bpsa = next_ps()
                    psg, bpsg = next_ps()
                    for (pp, bpp, cb) in ((psa, bpsa, o * 128), (psg, bpsg, D + o * 128)):
                        for kt in range(KT):
                            P.op("tensor", lambda e, kt=kt, pp=pp, cb=cb: e.matmul(pp[:, :n], wgl[:, kt, cb:cb + 128], YG[:, kt, :n],
                                                                               start=(kt == 0), stop=(kt == KT - 1)),
                                 reads=[Bwgl, BYG], writes=[bpp], inc=(kt == KT - 1))
                    P.op("scalar", lambda e, psg=psg: e.activation(T2, psg[:, :n], AF.Sigmoid), reads=[bpsg], writes=[BT2])
                    P.op("vector", lambda e, psa=psa: e.tensor_tensor(T2, T2, psa[:, :n], ALU.mult), reads=[BT2, bpsa], writes=[BT2])
                    P.op("gpsimd", lambda e, o=o: e.tensor_tensor(ht[:, o, :n], ht[:, o, :n], T2, ALU.add), reads=[BT2, Bht], writes=[Bht])
                P.dma("sync", hd_v[:, :, c0:c0 + n], ht[:, :, :n], reads=[Bht], writes=hb)

            for c0 in range(0, L, NT):
                s5_chunk(c0, min(NT, L - c0))

        for s in stages:
            if s[0] == "s5":
                s5_stage(0)
            if s[0] == "sb":
                sb_stage(4)
            if s[0] == "hg":
                hg_stage(6)
            if s[0] == "ffn":
                ffn_stage(s[1], 2 * s[1] + 1)
            elif s[0] == "rg":
                rg_stage(2)

        P.dma("sync", out, hd[:, NMETA:], reads=Bh, key="final")
        Bfin = Buf("fin")
        P.wait_all("sync", Bh)
        P.q["sync"].append((None, [("final", P.semval["final"])], None))
        P.replay()
        print("ops:", P.nops, {k: len(v) for k, v in P.q.items()}, "sems:", len(P.sems))
    return nc


ALL_STAGES = [("s5",), ("ffn", 0), ("rg",), ("ffn", 1), ("sb",), ("ffn", 2), ("hg",), ("ffn", 3)]


def kernel(**inputs):
    inp = {k: np.asarray(v) for k, v in inputs.items()}
    wl = host_layout(inp)
    wshapes = {k: v.shape for k, v in wl.items()}
    nc = build(ALL_STAGES, wshapes)
    B = inp["x"].shape[0]
    in_maps = []
    for b in range(B):
        h0 = np.concatenate([inp["meta_tokens"], inp["x"][b]], axis=0)
        m = dict(wl)
        m["h0"] = np.ascontiguousarray(h0.T)
        in_maps.append(m)
    res = run_bass_kernel_spmd(nc, in_maps, core_ids=list(range(B)))
    outs = [np.ascontiguousarray(r["out"].T) for r in res.results]
    return np.stack(outs, 0).astype(np.float32)
```
